# Optimizing a Trainium2 kernel written in Bass

```python
import functools
import jax, jax.numpy as jnp
from jax import lax
import numpy as np

D_MODEL = 1024
BATCH = 4
SEQ = 4096
DEPTH = 1
DEC_BATCH = 128
DEC_SEQ = 1
PAST_LEN = 8192
PAGE_SIZE = 128

N_HEADS = 8
N_KV_HEADS = 2
HEAD_DIM = 64
GQA_GROUP = N_HEADS // N_KV_HEADS
ATTN_WIDTH = N_HEADS * HEAD_DIM
KV_WIDTH = N_KV_HEADS * HEAD_DIM
WINDOW = 128
BLOCK = WINDOW
ATTN_SCALE = HEAD_DIM ** -0.5
NEG_INF = -1e30
POOL_WINDOWS = (2, 4, 8, 16)
N_POOL_GROUPS = len(POOL_WINDOWS)
POOL_WIDTH = D_MODEL - ATTN_WIDTH
POOL_GROUP_DIM = POOL_WIDTH // N_POOL_GROUPS
POOL_HIST = max(POOL_WINDOWS) - 1
MIX_WIDTH = ATTN_WIDTH + POOL_WIDTH
IN_WIDTH = ATTN_WIDTH + 2 * KV_WIDTH + POOL_WIDTH
D_FF = -(-8 * D_MODEL // (3 * 256)) * 256
RMS_EPS = 1e-5

kernel_name = "hymba_swa_sink_alibi_pool_swiglu_step"


def rms_norm(x, g):
    x32 = x.astype(jnp.float32)
    y = x32 * lax.rsqrt(jnp.mean(x32 * x32, axis=-1, keepdims=True) + RMS_EPS)
    return (y * g.astype(jnp.float32)).astype(x.dtype)


def alibi_slopes():
    return jnp.exp2(-8.0 * jnp.arange(1, N_HEADS + 1, dtype=jnp.float32) / N_HEADS)


def attend(q, kk, vv, rel, valid, sinks):
    scores = jnp.einsum('...qhgd,...khd->...hgqk', q, kk,
                        preferred_element_type=jnp.float32) * ATTN_SCALE
    slopes = alibi_slopes().reshape(N_KV_HEADS, GQA_GROUP)[:, :, None, None]
    scores = scores - slopes * rel.astype(jnp.float32)[..., None, None, :, :]
    scores = jnp.where(valid[..., None, None, :, :], scores, NEG_INF)
    sink = sinks.astype(jnp.float32).reshape(N_KV_HEADS, GQA_GROUP)[:, :, None, None]
    sink = jnp.broadcast_to(sink, scores.shape[:-1] + (1,))
    probs = jax.nn.softmax(jnp.concatenate([scores, sink], axis=-1), axis=-1)[..., :-1]
    return jnp.einsum('...hgqk,...khd->...qhgd', probs.astype(vv.dtype), vv)


def banded_attention(q, k, v, sinks):
    B, S = q.shape[:2]
    nb = S // BLOCK
    qb = q.reshape(B, nb, BLOCK, N_KV_HEADS, GQA_GROUP, HEAD_DIM)
    kb = k.reshape(B, nb, BLOCK, N_KV_HEADS, HEAD_DIM)
    vb = v.reshape(B, nb, BLOCK, N_KV_HEADS, HEAD_DIM)

    def with_prev(xb):
        prev = jnp.pad(xb, ((0, 0), (1, 0), (0, 0), (0, 0), (0, 0)))[:, :-1]
        return jnp.concatenate([prev, xb], axis=2)

    qi = jnp.arange(BLOCK)[:, None]
    ki = jnp.arange(2 * BLOCK)[None, :]
    rel = BLOCK + qi - ki
    key_pos = (jnp.arange(nb)[:, None, None] - 1) * BLOCK + ki[None]
    valid = (rel >= 0) & (rel <= WINDOW) & (key_pos >= 0)
    out = attend(qb, with_prev(kb), with_prev(vb), rel, valid, sinks)
    return out.reshape(B, S, ATTN_WIDTH)


def buffered_attention(q, k_new, v_new, buf_k, buf_v, sinks, pos0):
    B, T = q.shape[:2]
    L = buf_k.shape[1]
    kk = jnp.concatenate([buf_k, k_new.astype(buf_k.dtype)], axis=1)
    vv = jnp.concatenate([buf_v, v_new.astype(buf_v.dtype)], axis=1)
    qi = jnp.arange(T)[:, None]
    ki = jnp.arange(L + T)[None, :]
    rel = L + qi - ki
    key_pos = pos0 - L + ki
    valid = (rel >= 0) & (rel <= WINDOW) & (key_pos >= 0)
    out = attend(q, kk, vv, rel, valid, sinks).reshape(B, T, ATTN_WIDTH)
    return out, kk[:, -L:], vv[:, -L:]


def pool_mix(hist, u, pos0, w_pool, pool_scale):
    B, T, C = u.shape
    Lh = hist.shape[1]
    z = jnp.concatenate([hist.astype(u.dtype), u], axis=1).astype(jnp.float32)
    row_pos = pos0 - Lh + jnp.arange(Lh + T)
    z = jnp.where((row_pos >= 0)[None, :, None], z, 0.0)
    cz = jnp.concatenate([jnp.zeros((B, 1, C), jnp.float32), jnp.cumsum(z, axis=1)], axis=1)
    pos = pos0 + jnp.arange(T)
    outs = []
    for g, w in enumerate(POOL_WINDOWS):
        sl = slice(g * POOL_GROUP_DIM, (g + 1) * POOL_GROUP_DIM)
        win_sum = cz[:, Lh + 1:Lh + 1 + T, sl] - cz[:, Lh + 1 - w:Lh + 1 - w + T, sl]
        count = jnp.minimum(w, pos + 1).astype(jnp.float32)[None, :, None]
        outs.append(win_sum / count - z[:, Lh:, sl])
    m = jnp.stack(outs, axis=2).astype(u.dtype)
    y = jnp.einsum('btgc,gcd->btgd', m, w_pool).reshape(B, T, POOL_WIDTH) * pool_scale
    new_hist = jnp.concatenate([hist.astype(u.dtype), u], axis=1)[:, -Lh:]
    return y, new_hist


def split_proj(proj):
    B, T = proj.shape[:2]
    q = proj[..., :ATTN_WIDTH].reshape(B, T, N_KV_HEADS, GQA_GROUP, HEAD_DIM)
    k = proj[..., ATTN_WIDTH:ATTN_WIDTH + KV_WIDTH].reshape(B, T, N_KV_HEADS, HEAD_DIM)
    v = proj[..., ATTN_WIDTH + KV_WIDTH:ATTN_WIDTH + 2 * KV_WIDTH].reshape(B, T, N_KV_HEADS, HEAD_DIM)
    u = proj[..., ATTN_WIDTH + 2 * KV_WIDTH:]
    return q, k, v, u


def prompt_mixer(q, k, v, u, sinks, w_pool, pool_scale):
    B, S = u.shape[:2]
    a = banded_attention(q, k, v, sinks)
    hist = jnp.zeros((B, POOL_HIST, POOL_WIDTH), u.dtype)
    p, new_pool = pool_mix(hist, u, 0, w_pool, pool_scale)
    n_keep = min(WINDOW, S)
    return jnp.concatenate([a, p], axis=-1), (k[:, -n_keep:], v[:, -n_keep:], new_pool)


def sample_mixer(q, k, v, u, buf_k, buf_v, hist, sinks, w_pool, pool_scale):
    a, new_k, new_v = buffered_attention(q, k, v, buf_k, buf_v, sinks, PAST_LEN)
    p, new_pool = pool_mix(hist, u, PAST_LEN, w_pool, pool_scale)
    return jnp.concatenate([a, p], axis=-1), (new_k, new_v, new_pool)


def decoder_layer(x, mix, norm1, w_in, w_out, norm2, w_gate, w_up, w_down):
    h = rms_norm(x, norm1)
    mixed, state = mix(*split_proj(h @ w_in))
    x = x + mixed @ w_out
    h = rms_norm(x, norm2)
    x = x + (jax.nn.silu(h @ w_gate) * (h @ w_up)) @ w_down
    return x, state


def setup_inputs(seed: int = 0) -> dict:
    key = jax.random.key(seed)
    ks = jax.random.split(key, 18)
    wbuf = min(WINDOW, PAST_LEN)
    f32 = jnp.float32
    nrm = lambda k, shape, s=1.0: jax.random.normal(k, shape, f32) * s
    return {
        "x_prompt": nrm(ks[0], (BATCH, SEQ, D_MODEL)),
        "x_sample": nrm(ks[1], (DEC_BATCH, DEC_SEQ, D_MODEL)),
        "cache_k_window": nrm(ks[2], (DEPTH, DEC_BATCH, wbuf, N_KV_HEADS, HEAD_DIM)),
        "cache_v_window": nrm(ks[3], (DEPTH, DEC_BATCH, wbuf, N_KV_HEADS, HEAD_DIM)),
        "state_pool": nrm(ks[4], (DEPTH, DEC_BATCH, POOL_HIST, POOL_WIDTH)),
        "norm1": 1.0 + nrm(ks[5], (DEPTH, D_MODEL), 0.05),
        "w_in": nrm(ks[6], (DEPTH, D_MODEL, IN_WIDTH), D_MODEL ** -0.5),
        "attn_sinks": nrm(ks[7], (DEPTH, N_HEADS), 0.5),
        "w_pool": nrm(ks[8], (DEPTH, N_POOL_GROUPS, POOL_GROUP_DIM, POOL_GROUP_DIM), POOL_GROUP_DIM ** -0.5),
        "pool_scale": 1.0 + nrm(ks[9], (DEPTH, POOL_WIDTH), 0.1),
        "w_out": nrm(ks[10], (DEPTH, MIX_WIDTH, D_MODEL), MIX_WIDTH ** -0.5),
        "norm2": 1.0 + nrm(ks[11], (DEPTH, D_MODEL), 0.05),
        "w_gate": nrm(ks[12], (DEPTH, D_MODEL, D_FF), D_MODEL ** -0.5),
        "w_up": nrm(ks[13], (DEPTH, D_MODEL, D_FF), D_MODEL ** -0.5),
        "w_down": nrm(ks[14], (DEPTH, D_FF, D_MODEL), D_FF ** -0.5),
        "final_norm": 1.0 + nrm(ks[15], (D_MODEL,), 0.05),
    }


def reference(x_prompt, x_sample, cache_k_window, cache_v_window, state_pool,
              norm1, w_in, attn_sinks, w_pool, pool_scale, w_out, norm2,
              w_gate, w_up, w_down, final_norm):
    xp, xs = x_prompt, x_sample
    kp_list, vp_list, pp_list, ks_list, vs_list, ps_list = [], [], [], [], [], []
    for l in range(DEPTH):
        ffn = dict(norm1=norm1[l], w_in=w_in[l], w_out=w_out[l], norm2=norm2[l],
                   w_gate=w_gate[l], w_up=w_up[l], w_down=w_down[l])
        pmix = functools.partial(prompt_mixer, sinks=attn_sinks[l], w_pool=w_pool[l],
                                 pool_scale=pool_scale[l])
        smix = functools.partial(sample_mixer, buf_k=cache_k_window[l], buf_v=cache_v_window[l],
                                 hist=state_pool[l], sinks=attn_sinks[l], w_pool=w_pool[l],
                                 pool_scale=pool_scale[l])
        xp, (kp, vp, pp) = decoder_layer(xp, pmix, **ffn)
        xs, (ks_, vs_, ps_) = decoder_layer(xs, smix, **ffn)
        kp_list.append(kp); vp_list.append(vp); pp_list.append(pp)
        ks_list.append(ks_); vs_list.append(vs_); ps_list.append(ps_)
    y_prompt = rms_norm(xp, final_norm)
    y_sample = rms_norm(xs, final_norm)
    new_k_prompt = jnp.stack(kp_list, axis=0)
    new_v_prompt = jnp.stack(vp_list, axis=0)
    new_pool_prompt = jnp.stack(pp_list, axis=0)
    new_k_sample = jnp.stack(ks_list, axis=0)
    new_v_sample = jnp.stack(vs_list, axis=0)
    new_pool_sample = jnp.stack(ps_list, axis=0)
    return (y_prompt, y_sample, new_k_prompt, new_v_prompt, new_pool_prompt,
            new_k_sample, new_v_sample, new_pool_sample)
```

```python
import numpy as np
from contextlib import ExitStack
import concourse.bass as bass
import concourse.mybir as mybir
from concourse.bass_utils import run_bass_kernel_spmd

F32 = mybir.dt.float32
BF16 = mybir.dt.bfloat16
I32 = mybir.dt.int32
AF = mybir.ActivationFunctionType
ALU = mybir.AluOpType
AX = mybir.AxisListType

NCORES = 8
D = 1024
TPC = 2048
NT = 16
GT = 4
NG = 4
GN = 512
NF = 22
NS = 16
EPS = 1e-5
NX = 8
NWGU = 3
NWD = 8
MASKV = -30000.0
SMP_LEVEL = 99


class Prog:
    ENG = ("pe", "act", "dve", "pool", "sp")

    def __init__(self, nc):
        self.nc = nc
        self.ops = []
        self.last_writer = {}
        self.readers = {}
        self.capture = None
        self.queue = []

    def flush_chunk(self):
        q = self.queue
        while q and q[0][0] == "pe":
            self._add(*q.pop(0))
        while q and q[0][0] != "pe":
            self._add(*q.pop(0))

    def flush_all(self):
        while self.queue:
            self._add(*self.queue.pop(0))

    def _add(self, eng, fn, reads, writes, dma, semkey=None):
        if self.capture is not None:
            self.capture.append((eng, fn, reads, writes, dma, semkey))
            return None
        o = dict(eng=eng, fn=fn, dma=dma, semkey=semkey, idx=len(self.ops), deps=set(), raw=set(), signal=False)
        for k in reads:
            w = self.last_writer.get(k)
            if w is not None:
                o["deps"].add(w); o["raw"].add(w)
        for k in writes:
            w = self.last_writer.get(k)
            if w is not None:
                o["deps"].add(w)
            for r in self.readers.get(k, {}).values():
                for ri in r:
                    o["deps"].add(ri)
        o["deps"].discard(o["idx"])
        for k in writes:
            self.last_writer[k] = o["idx"]
            self.readers[k] = {}
        for k in reads:
            d = self.readers.setdefault(k, {})
            if dma:
                d.setdefault("dma", []).append(o["idx"])
            else:
                d[eng] = [o["idx"]]
        self.ops.append(o)
        return o

    def op(self, eng, fn, reads=(), writes=()):
        return self._add(eng, fn, list(reads), list(writes), False)

    def dma(self, eng, fn, reads=(), writes=(), semkey=None):
        return self._add(eng, fn, list(reads), list(writes), True, semkey)

    def emit(self, stack):
        nc = self.nc
        ops = self.ops
        for o in ops:
            for d in o["deps"]:
                a = ops[d]
                if a["dma"]:
                    continue
                if a["eng"] == o["eng"] and (a["eng"] == "pe" or d not in o["raw"]):
                    continue
                a["signal"] = True
        cnt = {e: 0 for e in self.ENG}
        dcnt = {}
        for o in ops:
            if o["dma"]:
                dcnt[o["semkey"]] = dcnt.get(o["semkey"], 0) + 16
                o["sigval"] = dcnt[o["semkey"]]
            elif o["signal"]:
                cnt[o["eng"]] += 1
                o["sigval"] = cnt[o["eng"]]
        esem = {e: stack.enter_context(nc.semaphore("s_" + e)) for e in self.ENG}
        dsem = {k: stack.enter_context(nc.semaphore("d_%d" % i)) for i, k in enumerate(dcnt)}
        self.n_sems = len(esem) + len(dsem)
        block = stack.enter_context(nc.Block())

        def stream(eng):
            def body(e):
                waited = {}
                for o in ops:
                    if o["eng"] != eng:
                        continue
                    need = {}
                    for d in o["deps"]:
                        a = ops[d]
                        if a["dma"]:
                            key = ("d", a["semkey"]); val = a["sigval"]
                        else:
                            if a["eng"] == eng and (eng == "pe" or d not in o["raw"]):
                                continue
                            key = ("e", a["eng"]); val = a["sigval"]
                        if need.get(key, 0) < val:
                            need[key] = val
                    for key, val in need.items():
                        if waited.get(key, 0) < val:
                            sem = dsem[key[1]] if key[0] == "d" else esem[key[1]]
                            e.wait_ge(sem, val)
                            waited[key] = val
                    ins = o["fn"](e)
                    if o["dma"]:
                        ins.then_inc(dsem[o["semkey"]], 16)
                    elif o["signal"]:
                        ins.then_inc(esem[eng], 1)
            return body

        block.tensor(stream("pe"))
        block.scalar(stream("act"))
        block.vector(stream("dve"))
        block.gpsimd(stream("pool"))
        block.sync(stream("sp"))


def build_program():
    nc = bass.Bass("TRN2", target_bir_lowering=False)

    def din(name, shape):
        return nc.dram_tensor(name, shape, F32, kind="ExternalInput").ap()

    def dout(name, shape):
        return nc.dram_tensor(name, shape, F32, kind="ExternalOutput").ap()

    x_d = din("x", [TPC, D]); xh_d = din("xh", [128, D]); pos_d = din("pos0", [128, 1])
    w_in_d = din("w_in", [D, 1280]); w_out_d = din("w_out", [D, D])
    w_gu_d = din("w_gu", [NF, 128, 2 * 8 * 128]); w_d_d = din("w_d", [2 * NF, 128, 512])
    w_pool_d = din("w_pool", [4, 128, 128])
    g1_d = din("g1", [128, 8]); g2_d = din("g2", [128, 8]); psc_d = din("psc", [128, 4])
    gf_d = din("gf", [128, D]); sink_d = din("sinks", [128, 8])
    xs_d = din("xs", [NS, D]); ck_d = din("ck", [NS, 128, 128]); cv_d = din("cv", [NS, 128, 128])
    sp_d = din("spool", [NS, 15, 512])
    y_d = dout("y", [TPC, D]); kvu_d = dout("kvu_last", [128, 768])
    ys_d = dout("ys", [NS, D]); nk_d = dout("nk", [NS, 128, 128]); nv_d = dout("nv", [NS, 128, 128])
    np_d = dout("npool", [NS, 15, 512])

    st = ExitStack()
    with st:
        def sb(name, shape, dt=F32):
            return st.enter_context(nc.sbuf_tensor("sb_" + name, shape, dt))

        w_in_sb = sb("w_in_sb", [128, 8, 1280], BF16)
        w_out_sb = sb("w_out_sb", [128, 8, D], BF16)
        w_pool_sb = sb("w_pool_sb", [128, 4, 128], BF16)
        wgu = sb("wgu", [128, NWGU, 2, 8, 128], BF16)
        wd = sb("wd", [128, NWD, 512], BF16)
        xs = sb("xs", [128, NX, D], F32)
        hT = sb("hT", [128, 8, GN], BF16)
        h2T = sb("h2T", [128, 8, GN], BF16)
        xn = sb("xn", [128, 2, D], BF16)
        qT = sb("qT", [128, 4, GN], BF16)
        NKR = 8
        kTp = sb("kTp", [128, 2, NKR, 128], BF16)
        NV = 8
        Vaug = sb("Vaug", [128, NV, 256], BF16)
        NU = 8
        u_tm = sb("u_tm", [128, NU, 512], BF16)
        bandM = sb("bandM", [128, 3, 4, 128], BF16)
        mT = sb("mT", [128, 4, GN], BF16)
        PT = sb("PT", [128, 8, GN], BF16)
        rec = sb("rec", [128, 2, GN], F32)
        mixT = sb("mixT", [128, 8, GN], BF16)
        junk = mixT[:].rearrange("p c n -> p (c n)")[:, 0:D]
        actT = sb("actT", [128, NF, GN], BF16)
        sg = sb("sg", [128, 2, GN], F32)
        biasT = sb("biasT", [128, 2, 2, GN], BF16)
        gft = sb("gft", [128, D], F32)
        g1 = sb("g1", [128, 8]); g2 = sb("g2", [128, 8]); psc = sb("psc", [128, 4])
        sinks = sb("sinks", [128, 8]); es = sb("es", [128, 8]); es_hi = sb("es_hi", [128, 8], BF16)
        es_hif = sb("es_hif", [128, 8]); es_lo = sb("es_lo", [128, 8], BF16)
        pos0 = sb("pos0", [128, 1]); flag = sb("flag", [128, 1])
        ss = sb("ss", [128, 8, 4]); var = sb("var", [128, 8, 4]); rstd = sb("rstd", [128, 8, 4])
        expm = sb("expm", [128, 4])
        ident = sb("ident", [128, 128], BF16)
        mix_tm = sb("mix_tm", [128, 2, 512], BF16)
        den = sb("den", [128, 2, 8, 1], F32)
        icnt = sb("icnt", [128, 4, 16], F32)
        io16 = sb("io16", [128, 16], I32)
        io16f = sb("io16f", [128, 16], F32)
        identf = PT[:, 0, 0:256].bitcast(F32)
        iot = PT[:, 1, 0:256].bitcast(I32)
        Rf = PT[:, 2, 0:256].bitcast(F32)
        tmpb_v = [PT[:, 3, 0:256].bitcast(F32), PT[:, 4, 0:256].bitcast(F32)]
        klast = rec[:].rearrange("p a n -> p (a n)")[:, 0:768]
        KL = [("rec", 0), ("rec", 1)]
        xsm = sb("xsm", [NS, D], F32)
        hTs = sb("hTs", [128, 8, NS], BF16)
        h2Ts = sb("h2Ts", [128, 8, NS], BF16)
        qTs = sb("qTs", [128, 4, NS], BF16)
        uTs = sb("uTs", [128, 4, NS], F32)
        tmps = sb("tmps", [128, 4, NS], F32)
        mTs = sb("mTs", [128, 4, NS], BF16)
        mixTs = sb("mixTs", [128, 8, NS], BF16)
        actTs = sb("actTs", [128, NF, NS], BF16)
        sgs = sb("sgs", [128, NS], F32)
        s_sb = sb("s_sb", [128, NS, 8], F32)
        PTs = sb("PTs", [128, NS, 8], BF16)
        iop = sb("iop", [128, 1], I32)
        relc = sb("relc", [128, 1], F32)
        sbias = sb("sbias", [128, 8], F32)
        snew = sb("snew", [NS, 8], F32)
        pnew = sb("pnew", [NS, 8], F32)
        bdmask = sb("bdmask", [NS, 2, NS, 4], F32)
        pbd = sb("pbd", [128, 2, NS, 4], BF16)
        vnew = sb("vnew", [128, 256], BF16)
        recs = sb("recs", [128, 2, 64], F32)
        essm = sb("essm", [128, 2, NS, 4], F32)
        PTf = PT[:].rearrange("p s n -> p (s n)")
        ckb = PTf[:, 0:2048].rearrange("p (b f) -> p b f", f=128)
        ckT = PTf[:, 2048:4096].rearrange("p (b f) -> p b f", f=128)
        cva = mixT[:].rearrange("p c n -> p (c n)").rearrange("p (b f) -> p b f", f=256)
        qTf = qT[:].rearrange("p c n -> p (c n)").bitcast(F32)
        mTb = mT[:].rearrange("p c n -> p (c n)")
        mTf = mTb.bitcast(F32)
        tok_q = qTf[0:NS, 0:512]
        tok_kvu = sb("tok_kvu", [NS, 768], F32)
        hist_v = [qTf[:, 512:1024], mTf[:, 512:1024]]
        xns = mTb[0:NS, 0:1024]
        selw = sb("selw", [128, 2, 4, NS], F32)
        gate_t = sb("gate_t", [128, 1], F32)

        class _Tok:
            def __getitem__(self, idx):
                p, cs = idx
                c0, c1 = cs.start, cs.stop
                if c1 <= 512:
                    return tok_q[:, c0:c1]
                assert c0 >= 512
                return tok_kvu[:, c0 - 512:c1 - 512]
        tok_s = _Tok()
        AL_KEYS = ["al_ckb", "al_ckT", "al_cvaO", "al_cva0", "al_cva1", "al_tok", "selw", "hist0", "hist1"]
        OWN_KEYS = ([("PT", i) for i in range(8)] + [("mixA", t) for t in range(4)] + [("mixP", c) for c in range(4)]
                    + ["qT"] + [("mT", c) for c in range(4)])
        ps = [st.enter_context(nc.psum_tensor("ps%d" % i, [128, 512], F32)) for i in range(8)]

        P = Prog(nc)
        rr = {"ps": 0, "xn": 0, "ss": 0, "PT": 0, "rec": 0, "sg": 0, "tmpb": 0, "den": 0, "mtm": 0}

        def nxt(name, n):
            v = rr[name]; rr[name] = (v + 1) % n
            return v

        held = set()

        def bank():
            while True:
                v = nxt("ps", 8)
                if v not in held:
                    return v

        def load_tab(nm, t, dsrc):
            P.dma("sp", (lambda e: e.dma_start(out=t[:], in_=dsrc)), writes=[nm], semkey=("ld", nm))

        def early_loads():
            load_x(-1)
            load_x(0)
            load_tab("g1", g1, g1_d); load_tab("pos0", pos0, pos_d)
            for T in range(1, GT):
                load_x(T)
            load_tab("psc", psc, psc_d); load_tab("sinks", sinks, sink_d); load_tab("g2", g2, g2_d)
            P.dma("sp", lambda e: e.dma_start(out=xsm[:], in_=xs_d), writes=["xsm"], semkey=("ld", "xsm"))
            for T in range(GT, NX - 1):
                load_x(T)
            load_tab("gft", gft, gf_d)
            P.dma("sp", lambda e: e.dma_start(out=nk_d[:, 0:127, :], in_=ck_d[:, 1:128, :]), writes=[("out", "nk0")], semkey=("st", "nk0"))
            P.dma("sp", lambda e: e.dma_start(out=nv_d[:, 0:127, :], in_=cv_d[:, 1:128, :]), writes=[("out", "nv0")], semkey=("st", "nv0"))
            P.dma("sp", lambda e: e.dma_start(out=np_d[:, 0:14, :], in_=sp_d[:, 1:15, :]), writes=[("out", "np0")], semkey=("st", "np0"))
            out_keys.extend([("out", "nk0"), ("out", "nv0"), ("out", "np0")])
        P.op("pool", lambda e: e.memset(expm[:], -0.5), writes=["expm"])
        P.op("pool", lambda e: e.memset(identf[:], 1.0), writes=[("PT", 0)])
        P.op("pool", lambda e: e.affine_select(out=identf[:], in_=identf[:], pattern=[[-1, 128]], compare_op=ALU.is_equal,
                                               fill=0.0, base=0, channel_multiplier=1), reads=[("PT", 0)], writes=[("PT", 0)])
        P.op("pool", lambda e: e.tensor_copy(out=ident[:], in_=identf[:]), reads=[("PT", 0)], writes=["ident"])
        w_in_v = w_in_d.rearrange("(kc p) n -> p kc n", p=128)
        P.dma("pool", lambda e: e.dma_start(out=w_in_sb[:, :, 512:1280], in_=w_in_v[:, :, 512:1280]), writes=["w_in"], semkey=("ld", "w_inA"))
        P.dma("pool", lambda e: e.dma_start(out=w_in_sb[:, :, 0:512], in_=w_in_v[:, :, 0:512]), writes=["w_inq"], semkey=("ld", "w_inB"))
        P.op("pool", lambda e: e.memset(Vaug[:, :, 64:192], 1.0), writes=["Vaug_ones"])
        P.op("pool", lambda e: e.memset(kTp[:], 0.0), writes=["kT_zero"])

        def late_setup():
            P.op("pool", lambda e: e.iota(iot[:], pattern=[[1, 128]], base=0, channel_multiplier=-1), writes=[("PT", 1)])
            P.op("dve", lambda e: e.tensor_copy(out=Rf[:], in_=iot[:]), reads=[("PT", 1)], writes=[("PT", 2)])
            for kvh in range(2):
                for g in range(4):
                    slope = 2.0 ** (-(kvh * 4 + g + 1))
                    for kb in range(2):
                        tb = nxt("tmpb", 2)
                        if kb == 1:
                            P.op("dve", lambda e, tb=tb, slope=slope: e.tensor_scalar(
                                out=tmpb_v[tb], in0=Rf[:], scalar1=-8.0 * slope, scalar2=None, op0=ALU.mult),
                                reads=[("PT", 2)], writes=[("PT", 3 + tb)])
                            P.op("pool", lambda e, tb=tb, kvh=kvh, g=g: e.affine_select(
                                out=biasT[:, 1, kvh, g * 128:(g + 1) * 128], in_=tmpb_v[tb], pattern=[[1, 128]],
                                compare_op=ALU.is_ge, fill=MASKV, base=0, channel_multiplier=-1),
                                reads=[("PT", 3 + tb)], writes=["biasT"])
                        else:
                            P.op("dve", lambda e, tb=tb, slope=slope: e.tensor_scalar(
                                out=tmpb_v[tb], in0=Rf[:], scalar1=128.0, scalar2=-8.0 * slope, op0=ALU.add, op1=ALU.mult),
                                reads=[("PT", 2)], writes=[("PT", 3 + tb)])
                            P.op("pool", lambda e, tb=tb, kvh=kvh, g=g: e.affine_select(
                                out=biasT[:, 0, kvh, g * 128:(g + 1) * 128], in_=tmpb_v[tb], pattern=[[-1, 128]],
                                compare_op=ALU.is_ge, fill=MASKV, base=0, channel_multiplier=1),
                                reads=[("PT", 3 + tb)], writes=["biasT"])
            P.op("act", lambda e: e.activation(out=es[:], in_=sinks[:], func=AF.Exp), reads=["sinks"], writes=["es"])
            P.op("dve", lambda e: e.tensor_copy(out=es_hi[:], in_=es[:]), reads=["es"], writes=["es_hi"])
            P.op("dve", lambda e: e.tensor_copy(out=es_hif[:], in_=es_hi[:]), reads=["es_hi"], writes=["es_hif"])
            P.op("dve", lambda e: e.tensor_tensor(out=es_lo[:], in0=es[:], in1=es_hif[:], op=ALU.subtract),
                 reads=["es", "es_hif"], writes=["es_lo"])
            P.op("pool", lambda e: e.iota(io16[:], pattern=[[1, 16]], base=1, channel_multiplier=0), writes=["io16"])
            P.op("dve", lambda e: e.tensor_copy(out=io16f[:], in_=io16[:]), reads=["io16"], writes=["io16f"])
            for c in range(4):
                P.op("dve", lambda e, c=c: e.tensor_scalar(out=icnt[:, c, :], in0=io16f[:], scalar1=pos0[:, 0:1],
                                                           scalar2=float(2 ** (c + 1)), op0=ALU.add, op1=ALU.min),
                     reads=["io16f", "pos0"], writes=["icnt"])
            P.op("dve", lambda e: e.reciprocal(out=icnt[:], in_=icnt[:]), reads=["icnt"], writes=["icnt"])
            sA, sB = tmpb_v[0], tmpb_v[1]
            sC = PT[:, 5, 0:256].bitcast(F32)
            KA, KB, KC = ("PT", 3), ("PT", 4), ("PT", 5)
            for wi in range(4):
                w = 2 ** (wi + 1)
                P.op("pool", lambda e: e.memset(sA, 1.0), reads=[KA], writes=[KA])
                P.op("pool", lambda e: e.affine_select(out=sA, in_=sA, pattern=[[1, 128]], compare_op=ALU.is_ge, fill=0.0, base=0,
                                                       channel_multiplier=-1), reads=[KA], writes=[KA])
                P.op("pool", lambda e, w=w: e.affine_select(out=sA, in_=sA, pattern=[[-1, 128]], compare_op=ALU.is_ge, fill=0.0, base=w - 1,
                                                            channel_multiplier=1), reads=[KA], writes=[KA])
                P.op("dve", lambda e, w=w: e.tensor_scalar(out=sB, in0=sA, scalar1=1.0 / w, scalar2=None, op0=ALU.mult), reads=[KA, KB], writes=[KB])
                P.op("dve", lambda e, wi=wi: e.tensor_tensor(out=bandM[:, 0, wi, :], in0=sB, in1=identf, op=ALU.subtract),
                     reads=[KB, ("PT", 0), "bandM"], writes=["bandM"])
                P.op("dve", lambda e, w=w: e.memset(sC, 1.0 / w), reads=[KC], writes=[KC])
                P.op("dve", lambda e, wi=wi: e.tensor_copy(out=sC[:, 0:16], in_=icnt[:, wi, :]), reads=[KC, "icnt"], writes=[KC])
                P.op("dve", lambda e: e.tensor_tensor(out=sC, in0=sC, in1=sA, op=ALU.mult), reads=[KC, KA], writes=[KC])
                P.op("dve", lambda e, wi=wi: e.tensor_tensor(out=bandM[:, 2, wi, :], in0=sC, in1=identf, op=ALU.subtract),
                     reads=[KC, ("PT", 0), "bandM"], writes=["bandM"])
                P.op("pool", lambda e, w=w: e.memset(sB, 1.0 / w), reads=[KB], writes=[KB])
                P.op("pool", lambda e, w=w, wi=wi: e.affine_select(out=bandM[:, 1, wi, :], in_=sB, pattern=[[-1, 128]], compare_op=ALU.is_ge,
                                                                   fill=0.0, base=-(129 - w), channel_multiplier=1),
                     reads=[KB, "bandM"], writes=["bandM"])

            P.dma("pool", lambda e: e.dma_start(out=w_pool_sb[:], in_=w_pool_d.rearrange("g c d -> c g d")),
                  writes=["w_pool"], semkey=("ld", "w_pool"))
            P.dma("pool", lambda e: e.dma_start(out=w_out_sb[:], in_=w_out_d.rearrange("(kc p) n -> p kc n", p=128)),
                  writes=["w_out"], semkey=("ld", "w_out"))


        def load_x(T):
            slot = (T % NX) if T >= 0 else NX - 1
            src = x_d[T * 128:(T + 1) * 128, :] if T >= 0 else xh_d
            P.dma("sp", lambda e: e.dma_start(out=xs[:, slot, :], in_=src), writes=[("x", slot)], semkey=("x", slot))

        def xslot(T):
            return (T % NX) if T >= 0 else NX - 1

        def norm_stage(tiles, gam, gam_key, dst, dst_key, rows=128, xn_priv=None):
            n = len(tiles)
            sslot = nxt("ss", 8)
            for i, (xap, xkey, off) in enumerate(tiles):
                jout = junk[0:rows, :] if xn_priv is None else xn_priv[0]
                jw = [] if xn_priv is None else [xn_priv[1]]
                P.op("act", lambda e, xap=xap, i=i, jout=jout: e.activation(out=jout, in_=xap, func=AF.Square,
                                                                            accum_out=ss[0:rows, sslot, i:i + 1]),
                     reads=[xkey] + jw, writes=[("ss", sslot)] + jw)
            P.op("pool", lambda e: e.tensor_scalar(out=var[0:rows, sslot, 0:n], in0=ss[0:rows, sslot, 0:n],
                                                   scalar1=1.0 / D, scalar2=EPS, op0=ALU.mult, op1=ALU.add),
                 reads=[("ss", sslot)], writes=[("var", sslot)])
            P.op("pool", lambda e: e.tensor_tensor(out=rstd[0:rows, sslot, 0:n], in0=var[0:rows, sslot, 0:n],
                                                   in1=expm[0:rows, 0:n], op=ALU.pow),
                 reads=[("var", sslot), "expm"], writes=[("rstd", sslot)])
            for i, (xap, xkey, off) in enumerate(tiles):
                if xn_priv is None:
                    s = nxt("xn", 2)
                    xn_ap, xn_key = xn[0:rows, s, :], ("xn", s)
                else:
                    xn_ap, xn_key = xn_priv
                P.op("act", lambda e, xap=xap, i=i, xn_ap=xn_ap: e.activation(out=xn_ap, in_=xap, func=AF.Copy,
                                                                              scale=rstd[0:rows, sslot, i:i + 1]),
                     reads=[xkey, ("rstd", sslot), xn_key], writes=[xn_key])
                b = bank()
                psb = ps[b][:].bitcast(BF16).rearrange("p (k t) -> p k t", t=128)
                for kc in range(8):
                    P.op("pe", lambda e, kc=kc, xn_ap=xn_ap, psb=psb: e.transpose(psb[:, kc, 0:rows], xn_ap[:, kc * 128:(kc + 1) * 128],
                                                                                  ident[0:rows, 0:rows]),
                         reads=[xn_key, "ident"], writes=[("ps", b)])
                P.op("dve", lambda e, psb=psb, off=off: e.tensor_tensor(
                    out=dst[:, :, off:off + rows], in0=psb[:, :, 0:rows], in1=gam[:, :, None].broadcast_to([128, 8, rows]),
                    op=ALU.mult), reads=[("ps", b), gam_key], writes=[dst_key])
            return sslot

        def mm_group(out_ap, b, pairs, reads):
            n = len(pairs)
            for i, (l, r) in enumerate(pairs):
                P.op("pe", lambda e, l=l, r=r, i=i: e.matmul(out_ap, lhsT=l, rhs=r, start=(i == 0), stop=(i == n - 1)),
                     reads=reads, writes=[("ps", b)])

        gu_issued = [0]
        wd_issued = [0]

        def issue_gu(upto):
            while gu_issued[0] < upto and gu_issued[0] < NG * NF:
                i = gu_issued[0]; f = i % NF; s = i % NWGU
                P.dma("pool", lambda e, f=f, s=s: e.dma_start(out=wgu[:, s].rearrange("p a k n -> p (a k n)"), in_=w_gu_d[f],
                                                              max_dma_last_dim=4096),
                      writes=[("wgu", s)], semkey=("wgu", s))
                gu_issued[0] += 1

        def issue_wd(upto):
            while wd_issued[0] < upto and wd_issued[0] < NG * 2 * NF:
                i = wd_issued[0]; j = i % (2 * NF); s = i % NWD
                P.dma("pool", lambda e, j=j, s=s: e.dma_start(out=wd[:, s, :], in_=w_d_d[j]),
                      writes=[("wd", s)], semkey=("wd", s))
                wd_issued[0] += 1

        out_keys = []
        SLOPES = [2.0 ** (-(h + 1)) for h in range(8)]

        CVA = ["al_cvaO", "al_cva0", "al_cva1"]

        def smp_dma():
            P.dma("pool", lambda e: e.dma_start(out=ckb, in_=ck_d.rearrange("b k f -> k b f")), reads=["al_ckb"], writes=["al_ckb"], semkey=("ld", "ckb"))
            P.op("pool", lambda e: e.memset(cva[:, :, 64:192], 1.0), reads=["al_cvaO"], writes=["al_cvaO"])
            cvsrc = cv_d.rearrange("b k f -> k b f")
            P.dma("pool", lambda e: e.dma_start(out=cva[:, :, 0:64], in_=cvsrc[:, :, 0:64]), reads=["al_cva0"], writes=["al_cva0"], semkey=("ld", "cva0"))
            P.dma("pool", lambda e: e.dma_start(out=cva[:, :, 192:256], in_=cvsrc[:, :, 64:128]), reads=["al_cva1"], writes=["al_cva1"], semkey=("ld", "cva1"))
            sp2 = sp_d.rearrange("b j c -> (b j) c")
            P.dma("sp", lambda e: e.dma_start(out=hist_v[0], in_=sp2[0:128, :]), reads=["hist0"], writes=["hist0"], semkey=("ld", "h0"))
            P.op("pool", lambda e: e.memset(hist_v[1], 0.0), reads=["hist1"], writes=["hist1"])
            P.dma("sp", lambda e: e.dma_start(out=hist_v[1][0:112, :], in_=sp2[128:240, :]), reads=["hist1"], writes=["hist1"], semkey=("ld", "h1"))

        def smp_setup():
            P.op("pool", lambda e: e.memset(vnew[:], 0.0), writes=["vnew"])
            P.op("pool", lambda e: e.memset(vnew[0:NS, 64:192], 1.0), reads=["vnew"], writes=["vnew"])
            P.op("pool", lambda e: e.memset(pbd[:], 0.0), writes=["pbd"])
            P.op("dve", lambda e: e.tensor_copy(out=relc[:], in_=iop[:]), reads=["iop"], writes=["relc"])
            for h in range(8):
                P.op("dve", lambda e, h=h: e.tensor_scalar(out=sbias[:, h:h + 1], in0=relc[:], scalar1=-8.0 * SLOPES[h], scalar2=None,
                                                           op0=ALU.mult), reads=["relc", "sbias"], writes=["sbias"])
            P.op("pool", lambda e: e.memset(bdmask[:], 1.0), writes=["bdmask"])
            P.op("pool", lambda e: e.affine_select(out=bdmask[:], in_=bdmask[:], pattern=[[0, 2], [1, NS], [0, 4]],
                                                   compare_op=ALU.is_equal, fill=0.0, base=0, channel_multiplier=-1),
                 reads=["bdmask"], writes=["bdmask"])
            P.op("dve", lambda e: e.tensor_copy(out=essm[:], in_=es[:].rearrange("p (k g) -> p k g", g=4)[:, :, None, :].broadcast_to([128, 2, NS, 4])),
                 reads=["es"], writes=["essm"])

        def smp_setup_selw():
            P.op("pool", lambda e: e.memset(selw[:], 1.0), reads=["selw"], writes=["selw"])
            for kt in range(2):
                for c in range(4):
                    w = 2 ** (c + 1)
                    P.op("pool", lambda e, kt=kt, c=c, w=w: e.affine_select(
                        out=selw[:, kt, c, :], in_=selw[:, kt, c, :], pattern=[[-15, NS]], compare_op=ALU.is_ge, fill=0.0,
                        base=kt * 128 - (16 - w), channel_multiplier=1), reads=["selw"], writes=["selw"])
                    P.op("pool", lambda e, kt=kt, c=c: e.affine_select(
                        out=selw[:, kt, c, :], in_=selw[:, kt, c, :], pattern=[[15, NS]], compare_op=ALU.is_ge, fill=0.0,
                        base=14 - kt * 128, channel_multiplier=-1), reads=["selw"], writes=["selw"])

        def smp_norm1():
            norm_stage([(xsm[:, :], "xsm", 0)], g1, "g1", hTs, "hTs", rows=NS, xn_priv=(xns, "al_tok"))

        def smp_inproj():
            b = bank()
            for c in range(4):
                mm_group(ps[b][:, c * NS:(c + 1) * NS], b, [(w_in_sb[:, kc, c * 128:(c + 1) * 128], hTs[:, kc, :]) for kc in range(8)],
                         ["w_inq", "hTs"])
            for c in range(4):
                mm_group(ps[b][:, (4 + c) * NS:(5 + c) * NS], b,
                         [(w_in_sb[:, kc, 768 + c * 128:768 + (c + 1) * 128], hTs[:, kc, :]) for kc in range(8)], ["w_in", "hTs"])
            P.op("dve", lambda e, b=b: e.tensor_copy(out=qTs[:], in_=ps[b][:, 0:4 * NS].rearrange("p (c n) -> p c n", n=NS)),
                 reads=[("ps", b)], writes=["qTs"])
            P.op("dve", lambda e, b=b: e.tensor_copy(out=uTs[:], in_=ps[b][:, 4 * NS:8 * NS].rearrange("p (c n) -> p c n", n=NS)),
                 reads=[("ps", b)], writes=["uTs"])
            for (c0, c1) in ((0, 512), (512, 1024), (1024, 1280)):
                b = bank()
                mm_group(ps[b][0:NS, 0:c1 - c0], b, [(hTs[:, kc, :], w_in_sb[:, kc, c0:c1]) for kc in range(8)],
                         ["w_inq" if c0 == 0 else "w_in", "hTs"])
                P.op("dve", lambda e, b=b, c0=c0, c1=c1: e.tensor_copy(out=tok_s[:, c0:c1], in_=ps[b][0:NS, 0:c1 - c0]),
                     reads=[("ps", b), "al_tok"], writes=["al_tok"])
            P.dma("sp", lambda e: e.dma_start(out=nk_d[:, 127, :], in_=tok_s[:, 512:640]), reads=["al_tok"], writes=[("out", "nk1")], semkey=("st", "nk1"))
            P.dma("sp", lambda e: e.dma_start(out=nv_d[:, 127, :], in_=tok_s[:, 640:768]), reads=["al_tok"], writes=[("out", "nv1")], semkey=("st", "nv1"))
            P.dma("sp", lambda e: e.dma_start(out=np_d[:, 14, :], in_=tok_s[:, 768:1280]), reads=["al_tok"], writes=[("out", "np1")], semkey=("st", "np1"))
            out_keys.extend([("out", "nk1"), ("out", "nv1"), ("out", "np1")])
            prodv = rec[0:NS, 0, :].rearrange("p (j h d) -> p j h d", h=2, d=64)
            P.op("dve", lambda e: e.tensor_tensor(
                out=prodv, in0=tok_s[:, 0:512].rearrange("p (j h d) -> p j h d", h=2, d=64),
                in1=tok_s[:, 512:640].rearrange("p (h d) -> p h d", d=64)[:, None, :, :].broadcast_to([NS, 4, 2, 64]), op=ALU.mult),
                reads=["al_tok"], writes=[("rec", 0)])
            P.op("dve", lambda e: e.tensor_reduce(out=snew[:], in_=rec[0:NS, 0, :].rearrange("p (a d) -> p a d", d=64), axis=AX.X, op=ALU.add),
                 reads=[("rec", 0)], writes=["snew"])
            P.op("act", lambda e: e.activation(out=pnew[:], in_=snew[:], func=AF.Exp, scale=0.125), reads=["snew"], writes=["pnew"])
            P.op("dve", lambda e: e.tensor_tensor(
                out=pbd[0:NS], in0=bdmask[:], in1=pnew[:].rearrange("p (j h) -> p h j", h=2)[:, :, None, :].broadcast_to([NS, 2, NS, 4]),
                op=ALU.mult), reads=["bdmask", "pnew", "pbd"], writes=["pbd"])
            P.op("dve", lambda e: e.tensor_copy(out=vnew[0:NS, :].rearrange("p (q d) -> p q d", d=64)[:, 0:4:3, :],
                                                in_=tok_s[:, 640:768].rearrange("p (h d) -> p h d", d=64)),
                 reads=["al_tok", "vnew"], writes=["vnew"])

        def smp_ktrans():
            for half in range(2):
                b = bank()
                psb = ps[b][:].bitcast(BF16).rearrange("p (k t) -> p k t", t=128)
                for i in range(8):
                    bb = half * 8 + i
                    P.op("pe", lambda e, psb=psb, i=i, bb=bb: e.transpose(psb[:, i, :], ckb[:, bb, :], ident[:]),
                         reads=["al_ckb", "ident"], writes=[("ps", b)])
                P.op("dve", lambda e, psb=psb, half=half: e.tensor_copy(out=ckT[:, half * 8:(half + 1) * 8, :], in_=psb),
                     reads=[("ps", b), "al_ckT"], writes=["al_ckT"])

        def smp_scores():
            for kvh in range(2):
                b = bank()
                r0 = kvh * 64
                for bb in range(NS):
                    P.op("pe", lambda e, b=b, bb=bb, r0=r0: e.matmul(
                        ps[b][:, bb * 4:bb * 4 + 4], lhsT=ckT[r0:r0 + 64, bb, :], rhs=qTs[r0:r0 + 64, :, bb],
                        start=True, stop=True), reads=["al_ckT", "qTs"], writes=[("ps", b)])
                P.op("dve", lambda e, b=b, kvh=kvh: e.tensor_tensor(
                    out=s_sb[:, :, kvh * 4:(kvh + 1) * 4], in0=ps[b][:, 0:NS * 4].rearrange("p (b g) -> p b g", g=4),
                    in1=sbias[:, None, kvh * 4:(kvh + 1) * 4].broadcast_to([128, NS, 4]), op=ALU.add),
                    reads=[("ps", b), "sbias", "s_sb"], writes=["s_sb"])
            P.op("act", lambda e: e.activation(out=PTs[:], in_=s_sb[:], func=AF.Exp, scale=0.125), reads=["s_sb"], writes=["PTs"])

        def smp_pv():
            for kvh in range(2):
                b = bank()
                a0, s0 = (0, 64) if kvh == 0 else (64, 0)
                for bb in range(NS):
                    P.op("pe", lambda e, b=b, kvh=kvh, bb=bb: e.matmul(
                        ps[b][:, bb * 4:(bb + 1) * 4], lhsT=vnew[:, kvh * 128:(kvh + 1) * 128], rhs=pbd[:, kvh, bb, :],
                        start=True, stop=False), reads=["vnew", "pbd"], writes=[("ps", b)])
                    P.op("pe", lambda e, b=b, kvh=kvh, bb=bb: e.matmul(
                        ps[b][:, bb * 4:(bb + 1) * 4], lhsT=cva[:, bb, kvh * 128:(kvh + 1) * 128], rhs=PTs[:, bb, kvh * 4:(kvh + 1) * 4],
                        start=False, stop=True), reads=CVA + ["PTs"], writes=[("ps", b)])
                P.op("dve", lambda e, b=b, kvh=kvh, s0=s0: e.tensor_tensor(
                    out=recs[s0:s0 + 64, kvh, :], in0=ps[b][s0:s0 + 64, 0:NS * 4],
                    in1=essm[s0:s0 + 64, kvh, :, :].rearrange("p b g -> p (b g)"), op=ALU.add),
                    reads=[("ps", b), "essm"], writes=[("recs", kvh)])
                P.op("dve", lambda e, kvh=kvh, s0=s0: e.reciprocal(out=recs[s0:s0 + 64, kvh, :], in_=recs[s0:s0 + 64, kvh, :]),
                     reads=[("recs", kvh)], writes=[("recs", kvh)])
                P.op("dve", lambda e, b=b, kvh=kvh, s0=s0, a0=a0: e.tensor_tensor(
                    out=mixTs[a0:a0 + 64, 0:4, :].rearrange("p g b -> p b g"),
                    in0=ps[b][a0:a0 + 64, 0:NS * 4].rearrange("p (b g) -> p b g", g=4),
                    in1=recs[s0:s0 + 64, kvh, :].rearrange("p (b g) -> p b g", g=4), op=ALU.mult),
                    reads=[("ps", b), ("recs", kvh)], writes=["mixTs"])

        def smp_pool():
            b = bank()
            for c in range(4):
                for kt, rows in ((0, 128), (1, 128)):
                    P.op("pe", lambda e, b=b, c=c, kt=kt, rows=rows: e.matmul(
                        ps[b][:, c * NS:(c + 1) * NS], lhsT=hist_v[kt][0:rows, c * 128:(c + 1) * 128], rhs=selw[0:rows, kt, c, :],
                        start=(kt == 0), stop=(kt == 1)), reads=["hist%d" % kt, "selw"], writes=[("ps", b)])
            P.op("dve", lambda e, b=b: e.tensor_tensor(out=tmps[:], in0=ps[b][:, 0:4 * NS].rearrange("p (c n) -> p c n", n=NS),
                                                       in1=uTs[:], op=ALU.add), reads=[("ps", b), "uTs"], writes=["tmps"])
            for c in range(4):
                P.op("dve", lambda e, c=c: e.scalar_tensor_tensor(out=mTs[:, c, :], in0=tmps[:, c, :], scalar=1.0 / (2 ** (c + 1)),
                                                                  in1=uTs[:, c, :], op0=ALU.mult, op1=ALU.subtract),
                     reads=["tmps", "uTs", "mTs"], writes=["mTs"])
            b2 = bank()
            for c in range(4):
                P.op("pe", lambda e, b2=b2, c=c: e.matmul(ps[b2][:, c * NS:(c + 1) * NS], lhsT=w_pool_sb[:, c, :], rhs=mTs[:, c, :],
                                                          start=True, stop=True), reads=["w_pool", "mTs"], writes=[("ps", b2)])
            for c in range(4):
                P.op("act", lambda e, b2=b2, c=c: e.activation(out=mixTs[:, 4 + c, :], in_=ps[b2][:, c * NS:(c + 1) * NS], func=AF.Copy,
                                                               scale=psc[:, c:c + 1]), reads=[("ps", b2), "psc", "mixTs"], writes=["mixTs"])

        def smp_wout():
            for hf in range(2):
                b = bank()
                mm_group(ps[b][0:NS, :], b, [(mixTs[:, ch, :], w_out_sb[:, ch, hf * 512:(hf + 1) * 512]) for ch in range(8)],
                         ["mixTs", "w_out"])
                P.op("dve", lambda e, b=b, hf=hf: e.tensor_tensor(out=xsm[:, hf * 512:(hf + 1) * 512], in0=ps[b][0:NS, :],
                                                                  in1=xsm[:, hf * 512:(hf + 1) * 512], op=ALU.add),
                     reads=[("ps", b), "xsm"], writes=["xsm"])
            norm_stage([(xsm[:, :], "xsm", 0)], g2, "g2", h2Ts, "h2Ts", rows=NS, xn_priv=(xns, "al_tok"))

        def smp_final():
            sslot = nxt("ss", 8)
            P.op("act", lambda e: e.activation(out=junk[0:NS, :], in_=xsm[:], func=AF.Square, accum_out=ss[0:NS, sslot, 0:1]),
                 reads=["xsm"], writes=[("ss", sslot)])
            P.op("pool", lambda e: e.tensor_scalar(out=var[0:NS, sslot, 0:1], in0=ss[0:NS, sslot, 0:1], scalar1=1.0 / D, scalar2=EPS,
                                                   op0=ALU.mult, op1=ALU.add), reads=[("ss", sslot)], writes=[("var", sslot)])
            P.op("pool", lambda e: e.tensor_tensor(out=rstd[0:NS, sslot, 0:1], in0=var[0:NS, sslot, 0:1], in1=expm[0:NS, 0:1], op=ALU.pow),
                 reads=[("var", sslot), "expm"], writes=[("rstd", sslot)])
            P.op("dve", lambda e: e.scalar_tensor_tensor(out=xsm[:], in0=xsm[:], scalar=rstd[0:NS, sslot, 0:1], in1=gft[0:NS, :],
                                                         op0=ALU.mult, op1=ALU.mult), reads=["xsm", ("rstd", sslot), "gft"], writes=["xsm"])
            P.dma("sp", lambda e: e.dma_start(out=ys_d, in_=xsm[:]), reads=["xsm"], writes=[("out", "ys")], semkey=("st", "ys"))
            out_keys.append(("out", "ys"))

        early_loads()
        P.op("dve", lambda e: e.tensor_scalar(out=flag[:], in0=pos0[:], scalar1=1.0, scalar2=None, op0=ALU.min),
             reads=["pos0"], writes=["flag"])
        P.op("pool", lambda e: e.iota(iop[:], pattern=[[0, 1]], base=128, channel_multiplier=-1), writes=["iop"])

        hs = xslot(-1)
        norm_stage([(xs[:, hs, :], ("x", hs), 0)], g1, "g1", h2T, ("h2T", 0))
        norm_stage([(xs[:, xslot(T), :], ("x", xslot(T)), T * 128) for T in range(GT)], g1, "g1", hT, "hT")
        b = bank()
        mm_group(ps[b][:, 0:128], b, [(w_in_sb[:, kc, 512:640], h2T[:, kc, 0:128]) for kc in range(8)], ["w_in", ("h2T", 0)])
        for kvh in range(2):
            r0 = kvh * 64
            P.op("dve", lambda e, b=b, kvh=kvh, r0=r0: e.tensor_copy(out=kTp[r0:r0 + 64, kvh, NKR - 1, :], in_=ps[b][r0:r0 + 64, 0:128]),
                 reads=[("ps", b), "kT_zero", ("kT", NKR - 1)], writes=[("kT", NKR - 1)])
        b = bank()
        mm_group(ps[b][:], b, [(h2T[:, kc, 0:128], w_in_sb[:, kc, 768:1280]) for kc in range(8)], ["w_in", ("h2T", 0)])
        P.op("dve", lambda e, b=b: e.tensor_scalar(out=u_tm[:, 0, :], in0=ps[b][:], scalar1=flag[:, 0:1], scalar2=None, op0=ALU.mult),
             reads=[("ps", b), "flag"], writes=[("utm", 0)])
        b = bank()
        mm_group(ps[b][:, 0:128], b, [(h2T[:, kc, 0:128], w_in_sb[:, kc, 640:768]) for kc in range(8)], ["w_in", ("h2T", 0)])
        vview0 = Vaug[:, 0, :].rearrange("p (b d) -> p b d", d=64)[:, 0:4:3, :]
        P.op("dve", lambda e, b=b: e.tensor_scalar(out=vview0, in0=ps[b][:, 0:128].rearrange("p (b d) -> p b d", d=64),
                                                   scalar1=flag[:, 0:1], scalar2=None, op0=ALU.mult),
             reads=[("ps", b), "flag", "Vaug_ones"], writes=[("V", 0)])
        P.op("dve", lambda e: e.tensor_copy(out=Vaug[:, 0, 64:192], in_=flag[:, 0:1].broadcast_to([128, 128])),
             reads=["flag", "Vaug_ones", ("V", 0)], writes=[("V", 0)])

        def norm_pre(xap, xkey, rows=128):
            sslot = nxt("ss", 8)
            sx = nxt("xn", 2)
            P.op("act", lambda e: e.activation(out=xn[0:rows, sx, :], in_=xap, func=AF.Square, accum_out=ss[0:rows, sslot, 0:1]),
                 reads=[xkey], writes=[("ss", sslot), ("xn", sx)])
            P.op("pool", lambda e: e.tensor_scalar(out=var[0:rows, sslot, 0:1], in0=ss[0:rows, sslot, 0:1], scalar1=1.0 / D, scalar2=EPS,
                                                   op0=ALU.mult, op1=ALU.add), reads=[("ss", sslot)], writes=[("var", sslot)])
            P.op("pool", lambda e: e.tensor_tensor(out=rstd[0:rows, sslot, 0:1], in0=var[0:rows, sslot, 0:1], in1=expm[0:rows, 0:1], op=ALU.pow),
                 reads=[("var", sslot), "expm"], writes=[("rstd", sslot)])
            P.op("dve", lambda e: e.tensor_scalar(out=xn[0:rows, sx, :], in0=xap, scalar1=rstd[0:rows, sslot, 0:1], scalar2=None, op0=ALU.mult),
                 reads=[xkey, ("rstd", sslot), ("xn", sx)], writes=[("xn", sx)])
            return sx

        def norm_pe(sx, gam, gam_key, dst, dst_key, off, rows=128):
            b = bank()
            psb = ps[b][:].bitcast(BF16).rearrange("p (k t) -> p k t", t=128)
            for kc in range(8):
                P.op("pe", lambda e, kc=kc: e.transpose(psb[:, kc, 0:rows], xn[0:rows, sx, kc * 128:(kc + 1) * 128], ident[0:rows, 0:rows]),
                     reads=[("xn", sx), "ident"], writes=[("ps", b)])
            P.op("dve", lambda e: e.tensor_tensor(out=dst[:, :, off:off + rows], in0=psb[:, :, 0:rows],
                                                  in1=gam[:, :, None].broadcast_to([128, 8, rows]), op=ALU.mult),
                 reads=[("ps", b), gam_key], writes=[dst_key])

        def attn_scores(g, t, T):
            pts = {}
            for kb, Tk in ((0, T - 1), (1, T)):
                ks = Tk % NKR
                for kvh in range(2):
                    b = bank()
                    pslot = nxt("PT", 8)
                    pts[(kb, kvh)] = pslot
                    P.op("pe", lambda e, b=b, ks=ks, kvh=kvh: e.matmul(
                        ps[b][:], lhsT=kTp[:, kvh, ks, :], rhs=qT[:, :, t * 128:(t + 1) * 128],
                        start=True, stop=False), reads=[("kT", ks), "qT"], writes=[("ps", b)])
                    P.op("pe", lambda e, b=b, kb=kb, kvh=kvh: e.matmul(ps[b][:], lhsT=ident[:], rhs=biasT[:, kb, kvh, :],
                                                                        start=False, stop=True),
                         reads=["ident", "biasT"], writes=[("ps", b)])
                    P.op("act", lambda e, b=b, pslot=pslot: e.activation(out=PT[:, pslot, :], in_=ps[b][:], func=AF.Exp, scale=0.125),
                         reads=[("ps", b)], writes=[("PT", pslot)])
            return pts

        def attn_pv(g, t, T, pts):
            pb = {}
            for kvh in range(2):
                b = bank()
                pb[kvh] = b
                vlo, vhi = (0, 65) if kvh == 0 else (191, 256)
                for gg in range(4):
                    for kb, Tk in ((0, T - 1), (1, T)):
                        vs = (Tk + 1) % NV
                        P.op("pe", lambda e, b=b, kvh=kvh, gg=gg, kb=kb, vs=vs, vlo=vlo, vhi=vhi, s_=pts[(kb, kvh)]: e.matmul(
                            ps[b][:, gg * 65:(gg + 1) * 65], lhsT=PT[:, s_, gg * 128:(gg + 1) * 128], rhs=Vaug[:, vs, vlo:vhi],
                            start=(kb == 0), stop=(kb == 1)), reads=[("V", vs), ("PT", pts[(kb, kvh)])], writes=[("ps", b)])
            ds = nxt("den", 2)
            ms = nxt("mtm", 2)
            for kvh in range(2):
                b = pb[kvh]
                pv = ps[b][:, 0:260].rearrange("p (g c) -> p g c", c=65)
                rc = 64 if kvh == 0 else 0
                P.op("dve", lambda e, pv=pv, rc=rc, kvh=kvh, ds=ds: e.tensor_tensor(
                    out=den[:, ds, kvh * 4:(kvh + 1) * 4, :], in0=pv[:, :, rc:rc + 1], in1=es[:, kvh * 4:(kvh + 1) * 4, None], op=ALU.add),
                    reads=[("ps", b), "es", ("den", ds)], writes=[("den", ds)])
            P.op("dve", lambda e, ds=ds: e.reciprocal(out=den[:, ds, :, :], in_=den[:, ds, :, :]), reads=[("den", ds)], writes=[("den", ds)])
            mv = mix_tm[:, ms, :].rearrange("p (j h d) -> p j h d", h=2, d=64)
            for kvh in range(2):
                b = pb[kvh]
                pv = ps[b][:, 0:260].rearrange("p (g c) -> p g c", c=65)
                a0 = 0 if kvh == 0 else 1
                P.op("dve", lambda e, pv=pv, a0=a0, kvh=kvh, ds=ds, mv=mv: e.tensor_tensor(
                    out=mv[:, :, kvh, :], in0=pv[:, :, a0:a0 + 64], in1=den[:, ds, kvh * 4:(kvh + 1) * 4, :].broadcast_to([128, 4, 64]),
                    op=ALU.mult), reads=[("ps", b), ("den", ds), ("mtm", ms)], writes=[("mtm", ms)])
            return ms

        def attn_tr(t, ms):
            bt = bank()
            psb = ps[bt][:].bitcast(BF16)[:, 0:512].rearrange("p (j q) -> p j q", q=128)
            for j in range(4):
                P.op("pe", lambda e, psb=psb, j=j, ms=ms: e.transpose(psb[:, j, :], mix_tm[:, ms, j * 128:(j + 1) * 128], ident[:]),
                     reads=[("mtm", ms), "ident"], writes=[("ps", bt)])
            P.op("act", lambda e, psb=psb: e.activation(out=mixT[:, 0:4, t * 128:(t + 1) * 128], in_=psb, func=AF.Copy),
                 reads=[("ps", bt)], writes=[("mixA", t)])

        load_x(NX - 1)
        if SMP_LEVEL >= 9:
            P.capture = []
            smp_norm1(); smp_inproj(); smp_ktrans(); smp_scores(); smp_pv(); smp_pool(); smp_wout()
            P.queue = P.capture
            P.capture = None
        SMP_G = 1

        hooks_on = [False]

        def hook():
            if hooks_on[0]:
                P.flush_chunk()

        for g in range(NG):
            tiles = [g * GT + t for t in range(GT)]
            if g == 0:
                late_setup()
            for t, T in enumerate(tiles):
                b = bank()
                us_ = (T + 1) % NU
                mm_group(ps[b][:], b, [(hT[:, kc, t * 128:(t + 1) * 128], w_in_sb[:, kc, 768:1280]) for kc in range(8)], ["w_in", "hT"])
                P.op("dve", lambda e, b=b, us_=us_: e.tensor_copy(out=u_tm[:, us_, :], in_=ps[b][:]), reads=[("ps", b), ("utm", us_)],
                     writes=[("utm", us_)])
                hook()
            for c in range(4):
                b = bank()
                mm_group(ps[b][:], b, [(w_in_sb[:, kc, c * 128:(c + 1) * 128], hT[:, kc, :]) for kc in range(8)], ["w_inq", "hT"])
                P.op("dve", lambda e, b=b, c=c: e.tensor_copy(out=qT[:, c, :], in_=ps[b][:]), reads=[("ps", b)], writes=["qT"])
                hook()
            b = bank()
            mm_group(ps[b][:], b, [(w_in_sb[:, kc, 512:640], hT[:, kc, :]) for kc in range(8)], ["w_in", "hT"])
            ks0 = (g * GT) % NKR
            for kvh in range(2):
                r0 = kvh * 64
                P.op("dve", lambda e, b=b, kvh=kvh, r0=r0, ks0=ks0: e.tensor_copy(
                    out=kTp[r0:r0 + 64, kvh, ks0:ks0 + GT, :], in_=ps[b][r0:r0 + 64, :].rearrange("p (t n) -> p t n", n=128)),
                    reads=[("ps", b), "kT_zero"] + [("kT", ks0 + i) for i in range(GT)], writes=[("kT", ks0 + i) for i in range(GT)])
            hook()
            b = bank()
            for t, T in enumerate(tiles):
                mm_group(ps[b][:, t * 128:(t + 1) * 128], b,
                         [(hT[:, kc, t * 128:(t + 1) * 128], w_in_sb[:, kc, 640:768]) for kc in range(8)], ["w_in", "hT"])
            for t, T in enumerate(tiles):
                vs_ = (T + 1) % NV
                vv = Vaug[:, vs_, :].rearrange("p (b d) -> p b d", d=64)[:, 0:4:3, :]
                if T + 1 == NV:
                    P.op("dve", lambda e: e.memset(Vaug[:, 0, 64:192], 1.0), reads=[("V", 0)], writes=[("V", 0)])
                P.op("dve", lambda e, b=b, t=t, vv=vv: e.tensor_copy(
                    out=vv, in_=ps[b][:, t * 128:(t + 1) * 128].rearrange("p (b d) -> p b d", d=64)),
                    reads=[("ps", b), "Vaug_ones", ("V", vs_)], writes=[("V", vs_)])
            hook()
            if g == NG - 1:
                b1 = bank()
                mm_group(ps[b1][:], b1, [(hT[:, kc, 384:512], w_in_sb[:, kc, 512:1024]) for kc in range(8)], ["w_in", "hT"])
                P.op("dve", lambda e, b1=b1: e.tensor_copy(out=klast[:, 0:512], in_=ps[b1][:]), reads=[("ps", b1)], writes=KL)
                b2 = bank()
                mm_group(ps[b2][:, 0:256], b2, [(hT[:, kc, 384:512], w_in_sb[:, kc, 1024:1280]) for kc in range(8)], ["w_in", "hT"])
                P.op("dve", lambda e, b2=b2: e.tensor_copy(out=klast[:, 512:768], in_=ps[b2][:, 0:256]),
                     reads=[("ps", b2)] + KL, writes=KL)
                P.dma("sp", lambda e: e.dma_start(out=kvu_d, in_=klast), reads=KL, writes=[("out", "kvu")],
                      semkey=("st", "kvu"))
                out_keys.append(("out", "kvu"))
            for c in range(4):
                b = bank()
                for t, T in enumerate(tiles):
                    us_, up_ = (T + 1) % NU, T % NU
                    kind = 2 if T == 0 else 0
                    P.op("pe", lambda e, b=b, t=t, us_=us_, kind=kind, c=c: e.matmul(
                        ps[b][:, t * 128:(t + 1) * 128], lhsT=u_tm[:, us_, c * 128:(c + 1) * 128], rhs=bandM[:, kind, c, :], start=True, stop=False),
                        reads=[("utm", us_), "bandM"], writes=[("ps", b)])
                    P.op("pe", lambda e, b=b, t=t, up_=up_, c=c: e.matmul(
                        ps[b][:, t * 128:(t + 1) * 128], lhsT=u_tm[:, up_, c * 128:(c + 1) * 128], rhs=bandM[:, 1, c, :], start=False, stop=True),
                        reads=[("utm", up_), "bandM"], writes=[("ps", b)])
                P.op("act", lambda e, b=b, c=c: e.activation(out=mT[:, c, :], in_=ps[b][:], func=AF.Copy), reads=[("ps", b)], writes=[("mT", c)])
            if g == 0:
                smp_setup()
                issue_gu(NWGU)
                issue_wd(NWD)
            def pool_mm():
                for c in range(4):
                    b = bank()
                    P.op("pe", lambda e, b=b, c=c: e.matmul(ps[b][:], lhsT=w_pool_sb[:, c, :], rhs=mT[:, c, :], start=True, stop=True),
                         reads=["w_pool", ("mT", c)], writes=[("ps", b)])
                    P.op("act", lambda e, b=b, c=c: e.activation(out=mixT[:, 4 + c, :], in_=ps[b][:], func=AF.Copy, scale=psc[:, c:c + 1]),
                         reads=[("ps", b), "psc"], writes=[("mixP", c)])

            npre = {}

            def wout(t):
                T = tiles[t]
                sl = xslot(T)
                for hf in range(2):
                    b = bank()
                    mm_group(ps[b][:], b, [(mixT[:, ch, t * 128:(t + 1) * 128], w_out_sb[:, ch, hf * 512:(hf + 1) * 512]) for ch in range(8)],
                             [("mixA", t), "w_out"] + [("mixP", c) for c in range(4)])
                    P.op("dve", lambda e, b=b, sl=sl, hf=hf: e.tensor_tensor(
                        out=xs[:, sl, hf * 512:(hf + 1) * 512], in0=ps[b][:], in1=xs[:, sl, hf * 512:(hf + 1) * 512], op=ALU.add),
                        reads=[("ps", b), ("x", sl)], writes=[("x", sl)])
                npre[t] = norm_pre(xs[:, sl, :], ("x", sl))

            def n2pe(t):
                norm_pe(npre[t], g2, "g2", h2T, ("h2T", t), t * 128)

            pts = {0: attn_scores(g, 0, tiles[0])}
            hook()
            pts[1] = attn_scores(g, 1, tiles[1]); hook()
            mss = {}
            mss[0] = attn_pv(g, 0, tiles[0], pts[0]); hook()
            pool_mm(); hook()
            pts[2] = attn_scores(g, 2, tiles[2]); hook()
            mss[1] = attn_pv(g, 1, tiles[1], pts[1]); hook()
            attn_tr(0, mss[0])
            pts[3] = attn_scores(g, 3, tiles[3]); hook()
            mss[2] = attn_pv(g, 2, tiles[2], pts[2]); hook()
            attn_tr(1, mss[1])
            wout(0); hook()
            mss[3] = attn_pv(g, 3, tiles[3], pts[3]); hook()
            attn_tr(2, mss[2])
            wout(1); hook()
            n2pe(0)
            attn_tr(3, mss[3])
            wout(2); hook()
            n2pe(1)
            wout(3); hook()
            H2 = [("h2T", 0), ("h2T", 1), ("h2T", 2), ("h2T", 3)]
            NSPLIT = 2
            early = {}

            def gu_half(f, half):
                gi = g * NF + f
                s_ = gi % NWGU
                if half == 0:
                    early[f] = (bank(), bank())
                    held.update(early[f])
                bg_, bu_ = early[f]
                c0, c1 = half * 256, (half + 1) * 256
                rk = [("wgu", s_)] + H2[2 * half:2 * half + 2]
                mm_group(ps[bg_][:, c0:c1], bg_, [(wgu[:, s_, 0, kc, :], h2T[:, kc, c0:c1]) for kc in range(8)], rk)
                mm_group(ps[bu_][:, c0:c1], bu_, [(wgu[:, s_, 1, kc, :], h2T[:, kc, c0:c1]) for kc in range(8)], rk)

            for f in range(NSPLIT):
                gu_half(f, 0)
            n2pe(2)
            n2pe(3)
            for f in range(NSPLIT):
                gu_half(f, 1)
                held.difference_update(early[f])

            smp_here = (g == SMP_G and SMP_LEVEL >= 9)
            smp_front = (g == 0 and SMP_LEVEL >= 9)
            if smp_front:
                P.op("pool", lambda e: e.memset(gate_t[:], 0.0), writes=OWN_KEYS + AL_KEYS)
                smp_dma()
                P.flush_chunk()
                hooks_on[0] = True
            for f in range(NF):
                gi = g * NF + f
                s = gi % NWGU
                if f < NSPLIT:
                    bg, bu = early[f]
                else:
                    bg = bank()
                    mm_group(ps[bg][:], bg, [(wgu[:, s, 0, kc, :], h2T[:, kc, :]) for kc in range(8)], [("wgu", s)] + H2)
                    bu = bank()
                    mm_group(ps[bu][:], bu, [(wgu[:, s, 1, kc, :], h2T[:, kc, :]) for kc in range(8)], [("wgu", s)] + H2)
                if smp_here:
                    bs = bank()
                    mm_group(ps[bs][:, 0:NS], bs, [(wgu[:, s, 0, kc, :], h2Ts[:, kc, :]) for kc in range(8)], [("wgu", s), "h2Ts"])
                    mm_group(ps[bs][:, NS:2 * NS], bs, [(wgu[:, s, 1, kc, :], h2Ts[:, kc, :]) for kc in range(8)], [("wgu", s), "h2Ts"])
                issue_gu(gi + NWGU + 1)
                sgi = nxt("sg", 2)
                P.op("act", lambda e, bg=bg, sgi=sgi: e.activation(out=sg[:, sgi, :], in_=ps[bg][:], func=AF.Silu),
                     reads=[("ps", bg)], writes=[("sg", sgi)])
                P.op("dve", lambda e, bu=bu, sgi=sgi, f=f: e.tensor_tensor(out=actT[:, f, :], in0=ps[bu][:], in1=sg[:, sgi, :], op=ALU.mult),
                     reads=[("ps", bu), ("sg", sgi)], writes=[("actT", f)])
                if smp_here:
                    P.op("act", lambda e, bs=bs: e.activation(out=sgs[:], in_=ps[bs][:, 0:NS], func=AF.Silu), reads=[("ps", bs)], writes=["sgs"])
                    P.op("dve", lambda e, bs=bs, f=f: e.tensor_tensor(out=actTs[:, f, :], in0=ps[bs][:, NS:2 * NS], in1=sgs[:], op=ALU.mult),
                         reads=[("ps", bs), "sgs"], writes=["actTs"])
                if smp_front:
                    if f == 3:
                        smp_setup_selw()
                    hook()
                    if f == NF - 3:
                        P.flush_all()
                        hooks_on[0] = False
                        P.op("pool", lambda e: e.memset(gate_t[:], 0.0), writes=AL_KEYS + OWN_KEYS)
            for hf in range(2):
                banks = [bank() for _ in range(GT)]
                bsd = bank() if smp_here else None
                held.update(banks)
                if bsd is not None:
                    held.add(bsd)
                hoist = (hf == 0 and g + 1 < NG)
                nxt_tiles = [(g + 1) * GT + t for t in range(GT)]
                hsx = {}
                if hoist:
                    hsx[0] = norm_pre(xs[:, xslot(nxt_tiles[0]), :], ("x", xslot(nxt_tiles[0])))
                for f in range(NF):
                    di = g * 2 * NF + hf * NF + f
                    s = di % NWD
                    for t in range(GT):
                        P.op("pe", lambda e, t=t, f=f, s=s, bb=banks[t]: e.matmul(
                            ps[bb][:], lhsT=actT[:, f, t * 128:(t + 1) * 128], rhs=wd[:, s, :], start=(f == 0), stop=(f == NF - 1)),
                            reads=[("actT", f), ("wd", s)], writes=[("ps", banks[t])])
                    if smp_here:
                        P.op("pe", lambda e, f=f, s=s, bsd=bsd: e.matmul(ps[bsd][0:NS, :], lhsT=actTs[:, f, :], rhs=wd[:, s, :],
                                                                       start=(f == 0), stop=(f == NF - 1)),
                             reads=["actTs", ("wd", s)], writes=[("ps", bsd)])
                    issue_wd(di + NWD + 1)
                    if hoist and f in (4, 9, 14, 19):
                        tt = (f - 4) // 5
                        norm_pe(hsx[tt], g1, "g1", hT, "hT", tt * 128)
                        if tt + 1 < GT:
                            hsx[tt + 1] = norm_pre(xs[:, xslot(nxt_tiles[tt + 1]), :], ("x", xslot(nxt_tiles[tt + 1])))
                held.difference_update(banks)
                if bsd is not None:
                    held.discard(bsd)
                if smp_here:
                    P.op("dve", lambda e, bsd=bsd, hf=hf: e.tensor_tensor(out=xsm[:, hf * 512:(hf + 1) * 512], in0=ps[bsd][0:NS, :],
                                                                        in1=xsm[:, hf * 512:(hf + 1) * 512], op=ALU.add),
                         reads=[("ps", bsd), "xsm"], writes=["xsm"])
                for t, T in enumerate(tiles):
                    sl = xslot(T)
                    P.op("dve", lambda e, bb=banks[t], sl=sl, hf=hf: e.tensor_tensor(
                        out=xs[:, sl, hf * 512:(hf + 1) * 512], in0=ps[bb][:], in1=xs[:, sl, hf * 512:(hf + 1) * 512], op=ALU.add),
                        reads=[("ps", banks[t]), ("x", sl)], writes=[("x", sl)])
            for t, T in enumerate(tiles):
                sl = xslot(T)
                sslot = nxt("ss", 8)
                P.op("act", lambda e, sl=sl, sslot=sslot: e.activation(out=junk[:], in_=xs[:, sl, :], func=AF.Square,
                                                                       accum_out=ss[:, sslot, 0:1]),
                     reads=[("x", sl)], writes=[("ss", sslot)])
                P.op("pool", lambda e, sslot=sslot: e.tensor_scalar(out=var[:, sslot, 0:1], in0=ss[:, sslot, 0:1], scalar1=1.0 / D, scalar2=EPS,
                                                                    op0=ALU.mult, op1=ALU.add), reads=[("ss", sslot)], writes=[("var", sslot)])
                P.op("pool", lambda e, sslot=sslot: e.tensor_tensor(out=rstd[:, sslot, 0:1], in0=var[:, sslot, 0:1], in1=expm[:, 0:1], op=ALU.pow),
                     reads=[("var", sslot), "expm"], writes=[("rstd", sslot)])
                P.op("dve", lambda e, sl=sl, sslot=sslot: e.scalar_tensor_tensor(
                    out=xs[:, sl, :], in0=xs[:, sl, :], scalar=rstd[:, sslot, 0:1], in1=gft[:], op0=ALU.mult, op1=ALU.mult),
                    reads=[("x", sl), ("rstd", sslot), "gft"], writes=[("x", sl)])
                P.dma("sp", lambda e, sl=sl, T=T: e.dma_start(out=y_d[T * 128:(T + 1) * 128, :], in_=xs[:, sl, :]),
                      reads=[("x", sl)], writes=[("out", "y", T)], semkey=("st", sl))
                out_keys.append(("out", "y", T))
                if T + NX < NT:
                    load_x(T + NX)

            if smp_here:
                smp_final()

        P.op("sp", lambda e: e.nop(), reads=out_keys)
        P.emit(st)
    return nc


_CACHE = {}


def _prep_weights(inp):
    w_in = np.asarray(inp["w_in"][0], np.float32)
    qcols = []
    for j in range(4):
        qcols += list(range(j * 64, (j + 1) * 64)) + list(range((4 + j) * 64, (5 + j) * 64))
    cols = qcols + list(range(512, 1280))
    w_in_p = np.ascontiguousarray(w_in[:, cols])
    w_out = np.asarray(inp["w_out"][0], np.float32)
    rows = qcols + list(range(512, 1024))
    w_out_p = np.ascontiguousarray(w_out[rows, :])
    wg = np.asarray(inp["w_gate"][0], np.float32).reshape(8, 128, NF, 128)
    wu = np.asarray(inp["w_up"][0], np.float32).reshape(8, 128, NF, 128)
    w_gu = np.ascontiguousarray(np.stack([wg, wu], 0).transpose(3, 2, 0, 1, 4)).reshape(NF, 128, 2 * 8 * 128)
    wdn = np.asarray(inp["w_down"][0], np.float32).reshape(NF, 128, 2, 512)
    w_d = np.ascontiguousarray(wdn.transpose(2, 0, 1, 3)).reshape(2 * NF, 128, 512)
    return dict(
        w_in=w_in_p, w_out=w_out_p, w_gu=w_gu, w_d=w_d,
        w_pool=np.ascontiguousarray(np.asarray(inp["w_pool"][0], np.float32)),
        g1=np.ascontiguousarray(np.asarray(inp["norm1"][0], np.float32).reshape(8, 128).T),
        g2=np.ascontiguousarray(np.asarray(inp["norm2"][0], np.float32).reshape(8, 128).T),
        psc=np.ascontiguousarray(np.asarray(inp["pool_scale"][0], np.float32).reshape(4, 128).T),
        gf=np.ascontiguousarray(np.broadcast_to(np.asarray(inp["final_norm"], np.float32)[None, :], (128, D))),
        sinks=np.ascontiguousarray(np.broadcast_to(np.asarray(inp["attn_sinks"][0], np.float32)[None, :], (128, 8))),
    )


def kernel(**inp):
    if "nc" not in _CACHE:
        _CACHE["nc"] = build_program()
    nc = _CACHE["nc"]
    xp = np.asarray(inp["x_prompt"], np.float32)
    xsm = np.asarray(inp["x_sample"], np.float32)[:, 0, :]
    ck = np.asarray(inp["cache_k_window"], np.float32)[0].reshape(128, 128, 128)
    cv = np.asarray(inp["cache_v_window"], np.float32)[0].reshape(128, 128, 128)
    spool = np.asarray(inp["state_pool"], np.float32)[0]
    wts = _prep_weights(inp)
    in_maps = []
    for c in range(NCORES):
        b, h = c // 2, c % 2
        m = dict(wts)
        m["x"] = np.ascontiguousarray(xp[b, h * TPC:(h + 1) * TPC])
        m["xh"] = np.ascontiguousarray(xp[b, TPC - 128:TPC]) if h == 1 else np.zeros((128, D), np.float32)
        m["pos0"] = np.full((128, 1), float(h * TPC), np.float32)
        m["xs"] = np.ascontiguousarray(xsm[c * NS:(c + 1) * NS])
        m["ck"] = np.ascontiguousarray(ck[c * NS:(c + 1) * NS])
        m["cv"] = np.ascontiguousarray(cv[c * NS:(c + 1) * NS])
        m["spool"] = np.ascontiguousarray(spool[c * NS:(c + 1) * NS])
        in_maps.append(m)
    res = run_bass_kernel_spmd(nc, in_maps, core_ids=list(range(NCORES)))
    R = res.results
    y_prompt = np.stack([np.concatenate([R[2 * b]["y"], R[2 * b + 1]["y"]], 0) for b in range(4)], 0)
    kvu = np.stack([R[2 * b + 1]["kvu_last"] for b in range(4)], 0)
    new_k_prompt = np.ascontiguousarray(kvu[:, :, 0:128]).reshape(1, 4, 128, 2, 64)
    new_v_prompt = np.ascontiguousarray(kvu[:, :, 128:256]).reshape(1, 4, 128, 2, 64)
    new_pool_prompt = np.ascontiguousarray(kvu[:, 113:128, 256:768]).reshape(1, 4, 15, 512)
    y_sample = np.concatenate([R[c]["ys"] for c in range(NCORES)], 0).reshape(128, 1, D)
    new_k_sample = np.concatenate([R[c]["nk"] for c in range(NCORES)], 0).reshape(1, 128, 128, 2, 64)
    new_v_sample = np.concatenate([R[c]["nv"] for c in range(NCORES)], 0).reshape(1, 128, 128, 2, 64)
    new_pool_sample = np.concatenate([R[c]["npool"] for c in range(NCORES)], 0).reshape(1, 128, 15, 512)
    return (y_prompt.astype(np.float32), y_sample.astype(np.float32), new_k_prompt, new_v_prompt, new_pool_prompt,
            new_k_sample, new_v_sample, new_pool_sample)
```

```python
import numpy as np
from contextlib import ExitStack
import concourse.bass as bass
import concourse.mybir as mybir
from concourse.bass_utils import run_bass_kernel_spmd

F32 = mybir.dt.float32
BF16 = mybir.dt.bfloat16
I32 = mybir.dt.int32
AF = mybir.ActivationFunctionType
ALU = mybir.AluOpType
AX = mybir.AxisListType

NCORES = 8
D = 1024
TPC = 2048
NT = 16
GT = 4
NG = 4
GN = 512
NF = 22
NS = 16
EPS = 1e-5
NX = 8
NWGU = 3
NWD = 8
MASKV = -30000.0
SMP_LEVEL = 99


class Prog:
    ENG = ("pe", "act", "dve", "pool", "sp")

    def __init__(self, nc):
        self.nc = nc
        self.ops = []
        self.last_writer = {}
        self.readers = {}
        self.capture = None
        self.queue = []

    def flush_chunk(self):
        q = self.queue
        while q and q[0][0] == "pe":
            self._add(*q.pop(0))
        while q and q[0][0] != "pe":
            self._add(*q.pop(0))

    def flush_all(self):
        while self.queue:
            self._add(*self.queue.pop(0))

    def _add(self, eng, fn, reads, writes, dma, semkey=None):
        if self.capture is not None:
            self.capture.append((eng, fn, reads, writes, dma, semkey))
            return None
        o = dict(eng=eng, fn=fn, dma=dma, semkey=semkey, idx=len(self.ops), deps=set(), raw=set(), signal=False)
        for k in reads:
            w = self.last_writer.get(k)
            if w is not None:
                o["deps"].add(w); o["raw"].add(w)
        for k in writes:
            w = self.last_writer.get(k)
            if w is not None:
                o["deps"].add(w)
            for r in self.readers.get(k, {}).values():
                for ri in r:
                    o["deps"].add(ri)
        o["deps"].discard(o["idx"])
        for k in writes:
            self.last_writer[k] = o["idx"]
            self.readers[k] = {}
        for k in reads:
            d = self.readers.setdefault(k, {})
            if dma:
                d.setdefault("dma", []).append(o["idx"])
            else:
                d[eng] = [o["idx"]]
        self.ops.append(o)
        return o

    def op(self, eng, fn, reads=(), writes=()):
        return self._add(eng, fn, list(reads), list(writes), False)

    def dma(self, eng, fn, reads=(), writes=(), semkey=None):
        return self._add(eng, fn, list(reads), list(writes), True, semkey)

    def emit(self, stack):
        nc = self.nc
        ops = self.ops
        for o in ops:
            for d in o["deps"]:
                a = ops[d]
                if a["dma"]:
                    continue
                if a["eng"] == o["eng"] and (a["eng"] == "pe" or d not in o["raw"]):
                    continue
                a["signal"] = True
        cnt = {e: 0 for e in self.ENG}
        dcnt = {}
        for o in ops:
            if o["dma"]:
                dcnt[o["semkey"]] = dcnt.get(o["semkey"], 0) + 16
                o["sigval"] = dcnt[o["semkey"]]
            elif o["signal"]:
                cnt[o["eng"]] += 1
                o["sigval"] = cnt[o["eng"]]
        esem = {e: stack.enter_context(nc.semaphore("s_" + e)) for e in self.ENG}
        dsem = {k: stack.enter_context(nc.semaphore("d_%d" % i)) for i, k in enumerate(dcnt)}
        self.n_sems = len(esem) + len(dsem)
        block = stack.enter_context(nc.Block())

        def stream(eng):
            def body(e):
                waited = {}
                for o in ops:
                    if o["eng"] != eng:
                        continue
                    need = {}
                    for d in o["deps"]:
                        a = ops[d]
                        if a["dma"]:
                            key = ("d", a["semkey"]); val = a["sigval"]
                        else:
                            if a["eng"] == eng and (eng == "pe" or d not in o["raw"]):
                                continue
                            key = ("e", a["eng"]); val = a["sigval"]
                        if need.get(key, 0) < val:
                            need[key] = val
                    for key, val in need.items():
                        if waited.get(key, 0) < val:
                            sem = dsem[key[1]] if key[0] == "d" else esem[key[1]]
                            e.wait_ge(sem, val)
                            waited[key] = val
                    ins = o["fn"](e)
                    if o["dma"]:
                        ins.then_inc(dsem[o["semkey"]], 16)
                    elif o["signal"]:
                        ins.then_inc(esem[eng], 1)
            return body

        block.tensor(stream("pe"))
        block.scalar(stream("act"))
        block.vector(stream("dve"))
        block.gpsimd(stream("pool"))
        block.sync(stream("sp"))


def build_program():
    nc = bass.Bass("TRN2", target_bir_lowering=False)

    def din(name, shape):
        return nc.dram_tensor(name, shape, F32, kind="ExternalInput").ap()

    def dout(name, shape):
        return nc.dram_tensor(name, shape, F32, kind="ExternalOutput").ap()

    x_d = din("x", [TPC, D]); xh_d = din("xh", [128, D]); pos_d = din("pos0", [128, 1])
    w_in_d = din("w_in", [D, 1280]); w_out_d = din("w_out", [D, D])
    w_gu_d = din("w_gu", [NF, 128, 2 * 8 * 128]); w_d_d = din("w_d", [2 * NF, 128, 512])
    w_pool_d = din("w_pool", [4, 128, 128])
    g1_d = din("g1", [128, 8]); g2_d = din("g2", [128, 8]); psc_d = din("psc", [128, 4])
    gf_d = din("gf", [128, D]); sink_d = din("sinks", [128, 8])
    xs_d = din("xs", [NS, D]); ck_d = din("ck", [NS, 128, 128]); cv_d = din("cv", [NS, 128, 128])
    sp_d = din("spool", [NS, 15, 512])
    y_d = dout("y", [TPC, D]); kvu_d = dout("kvu_last", [128, 768])
    ys_d = dout("ys", [NS, D]); nk_d = dout("nk", [NS, 128, 128]); nv_d = dout("nv", [NS, 128, 128])
    np_d = dout("npool", [NS, 15, 512])

    st = ExitStack()
    with st:
        def sb(name, shape, dt=F32):
            return st.enter_context(nc.sbuf_tensor("sb_" + name, shape, dt))

        w_in_sb = sb("w_in_sb", [128, 8, 1280], BF16)
        w_out_sb = sb("w_out_sb", [128, 8, D], BF16)
        w_pool_sb = sb("w_pool_sb", [128, 4, 128], BF16)
        wgu = sb("wgu", [128, NWGU, 2, 8, 128], BF16)
        wd = sb("wd", [128, NWD, 512], BF16)
        xs = sb("xs", [128, NX, D], F32)
        hT = sb("hT", [128, 8, GN], BF16)
        h2T = sb("h2T", [128, 8, GN], BF16)
        xn = sb("xn", [128, 2, D], BF16)
        qT = sb("qT", [128, 4, GN], BF16)
        NKR = 8
        kTp = sb("kTp", [128, 2, NKR, 128], BF16)
        NV = 8
        Vaug = sb("Vaug", [128, NV, 256], BF16)
        uT = sb("uT", [128, 4, 16 + GN], F32)
        ptmp = sb("ptmp", [128, 2, 16 + GN], F32)
        mT = sb("mT", [128, 4, GN], BF16)
        PT = sb("PT", [128, 8, GN], BF16)
        rec = sb("rec", [128, 2, GN], F32)
        mixT = sb("mixT", [128, 8, GN], BF16)
        junk = mixT[:].rearrange("p c n -> p (c n)")[:, 0:D]
        actT = sb("actT", [128, NF, GN], BF16)
        sg = sb("sg", [128, 2, GN], F32)
        biasT = sb("biasT", [128, 2, 2, GN], BF16)
        gft = sb("gft", [128, D], F32)
        g1 = sb("g1", [128, 8]); g2 = sb("g2", [128, 8]); psc = sb("psc", [128, 4])
        sinks = sb("sinks", [128, 8]); es = sb("es", [128, 8]); es_hi = sb("es_hi", [128, 8], BF16)
        es_hif = sb("es_hif", [128, 8]); es_lo = sb("es_lo", [128, 8], BF16)
        pos0 = sb("pos0", [128, 1]); flag = sb("flag", [128, 1])
        ss = sb("ss", [128, 8, 4]); var = sb("var", [128, 8, 4]); rstd = sb("rstd", [128, 8, 4])
        expm = sb("expm", [128, 4])
        ident = sb("ident", [128, 128], BF16)
        mix_tm = sb("mix_tm", [128, 2, 512], BF16)
        den = sb("den", [128, 2, 8, 1], F32)
        icnt = sb("icnt", [128, 4, 16], F32)
        io16 = sb("io16", [128, 16], I32)
        io16f = sb("io16f", [128, 16], F32)
        fix16 = sb("fix16", [128, 16], F32)
        identf = PT[:, 0, 0:256].bitcast(F32)
        iot = PT[:, 1, 0:256].bitcast(I32)
        Rf = PT[:, 2, 0:256].bitcast(F32)
        tmpb_v = [PT[:, 3, 0:256].bitcast(F32), PT[:, 4, 0:256].bitcast(F32)]
        klast = rec[:].rearrange("p a n -> p (a n)")[:, 0:768]
        KL = [("rec", 0), ("rec", 1)]
        xsm = sb("xsm", [NS, D], F32)
        hTs = sb("hTs", [128, 8, NS], BF16)
        h2Ts = sb("h2Ts", [128, 8, NS], BF16)
        qTs = sb("qTs", [128, 4, NS], BF16)
        uTs = sb("uTs", [128, 4, NS], F32)
        tmps = sb("tmps", [128, 4, NS], F32)
        mTs = sb("mTs", [128, 4, NS], BF16)
        mixTs = sb("mixTs", [128, 8, NS], BF16)
        actTs = sb("actTs", [128, NF, NS], BF16)
        sgs = sb("sgs", [128, NS], F32)
        s_sb = sb("s_sb", [128, NS, 8], F32)
        PTs = sb("PTs", [128, NS, 8], BF16)
        iop = sb("iop", [128, 1], I32)
        relc = sb("relc", [128, 1], F32)
        sbias = sb("sbias", [128, 8], F32)
        snew = sb("snew", [NS, 8], F32)
        pnew = sb("pnew", [NS, 8], F32)
        bdmask = sb("bdmask", [NS, 2, NS, 4], F32)
        pbd = sb("pbd", [128, 2, NS, 4], BF16)
        vnew = sb("vnew", [128, 256], BF16)
        recs = sb("recs", [128, 2, 64], F32)
        essm = sb("essm", [128, 2, NS, 4], F32)
        PTf = PT[:].rearrange("p s n -> p (s n)")
        ckb = PTf[:, 0:2048].rearrange("p (b f) -> p b f", f=128)
        ckT = PTf[:, 2048:4096].rearrange("p (b f) -> p b f", f=128)
        cva = mixT[:].rearrange("p c n -> p (c n)").rearrange("p (b f) -> p b f", f=256)
        qTf = qT[:].rearrange("p c n -> p (c n)").bitcast(F32)
        mTb = mT[:].rearrange("p c n -> p (c n)")
        mTf = mTb.bitcast(F32)
        ptf = ptmp[:].rearrange("p a n -> p (a n)")
        tok_q = qTf[0:NS, 0:512]
        tok_kvu = ptf[0:NS, 0:768]
        hist_v = [qTf[:, 512:1024], mTf[:, 512:1024]]
        xns = mTb[0:NS, 0:1024]
        selw = ptf[:, 768:896].rearrange("p (k c n) -> p k c n", k=2, c=4)
        gate_t = sb("gate_t", [128, 1], F32)
        zb = sb("zb", [128, 128], BF16)
        identF = sb("identF", [128, 128], F32)
        ysT = sb("ysT", [128, 8 * NS], F32)

        class _Tok:
            def __getitem__(self, idx):
                p, cs = idx
                c0, c1 = cs.start, cs.stop
                if c1 <= 512:
                    return tok_q[:, c0:c1]
                assert c0 >= 512
                return tok_kvu[:, c0 - 512:c1 - 512]
        tok_s = _Tok()
        AL_KEYS = ["al_ckb", "al_ckT", "al_cvaO", "al_cva0", "al_cva1", "al_tok", "selw", "hist0", "hist1"]
        OWN_KEYS = ([("PT", i) for i in range(8)] + [("mixA", t) for t in range(4)] + [("mixP", c) for c in range(4)]
                    + ["qT"] + [("mT", c) for c in range(4)] + [("ptmp", 0), ("ptmp", 1)])
        ps = [st.enter_context(nc.psum_tensor("ps%d" % i, [128, 512], F32)) for i in range(8)]

        P = Prog(nc)
        rr = {"ps": 0, "xn": 0, "ss": 0, "PT": 0, "rec": 0, "sg": 0, "tmpb": 0, "den": 0, "mtm": 0}

        def nxt(name, n):
            v = rr[name]; rr[name] = (v + 1) % n
            return v

        held = set()

        def bank():
            while True:
                v = nxt("ps", 8)
                if v not in held:
                    return v

        def load_tab(nm, t, dsrc):
            P.dma("sp", (lambda e: e.dma_start(out=t[:], in_=dsrc)), writes=[nm], semkey=("ld", nm))

        def early_loads():
            load_x(-1)
            load_x(0)
            load_tab("g1", g1, g1_d); load_tab("pos0", pos0, pos_d)
            for T in range(1, GT):
                load_x(T)
            load_tab("psc", psc, psc_d); load_tab("sinks", sinks, sink_d); load_tab("g2", g2, g2_d)
            P.dma("sp", lambda e: e.dma_start(out=xsm[:], in_=xs_d), writes=["xsm"], semkey=("ld", "xsm"))
            for T in range(GT, NX - 1):
                load_x(T)
            load_tab("gft", gft, gf_d)
            P.dma("sp", lambda e: e.dma_start(out=nk_d[:, 0:127, :], in_=ck_d[:, 1:128, :]), writes=[("out", "nk0")], semkey=("st", "nk0"))
            P.dma("sp", lambda e: e.dma_start(out=nv_d[:, 0:127, :], in_=cv_d[:, 1:128, :]), writes=[("out", "nv0")], semkey=("st", "nv0"))
            P.dma("sp", lambda e: e.dma_start(out=np_d[:, 0:14, :], in_=sp_d[:, 1:15, :]), writes=[("out", "np0")], semkey=("st", "np0"))
            out_keys.extend([("out", "nk0"), ("out", "nv0"), ("out", "np0")])
        P.op("pool", lambda e: e.memset(expm[:], -0.5), writes=["expm"])
        P.op("pool", lambda e: e.memset(identf[:], 1.0), writes=[("PT", 0)])
        P.op("pool", lambda e: e.affine_select(out=identf[:], in_=identf[:], pattern=[[-1, 128]], compare_op=ALU.is_equal,
                                               fill=0.0, base=0, channel_multiplier=1), reads=[("PT", 0)], writes=[("PT", 0)])
        P.op("pool", lambda e: e.tensor_copy(out=ident[:], in_=identf[:]), reads=[("PT", 0)], writes=["ident"])
        P.op("pool", lambda e: e.tensor_copy(out=identF[:], in_=identf[:]), reads=[("PT", 0)], writes=["identF"])
        P.op("pool", lambda e: e.memset(zb[:], 0.0), writes=["zb"])
        w_in_v = w_in_d.rearrange("(kc p) n -> p kc n", p=128)
        P.dma("pool", lambda e: e.dma_start(out=w_in_sb[:, :, 512:1280], in_=w_in_v[:, :, 512:1280]), writes=["w_in"], semkey=("ld", "w_inA"))
        P.dma("pool", lambda e: e.dma_start(out=w_in_sb[:, :, 0:512], in_=w_in_v[:, :, 0:512]), writes=["w_inq"], semkey=("ld", "w_inB"))
        P.op("pool", lambda e: e.memset(Vaug[:, :, 64:192], 1.0), writes=["Vaug_ones"])
        P.op("pool", lambda e: e.memset(kTp[:], 0.0), writes=["kT_zero"])
        P.op("pool", lambda e: e.memset(uT[:, :, 0:1], 0.0), writes=["uT_halo"])

        def late_setup():
            P.op("pool", lambda e: e.iota(iot[:], pattern=[[1, 128]], base=0, channel_multiplier=-1), writes=[("PT", 1)])
            P.op("dve", lambda e: e.tensor_copy(out=Rf[:], in_=iot[:]), reads=[("PT", 1)], writes=[("PT", 2)])
            for kvh in range(2):
                for g in range(4):
                    slope = 2.0 ** (-(kvh * 4 + g + 1))
                    for kb in range(2):
                        tb = nxt("tmpb", 2)
                        if kb == 1:
                            P.op("dve", lambda e, tb=tb, slope=slope: e.tensor_scalar(
                                out=tmpb_v[tb], in0=Rf[:], scalar1=-8.0 * slope, scalar2=None, op0=ALU.mult),
                                reads=[("PT", 2)], writes=[("PT", 3 + tb)])
                            P.op("pool", lambda e, tb=tb, kvh=kvh, g=g: e.affine_select(
                                out=biasT[:, 1, kvh, g * 128:(g + 1) * 128], in_=tmpb_v[tb], pattern=[[1, 128]],
                                compare_op=ALU.is_ge, fill=MASKV, base=0, channel_multiplier=-1),
                                reads=[("PT", 3 + tb)], writes=["biasT"])
                        else:
                            P.op("dve", lambda e, tb=tb, slope=slope: e.tensor_scalar(
                                out=tmpb_v[tb], in0=Rf[:], scalar1=128.0, scalar2=-8.0 * slope, op0=ALU.add, op1=ALU.mult),
                                reads=[("PT", 2)], writes=[("PT", 3 + tb)])
                            P.op("pool", lambda e, tb=tb, kvh=kvh, g=g: e.affine_select(
                                out=biasT[:, 0, kvh, g * 128:(g + 1) * 128], in_=tmpb_v[tb], pattern=[[-1, 128]],
                                compare_op=ALU.is_ge, fill=MASKV, base=0, channel_multiplier=1),
                                reads=[("PT", 3 + tb)], writes=["biasT"])
            P.op("act", lambda e: e.activation(out=es[:], in_=sinks[:], func=AF.Exp), reads=["sinks"], writes=["es"])
            P.op("dve", lambda e: e.tensor_copy(out=es_hi[:], in_=es[:]), reads=["es"], writes=["es_hi"])
            P.op("dve", lambda e: e.tensor_copy(out=es_hif[:], in_=es_hi[:]), reads=["es_hi"], writes=["es_hif"])
            P.op("dve", lambda e: e.tensor_tensor(out=es_lo[:], in0=es[:], in1=es_hif[:], op=ALU.subtract),
                 reads=["es", "es_hif"], writes=["es_lo"])
            P.op("pool", lambda e: e.iota(io16[:], pattern=[[1, 16]], base=1, channel_multiplier=0), writes=["io16"])
            P.op("dve", lambda e: e.tensor_copy(out=io16f[:], in_=io16[:]), reads=["io16"], writes=["io16f"])
            for c in range(4):
                P.op("dve", lambda e, c=c: e.tensor_scalar(out=icnt[:, c, :], in0=io16f[:], scalar1=pos0[:, 0:1],
                                                           scalar2=float(2 ** (c + 1)), op0=ALU.add, op1=ALU.min),
                     reads=["io16f", "pos0"], writes=["icnt"])
            P.op("dve", lambda e: e.reciprocal(out=icnt[:], in_=icnt[:]), reads=["icnt"], writes=["icnt"])

            P.dma("pool", lambda e: e.dma_start(out=w_pool_sb[:], in_=w_pool_d.rearrange("g c d -> c g d")),
                  writes=["w_pool"], semkey=("ld", "w_pool"))
            P.dma("pool", lambda e: e.dma_start(out=w_out_sb[:], in_=w_out_d.rearrange("(kc p) n -> p kc n", p=128)),
                  writes=["w_out"], semkey=("ld", "w_out"))


        def load_x(T):
            slot = (T % NX) if T >= 0 else NX - 1
            src = x_d[T * 128:(T + 1) * 128, :] if T >= 0 else xh_d
            P.dma("sp", lambda e: e.dma_start(out=xs[:, slot, :], in_=src), writes=[("x", slot)], semkey=("x", slot))

        def xslot(T):
            return (T % NX) if T >= 0 else NX - 1

        def norm_stage(tiles, gam, gam_key, dst, dst_key, rows=128, xn_priv=None):
            n = len(tiles)
            sslot = nxt("ss", 8)
            for i, (xap, xkey, off) in enumerate(tiles):
                jout = junk[0:rows, :] if xn_priv is None else xn_priv[0]
                jw = [] if xn_priv is None else [xn_priv[1]]
                P.op("act", lambda e, xap=xap, i=i, jout=jout: e.activation(out=jout, in_=xap, func=AF.Square,
                                                                            accum_out=ss[0:rows, sslot, i:i + 1]),
                     reads=[xkey] + jw, writes=[("ss", sslot)] + jw)
            P.op("pool", lambda e: e.tensor_scalar(out=var[0:rows, sslot, 0:n], in0=ss[0:rows, sslot, 0:n],
                                                   scalar1=1.0 / D, scalar2=EPS, op0=ALU.mult, op1=ALU.add),
                 reads=[("ss", sslot)], writes=[("var", sslot)])
            P.op("pool", lambda e: e.tensor_tensor(out=rstd[0:rows, sslot, 0:n], in0=var[0:rows, sslot, 0:n],
                                                   in1=expm[0:rows, 0:n], op=ALU.pow),
                 reads=[("var", sslot), "expm"], writes=[("rstd", sslot)])
            for i, (xap, xkey, off) in enumerate(tiles):
                if xn_priv is None:
                    s = nxt("xn", 2)
                    xn_ap, xn_key = xn[0:rows, s, :], ("xn", s)
                else:
                    xn_ap, xn_key = xn_priv
                P.op("act", lambda e, xap=xap, i=i, xn_ap=xn_ap: e.activation(out=xn_ap, in_=xap, func=AF.Copy,
                                                                              scale=rstd[0:rows, sslot, i:i + 1]),
                     reads=[xkey, ("rstd", sslot), xn_key], writes=[xn_key])
                b = bank()
                psb = ps[b][:].bitcast(BF16).rearrange("p (k t) -> p k t", t=128)
                for kc in range(8):
                    P.op("pe", lambda e, kc=kc, xn_ap=xn_ap, psb=psb: e.transpose(psb[:, kc, 0:rows], xn_ap[:, kc * 128:(kc + 1) * 128],
                                                                                  ident[0:rows, 0:rows]),
                         reads=[xn_key, "ident"], writes=[("ps", b)])
                P.op("dve", lambda e, psb=psb, off=off: e.tensor_tensor(
                    out=dst[:, :, off:off + rows], in0=psb[:, :, 0:rows], in1=gam[:, :, None].broadcast_to([128, 8, rows]),
                    op=ALU.mult), reads=[("ps", b), gam_key], writes=[dst_key])
            return sslot

        def mm_group(out_ap, b, pairs, reads):
            n = len(pairs)
            for i, (l, r) in enumerate(pairs):
                P.op("pe", lambda e, l=l, r=r, i=i: e.matmul(out_ap, lhsT=l, rhs=r, start=(i == 0), stop=(i == n - 1)),
                     reads=reads, writes=[("ps", b)])

        gu_issued = [0]
        wd_issued = [0]

        def issue_gu(upto):
            while gu_issued[0] < upto and gu_issued[0] < NG * NF:
                i = gu_issued[0]; f = i % NF; s = i % NWGU
                P.dma("pool", lambda e, f=f, s=s: e.dma_start(out=wgu[:, s].rearrange("p a k n -> p (a k n)"), in_=w_gu_d[f],
                                                              max_dma_last_dim=4096),
                      writes=[("wgu", s)], semkey=("wgu", s))
                gu_issued[0] += 1

        def issue_wd(upto):
            while wd_issued[0] < upto and wd_issued[0] < NG * 2 * NF:
                i = wd_issued[0]; j = i % (2 * NF); s = i % NWD
                P.dma("pool", lambda e, j=j, s=s: e.dma_start(out=wd[:, s, :], in_=w_d_d[j]),
                      writes=[("wd", s)], semkey=("wd", s))
                wd_issued[0] += 1

        out_keys = []
        SLOPES = [2.0 ** (-(h + 1)) for h in range(8)]

        CVA = ["al_cvaO", "al_cva0", "al_cva1"]

        def smp_dma():
            P.dma("pool", lambda e: e.dma_start(out=ckb, in_=ck_d.rearrange("b k f -> k b f")), reads=["al_ckb"], writes=["al_ckb"], semkey=("ld", "ckb"))
            P.op("pool", lambda e: e.memset(cva[:, :, 64:192], 1.0), reads=["al_cvaO"], writes=["al_cvaO"])
            cvsrc = cv_d.rearrange("b k f -> k b f")
            P.dma("pool", lambda e: e.dma_start(out=cva[:, :, 0:64], in_=cvsrc[:, :, 0:64]), reads=["al_cva0"], writes=["al_cva0"], semkey=("ld", "cva0"))
            P.dma("pool", lambda e: e.dma_start(out=cva[:, :, 192:256], in_=cvsrc[:, :, 64:128]), reads=["al_cva1"], writes=["al_cva1"], semkey=("ld", "cva1"))
            sp2 = sp_d.rearrange("b j c -> (b j) c")
            P.dma("sp", lambda e: e.dma_start(out=hist_v[0], in_=sp2[0:128, :]), reads=["hist0"], writes=["hist0"], semkey=("ld", "h0"))
            P.op("pool", lambda e: e.memset(hist_v[1], 0.0), reads=["hist1"], writes=["hist1"])
            P.dma("sp", lambda e: e.dma_start(out=hist_v[1][0:112, :], in_=sp2[128:240, :]), reads=["hist1"], writes=["hist1"], semkey=("ld", "h1"))

        def smp_setup():
            P.op("pool", lambda e: e.memset(vnew[:], 0.0), writes=["vnew"])
            P.op("pool", lambda e: e.memset(vnew[0:NS, 64:192], 1.0), reads=["vnew"], writes=["vnew"])
            P.op("pool", lambda e: e.memset(pbd[:], 0.0), writes=["pbd"])
            P.op("dve", lambda e: e.tensor_copy(out=relc[:], in_=iop[:]), reads=["iop"], writes=["relc"])
            for h in range(8):
                P.op("dve", lambda e, h=h: e.tensor_scalar(out=sbias[:, h:h + 1], in0=relc[:], scalar1=-8.0 * SLOPES[h], scalar2=None,
                                                           op0=ALU.mult), reads=["relc", "sbias"], writes=["sbias"])
            P.op("pool", lambda e: e.memset(bdmask[:], 1.0), writes=["bdmask"])
            P.op("pool", lambda e: e.affine_select(out=bdmask[:], in_=bdmask[:], pattern=[[0, 2], [1, NS], [0, 4]],
                                                   compare_op=ALU.is_equal, fill=0.0, base=0, channel_multiplier=-1),
                 reads=["bdmask"], writes=["bdmask"])
            P.op("dve", lambda e: e.tensor_copy(out=essm[:], in_=es[:].rearrange("p (k g) -> p k g", g=4)[:, :, None, :].broadcast_to([128, 2, NS, 4])),
                 reads=["es"], writes=["essm"])

        def smp_setup_selw():
            P.op("pool", lambda e: e.memset(selw, 1.0), reads=["selw"], writes=["selw"])
            for kt in range(2):
                for c in range(4):
                    w = 2 ** (c + 1)
                    P.op("pool", lambda e, kt=kt, c=c, w=w: e.affine_select(
                        out=selw[:, kt, c, :], in_=selw[:, kt, c, :], pattern=[[-15, NS]], compare_op=ALU.is_ge, fill=0.0,
                        base=kt * 128 - (16 - w), channel_multiplier=1), reads=["selw"], writes=["selw"])
                    P.op("pool", lambda e, kt=kt, c=c: e.affine_select(
                        out=selw[:, kt, c, :], in_=selw[:, kt, c, :], pattern=[[15, NS]], compare_op=ALU.is_ge, fill=0.0,
                        base=14 - kt * 128, channel_multiplier=-1), reads=["selw"], writes=["selw"])

        def smp_norm1():
            norm_stage([(xsm[:, :], "xsm", 0)], g1, "g1", hTs, "hTs", rows=NS, xn_priv=(xns, "al_tok"))

        def smp_inproj():
            b = bank()
            for c in range(4):
                mm_group(ps[b][:, c * NS:(c + 1) * NS], b, [(w_in_sb[:, kc, c * 128:(c + 1) * 128], hTs[:, kc, :]) for kc in range(8)],
                         ["w_inq", "hTs"])
            for c in range(4):
                mm_group(ps[b][:, (4 + c) * NS:(5 + c) * NS], b,
                         [(w_in_sb[:, kc, 768 + c * 128:768 + (c + 1) * 128], hTs[:, kc, :]) for kc in range(8)], ["w_in", "hTs"])
            P.op("dve", lambda e, b=b: e.tensor_copy(out=qTs[:], in_=ps[b][:, 0:4 * NS].rearrange("p (c n) -> p c n", n=NS)),
                 reads=[("ps", b)], writes=["qTs"])
            P.op("dve", lambda e, b=b: e.tensor_copy(out=uTs[:], in_=ps[b][:, 4 * NS:8 * NS].rearrange("p (c n) -> p c n", n=NS)),
                 reads=[("ps", b)], writes=["uTs"])
            for (c0, c1) in ((0, 512), (512, 1024), (1024, 1280)):
                b = bank()
                mm_group(ps[b][0:NS, 0:c1 - c0], b, [(hTs[:, kc, :], w_in_sb[:, kc, c0:c1]) for kc in range(8)],
                         ["w_inq" if c0 == 0 else "w_in", "hTs"])
                P.op("dve", lambda e, b=b, c0=c0, c1=c1: e.tensor_copy(out=tok_s[:, c0:c1], in_=ps[b][0:NS, 0:c1 - c0]),
                     reads=[("ps", b), "al_tok"], writes=["al_tok"])
            P.dma("sp", lambda e: e.dma_start(out=nk_d[:, 127, :], in_=tok_s[:, 512:640]), reads=["al_tok"], writes=[("out", "nk1")], semkey=("st", "nk1"))
            P.dma("sp", lambda e: e.dma_start(out=nv_d[:, 127, :], in_=tok_s[:, 640:768]), reads=["al_tok"], writes=[("out", "nv1")], semkey=("st", "nv1"))
            P.dma("sp", lambda e: e.dma_start(out=np_d[:, 14, :], in_=tok_s[:, 768:1280]), reads=["al_tok"], writes=[("out", "np1")], semkey=("st", "np1"))
            out_keys.extend([("out", "nk1"), ("out", "nv1"), ("out", "np1")])
            prodv = rec[0:NS, 0, :].rearrange("p (j h d) -> p j h d", h=2, d=64)
            P.op("dve", lambda e: e.tensor_tensor(
                out=prodv, in0=tok_s[:, 0:512].rearrange("p (j h d) -> p j h d", h=2, d=64),
                in1=tok_s[:, 512:640].rearrange("p (h d) -> p h d", d=64)[:, None, :, :].broadcast_to([NS, 4, 2, 64]), op=ALU.mult),
                reads=["al_tok"], writes=[("rec", 0)])
            P.op("dve", lambda e: e.tensor_reduce(out=snew[:], in_=rec[0:NS, 0, :].rearrange("p (a d) -> p a d", d=64), axis=AX.X, op=ALU.add),
                 reads=[("rec", 0)], writes=["snew"])
            P.op("act", lambda e: e.activation(out=pnew[:], in_=snew[:], func=AF.Exp, scale=0.125), reads=["snew"], writes=["pnew"])
            P.op("dve", lambda e: e.tensor_tensor(
                out=pbd[0:NS], in0=bdmask[:], in1=pnew[:].rearrange("p (j h) -> p h j", h=2)[:, :, None, :].broadcast_to([NS, 2, NS, 4]),
                op=ALU.mult), reads=["bdmask", "pnew", "pbd"], writes=["pbd"])
            P.op("dve", lambda e: e.tensor_copy(out=vnew[0:NS, :].rearrange("p (q d) -> p q d", d=64)[:, 0:4:3, :],
                                                in_=tok_s[:, 640:768].rearrange("p (h d) -> p h d", d=64)),
                 reads=["al_tok", "vnew"], writes=["vnew"])

        def smp_ktrans():
            for half in range(2):
                b = bank()
                psb = ps[b][:].bitcast(BF16).rearrange("p (k t) -> p k t", t=128)
                for i in range(8):
                    bb = half * 8 + i
                    P.op("pe", lambda e, psb=psb, i=i, bb=bb: e.transpose(psb[:, i, :], ckb[:, bb, :], ident[:]),
                         reads=["al_ckb", "ident"], writes=[("ps", b)])
                P.op("dve", lambda e, psb=psb, half=half: e.tensor_copy(out=ckT[:, half * 8:(half + 1) * 8, :], in_=psb),
                     reads=[("ps", b), "al_ckT"], writes=["al_ckT"])

        def smp_scores():
            for kvh in range(2):
                b = bank()
                r0 = kvh * 64
                for bb in range(NS):
                    P.op("pe", lambda e, b=b, bb=bb, r0=r0: e.matmul(
                        ps[b][:, bb * 4:bb * 4 + 4], lhsT=ckT[r0:r0 + 64, bb, :], rhs=qTs[r0:r0 + 64, :, bb],
                        start=True, stop=True), reads=["al_ckT", "qTs"], writes=[("ps", b)])
                P.op("dve", lambda e, b=b, kvh=kvh: e.tensor_tensor(
                    out=s_sb[:, :, kvh * 4:(kvh + 1) * 4], in0=ps[b][:, 0:NS * 4].rearrange("p (b g) -> p b g", g=4),
                    in1=sbias[:, None, kvh * 4:(kvh + 1) * 4].broadcast_to([128, NS, 4]), op=ALU.add),
                    reads=[("ps", b), "sbias", "s_sb"], writes=["s_sb"])
            P.op("act", lambda e: e.activation(out=PTs[:], in_=s_sb[:], func=AF.Exp, scale=0.125), reads=["s_sb"], writes=["PTs"])

        def smp_pv():
            for kvh in range(2):
                b = bank()
                a0, s0 = (0, 64) if kvh == 0 else (64, 0)
                for bb in range(NS):
                    P.op("pe", lambda e, b=b, kvh=kvh, bb=bb: e.matmul(
                        ps[b][:, bb * 4:(bb + 1) * 4], lhsT=vnew[:, kvh * 128:(kvh + 1) * 128], rhs=pbd[:, kvh, bb, :],
                        start=True, stop=False), reads=["vnew", "pbd"], writes=[("ps", b)])
                    P.op("pe", lambda e, b=b, kvh=kvh, bb=bb: e.matmul(
                        ps[b][:, bb * 4:(bb + 1) * 4], lhsT=cva[:, bb, kvh * 128:(kvh + 1) * 128], rhs=PTs[:, bb, kvh * 4:(kvh + 1) * 4],
                        start=False, stop=True), reads=CVA + ["PTs"], writes=[("ps", b)])
                P.op("dve", lambda e, b=b, kvh=kvh, s0=s0: e.tensor_tensor(
                    out=recs[s0:s0 + 64, kvh, :], in0=ps[b][s0:s0 + 64, 0:NS * 4],
                    in1=essm[s0:s0 + 64, kvh, :, :].rearrange("p b g -> p (b g)"), op=ALU.add),
                    reads=[("ps", b), "essm"], writes=[("recs", kvh)])
                P.op("dve", lambda e, kvh=kvh, s0=s0: e.reciprocal(out=recs[s0:s0 + 64, kvh, :], in_=recs[s0:s0 + 64, kvh, :]),
                     reads=[("recs", kvh)], writes=[("recs", kvh)])
                P.op("dve", lambda e, b=b, kvh=kvh, s0=s0, a0=a0: e.tensor_tensor(
                    out=mixTs[a0:a0 + 64, 0:4, :].rearrange("p g b -> p b g"),
                    in0=ps[b][a0:a0 + 64, 0:NS * 4].rearrange("p (b g) -> p b g", g=4),
                    in1=recs[s0:s0 + 64, kvh, :].rearrange("p (b g) -> p b g", g=4), op=ALU.mult),
                    reads=[("ps", b), ("recs", kvh)], writes=["mixTs"])

        def smp_pool():
            b = bank()
            for c in range(4):
                for kt, rows in ((0, 128), (1, 128)):
                    P.op("pe", lambda e, b=b, c=c, kt=kt, rows=rows: e.matmul(
                        ps[b][:, c * NS:(c + 1) * NS], lhsT=hist_v[kt][0:rows, c * 128:(c + 1) * 128], rhs=selw[0:rows, kt, c, :],
                        start=(kt == 0), stop=(kt == 1)), reads=["hist%d" % kt, "selw"], writes=[("ps", b)])
            P.op("dve", lambda e, b=b: e.tensor_tensor(out=tmps[:], in0=ps[b][:, 0:4 * NS].rearrange("p (c n) -> p c n", n=NS),
                                                       in1=uTs[:], op=ALU.add), reads=[("ps", b), "uTs"], writes=["tmps"])
            for c in range(4):
                P.op("dve", lambda e, c=c: e.scalar_tensor_tensor(out=mTs[:, c, :], in0=tmps[:, c, :], scalar=1.0 / (2 ** (c + 1)),
                                                                  in1=uTs[:, c, :], op0=ALU.mult, op1=ALU.subtract),
                     reads=["tmps", "uTs", "mTs"], writes=["mTs"])
            b2 = bank()
            for c in range(4):
                P.op("pe", lambda e, b2=b2, c=c: e.matmul(ps[b2][:, c * NS:(c + 1) * NS], lhsT=w_pool_sb[:, c, :], rhs=mTs[:, c, :],
                                                          start=True, stop=True), reads=["w_pool", "mTs"], writes=[("ps", b2)])
            for c in range(4):
                P.op("act", lambda e, b2=b2, c=c: e.activation(out=mixTs[:, 4 + c, :], in_=ps[b2][:, c * NS:(c + 1) * NS], func=AF.Copy,
                                                               scale=psc[:, c:c + 1]), reads=[("ps", b2), "psc", "mixTs"], writes=["mixTs"])

        def smp_wout():
            for hf in range(2):
                b = bank()
                mm_group(ps[b][0:NS, :], b, [(mixTs[:, ch, :], w_out_sb[:, ch, hf * 512:(hf + 1) * 512]) for ch in range(8)],
                         ["mixTs", "w_out"])
                P.op("dve", lambda e, b=b, hf=hf: e.tensor_tensor(out=xsm[:, hf * 512:(hf + 1) * 512], in0=ps[b][0:NS, :],
                                                                  in1=xsm[:, hf * 512:(hf + 1) * 512], op=ALU.add),
                     reads=[("ps", b), "xsm"], writes=["xsm"])
            norm_stage([(xsm[:, :], "xsm", 0)], g2, "g2", h2Ts, "h2Ts", rows=NS, xn_priv=(xns, "al_tok"))

        def smp_final():
            sslot = nxt("ss", 8)
            P.op("act", lambda e: e.activation(out=junk[0:NS, :], in_=xsm[:], func=AF.Square, accum_out=ss[0:NS, sslot, 0:1]),
                 reads=["xsm"], writes=[("ss", sslot)])
            P.op("pool", lambda e: e.tensor_scalar(out=var[0:NS, sslot, 0:1], in0=ss[0:NS, sslot, 0:1], scalar1=1.0 / D, scalar2=EPS,
                                                   op0=ALU.mult, op1=ALU.add), reads=[("ss", sslot)], writes=[("var", sslot)])
            P.op("pool", lambda e: e.tensor_tensor(out=rstd[0:NS, sslot, 0:1], in0=var[0:NS, sslot, 0:1], in1=expm[0:NS, 0:1], op=ALU.pow),
                 reads=[("var", sslot), "expm"], writes=[("rstd", sslot)])
            P.op("dve", lambda e: e.scalar_tensor_tensor(out=xsm[:], in0=xsm[:], scalar=rstd[0:NS, sslot, 0:1], in1=gft[0:NS, :],
                                                         op0=ALU.mult, op1=ALU.mult), reads=["xsm", ("rstd", sslot), "gft"], writes=["xsm"])
            P.dma("sp", lambda e: e.dma_start(out=ys_d, in_=xsm[:]), reads=["xsm"], writes=[("out", "ys")], semkey=("st", "ys"))
            out_keys.append(("out", "ys"))

        early_loads()
        P.op("dve", lambda e: e.tensor_scalar(out=flag[:], in0=pos0[:], scalar1=1.0, scalar2=None, op0=ALU.min),
             reads=["pos0"], writes=["flag"])
        P.op("pool", lambda e: e.iota(iop[:], pattern=[[0, 1]], base=128, channel_multiplier=-1), writes=["iop"])

        hs = xslot(-1)
        norm_stage([(xs[:, hs, :], ("x", hs), 0)], g1, "g1", h2T, ("h2T", 0))
        norm_stage([(xs[:, xslot(T), :], ("x", xslot(T)), T * 128) for T in range(GT)], g1, "g1", hT, "hT")
        b = bank()
        mm_group(ps[b][:, 0:128], b, [(w_in_sb[:, kc, 512:640], h2T[:, kc, 0:128]) for kc in range(8)], ["w_in", ("h2T", 0)])
        for kvh in range(2):
            r0 = kvh * 64
            P.op("dve", lambda e, b=b, kvh=kvh, r0=r0: e.tensor_copy(out=kTp[r0:r0 + 64, kvh, NKR - 1, :], in_=ps[b][r0:r0 + 64, 0:128]),
                 reads=[("ps", b), "kT_zero", ("kT", NKR - 1)], writes=[("kT", NKR - 1)])
        for c in range(4):
            b = bank()
            mm_group(ps[b][:, 0:128], b, [(w_in_sb[:, kc, 768 + c * 128:768 + (c + 1) * 128], h2T[:, kc, 0:128]) for kc in range(8)],
                     ["w_in", ("h2T", 0)])
            P.op("dve", lambda e, b=b, c=c: e.tensor_scalar(out=uT[:, c, 0:16], in0=ps[b][:, 112:128], scalar1=flag[:, 0:1],
                                                            scalar2=None, op0=ALU.mult),
                 reads=[("ps", b), "flag", "uT_halo"], writes=["uT_halo"])
        b = bank()
        mm_group(ps[b][:, 0:128], b, [(h2T[:, kc, 0:128], w_in_sb[:, kc, 640:768]) for kc in range(8)], ["w_in", ("h2T", 0)])
        vview0 = Vaug[:, 0, :].rearrange("p (b d) -> p b d", d=64)[:, 0:4:3, :]
        P.op("dve", lambda e, b=b: e.tensor_scalar(out=vview0, in0=ps[b][:, 0:128].rearrange("p (b d) -> p b d", d=64),
                                                   scalar1=flag[:, 0:1], scalar2=None, op0=ALU.mult),
             reads=[("ps", b), "flag", "Vaug_ones"], writes=[("V", 0)])
        P.op("dve", lambda e: e.tensor_copy(out=Vaug[:, 0, 64:192], in_=flag[:, 0:1].broadcast_to([128, 128])),
             reads=["flag", "Vaug_ones", ("V", 0)], writes=[("V", 0)])

        def norm_pre(xap, xkey, rows=128):
            sslot = nxt("ss", 8)
            sx = nxt("xn", 2)
            P.op("act", lambda e: e.activation(out=xn[0:rows, sx, :], in_=xap, func=AF.Square, accum_out=ss[0:rows, sslot, 0:1]),
                 reads=[xkey], writes=[("ss", sslot), ("xn", sx)])
            P.op("pool", lambda e: e.tensor_scalar(out=var[0:rows, sslot, 0:1], in0=ss[0:rows, sslot, 0:1], scalar1=1.0 / D, scalar2=EPS,
                                                   op0=ALU.mult, op1=ALU.add), reads=[("ss", sslot)], writes=[("var", sslot)])
            P.op("pool", lambda e: e.tensor_tensor(out=rstd[0:rows, sslot, 0:1], in0=var[0:rows, sslot, 0:1], in1=expm[0:rows, 0:1], op=ALU.pow),
                 reads=[("var", sslot), "expm"], writes=[("rstd", sslot)])
            P.op("dve", lambda e: e.tensor_scalar(out=xn[0:rows, sx, :], in0=xap, scalar1=rstd[0:rows, sslot, 0:1], scalar2=None, op0=ALU.mult),
                 reads=[xkey, ("rstd", sslot), ("xn", sx)], writes=[("xn", sx)])
            return sx

        def norm_pe(sx, gam, gam_key, dst, dst_key, off, rows=128):
            b = bank()
            psb = ps[b][:].bitcast(BF16).rearrange("p (k t) -> p k t", t=128)
            for kc in range(8):
                P.op("pe", lambda e, kc=kc: e.transpose(psb[:, kc, 0:rows], xn[0:rows, sx, kc * 128:(kc + 1) * 128], ident[0:rows, 0:rows]),
                     reads=[("xn", sx), "ident"], writes=[("ps", b)])
            P.op("dve", lambda e: e.tensor_tensor(out=dst[:, :, off:off + rows], in0=psb[:, :, 0:rows],
                                                  in1=gam[:, :, None].broadcast_to([128, 8, rows]), op=ALU.mult),
                 reads=[("ps", b), gam_key], writes=[dst_key])

        def attn_scores(g, t, T):
            pts = {}
            for kb, Tk in ((0, T - 1), (1, T)):
                ks = Tk % NKR
                for kvh in range(2):
                    b = bank()
                    pslot = nxt("PT", 8)
                    pts[(kb, kvh)] = pslot
                    P.op("pe", lambda e, b=b, ks=ks, kvh=kvh: e.matmul(
                        ps[b][:], lhsT=kTp[:, kvh, ks, :], rhs=qT[:, :, t * 128:(t + 1) * 128],
                        start=True, stop=False), reads=[("kT", ks), "qT"], writes=[("ps", b)])
                    P.op("pe", lambda e, b=b, kb=kb, kvh=kvh: e.matmul(ps[b][:], lhsT=ident[:], rhs=biasT[:, kb, kvh, :],
                                                                        start=False, stop=True),
                         reads=["ident", "biasT"], writes=[("ps", b)])
                    P.op("act", lambda e, b=b, pslot=pslot: e.activation(out=PT[:, pslot, :], in_=ps[b][:], func=AF.Exp, scale=0.125),
                         reads=[("ps", b)], writes=[("PT", pslot)])
            return pts

        def attn_pv(g, t, T, pts):
            pb = {}
            for kvh in range(2):
                b = bank()
                pb[kvh] = b
                vlo, vhi = (0, 65) if kvh == 0 else (191, 256)
                for gg in range(4):
                    for kb, Tk in ((0, T - 1), (1, T)):
                        vs = (Tk + 1) % NV
                        P.op("pe", lambda e, b=b, kvh=kvh, gg=gg, kb=kb, vs=vs, vlo=vlo, vhi=vhi, s_=pts[(kb, kvh)]: e.matmul(
                            ps[b][:, gg * 65:(gg + 1) * 65], lhsT=PT[:, s_, gg * 128:(gg + 1) * 128], rhs=Vaug[:, vs, vlo:vhi],
                            start=(kb == 0), stop=(kb == 1)), reads=[("V", vs), ("PT", pts[(kb, kvh)])], writes=[("ps", b)])
            ds = nxt("den", 2)
            ms = nxt("mtm", 2)
            for kvh in range(2):
                b = pb[kvh]
                pv = ps[b][:, 0:260].rearrange("p (g c) -> p g c", c=65)
                rc = 64 if kvh == 0 else 0
                P.op("dve", lambda e, pv=pv, rc=rc, kvh=kvh, ds=ds: e.tensor_tensor(
                    out=den[:, ds, kvh * 4:(kvh + 1) * 4, :], in0=pv[:, :, rc:rc + 1], in1=es[:, kvh * 4:(kvh + 1) * 4, None], op=ALU.add),
                    reads=[("ps", b), "es", ("den", ds)], writes=[("den", ds)])
            P.op("dve", lambda e, ds=ds: e.reciprocal(out=den[:, ds, :, :], in_=den[:, ds, :, :]), reads=[("den", ds)], writes=[("den", ds)])
            mv = mix_tm[:, ms, :].rearrange("p (j h d) -> p j h d", h=2, d=64)
            for kvh in range(2):
                b = pb[kvh]
                pv = ps[b][:, 0:260].rearrange("p (g c) -> p g c", c=65)
                a0 = 0 if kvh == 0 else 1
                P.op("dve", lambda e, pv=pv, a0=a0, kvh=kvh, ds=ds, mv=mv: e.tensor_tensor(
                    out=mv[:, :, kvh, :], in0=pv[:, :, a0:a0 + 64], in1=den[:, ds, kvh * 4:(kvh + 1) * 4, :].broadcast_to([128, 4, 64]),
                    op=ALU.mult), reads=[("ps", b), ("den", ds), ("mtm", ms)], writes=[("mtm", ms)])
            return ms

        def attn_tr(t, ms):
            bt = bank()
            psb = ps[bt][:].bitcast(BF16)[:, 0:512].rearrange("p (j q) -> p j q", q=128)
            for j in range(4):
                P.op("pe", lambda e, psb=psb, j=j, ms=ms: e.transpose(psb[:, j, :], mix_tm[:, ms, j * 128:(j + 1) * 128], ident[:]),
                     reads=[("mtm", ms), "ident"], writes=[("ps", bt)])
            P.op("act", lambda e, psb=psb: e.activation(out=mixT[:, 0:4, t * 128:(t + 1) * 128], in_=psb, func=AF.Copy),
                 reads=[("ps", bt)], writes=[("mixA", t)])

        load_x(NX - 1)
        if SMP_LEVEL >= 9:
            P.capture = []
            smp_norm1(); smp_inproj(); smp_ktrans(); smp_scores(); smp_pv(); smp_pool(); smp_wout()
            P.queue = P.capture
            P.capture = None
        SMP_G = 1

        hooks_on = [False]

        def hook():
            if hooks_on[0]:
                P.flush_chunk()

        for g in range(NG):
            tiles = [g * GT + t for t in range(GT)]
            if g == 0:
                late_setup()
            if g > 0:
                P.op("pool", lambda e: e.tensor_copy(out=uT[:, :, 0:16], in_=uT[:, :, GN:GN + 16]),
                     reads=[("uT_body", 0), ("uT_body", 1), ("uT_body", 2), ("uT_body", 3)] + ["uT_halo"], writes=["uT_halo"])
            for c in range(4):
                b = bank()
                mm_group(ps[b][:], b, [(w_in_sb[:, kc, 768 + c * 128:768 + (c + 1) * 128], hT[:, kc, :]) for kc in range(8)],
                         ["w_in", "hT"])
                P.op("dve", lambda e, b=b, c=c: e.tensor_copy(out=uT[:, c, 16:16 + GN], in_=ps[b][:]),
                     reads=[("ps", b), "uT_halo"], writes=[("uT_body", c)])
                hook()
            for c in range(4):
                b = bank()
                mm_group(ps[b][:], b, [(w_in_sb[:, kc, c * 128:(c + 1) * 128], hT[:, kc, :]) for kc in range(8)], ["w_inq", "hT"])
                P.op("dve", lambda e, b=b, c=c: e.tensor_copy(out=qT[:, c, :], in_=ps[b][:]), reads=[("ps", b)], writes=["qT"])
                hook()
            b = bank()
            mm_group(ps[b][:], b, [(w_in_sb[:, kc, 512:640], hT[:, kc, :]) for kc in range(8)], ["w_in", "hT"])
            ks0 = (g * GT) % NKR
            for kvh in range(2):
                r0 = kvh * 64
                P.op("dve", lambda e, b=b, kvh=kvh, r0=r0, ks0=ks0: e.tensor_copy(
                    out=kTp[r0:r0 + 64, kvh, ks0:ks0 + GT, :], in_=ps[b][r0:r0 + 64, :].rearrange("p (t n) -> p t n", n=128)),
                    reads=[("ps", b), "kT_zero"] + [("kT", ks0 + i) for i in range(GT)], writes=[("kT", ks0 + i) for i in range(GT)])
            hook()
            b = bank()
            for t, T in enumerate(tiles):
                mm_group(ps[b][:, t * 128:(t + 1) * 128], b,
                         [(hT[:, kc, t * 128:(t + 1) * 128], w_in_sb[:, kc, 640:768]) for kc in range(8)], ["w_in", "hT"])
            for t, T in enumerate(tiles):
                vs_ = (T + 1) % NV
                vv = Vaug[:, vs_, :].rearrange("p (b d) -> p b d", d=64)[:, 0:4:3, :]
                if T + 1 == NV:
                    P.op("dve", lambda e: e.memset(Vaug[:, 0, 64:192], 1.0), reads=[("V", 0)], writes=[("V", 0)])
                P.op("dve", lambda e, b=b, t=t, vv=vv: e.tensor_copy(
                    out=vv, in_=ps[b][:, t * 128:(t + 1) * 128].rearrange("p (b d) -> p b d", d=64)),
                    reads=[("ps", b), "Vaug_ones", ("V", vs_)], writes=[("V", vs_)])
            hook()
            if g == NG - 1:
                b1 = bank()
                mm_group(ps[b1][:], b1, [(hT[:, kc, 384:512], w_in_sb[:, kc, 512:1024]) for kc in range(8)], ["w_in", "hT"])
                P.op("dve", lambda e, b1=b1: e.tensor_copy(out=klast[:, 0:512], in_=ps[b1][:]), reads=[("ps", b1)], writes=KL)
                b2 = bank()
                mm_group(ps[b2][:, 0:256], b2, [(hT[:, kc, 384:512], w_in_sb[:, kc, 1024:1280]) for kc in range(8)], ["w_in", "hT"])
                P.op("dve", lambda e, b2=b2: e.tensor_copy(out=klast[:, 512:768], in_=ps[b2][:, 0:256]),
                     reads=[("ps", b2)] + KL, writes=KL)
                P.dma("sp", lambda e: e.dma_start(out=kvu_d, in_=klast), reads=KL, writes=[("out", "kvu")],
                      semkey=("st", "kvu"))
                out_keys.append(("out", "kvu"))
            for c in range(4):
                w = 2 ** (c + 1)
                cur = uT[:, c, :]
                for k in range(c + 1):
                    d = 2 ** k
                    dstp = ptmp[:, k % 2, :]
                    P.op("pool", lambda e, cur=cur, dstp=dstp, d=d: e.tensor_tensor(
                        out=dstp[:, d:16 + GN], in0=cur[:, d:16 + GN], in1=cur[:, 0:16 + GN - d], op=ALU.add),
                        reads=[("uT_body", c), "uT_halo", ("ptmp", (k + 1) % 2)] if k > 0 else [("uT_body", c), "uT_halo"],
                        writes=[("ptmp", k % 2)])
                    cur = dstp
                last = c % 2
                oth = (c + 1) % 2
                if g == 0:
                    P.op("pool", lambda e, cur=cur, c=c: e.tensor_tensor(out=fix16[:], in0=cur[:, 16:32], in1=icnt[:, c, :], op=ALU.mult),
                         reads=[("ptmp", last), "icnt"], writes=["fix16"])
                P.op("pool", lambda e, cur=cur, oth=oth, w=w: e.tensor_scalar(
                    out=ptmp[:, oth, 16:16 + GN], in0=cur[:, 16:16 + GN], scalar1=1.0 / w, scalar2=0.0, op0=ALU.mult, op1=ALU.add),
                    reads=[("ptmp", last)], writes=[("ptmp", oth)])
                P.op("pool", lambda e, oth=oth, c=c: e.tensor_tensor(out=mT[:, c, :], in0=ptmp[:, oth, 16:16 + GN], in1=uT[:, c, 16:16 + GN],
                                                                     op=ALU.subtract), reads=[("ptmp", oth), ("uT_body", c)], writes=[("mT", c)])
                if g == 0:
                    P.op("pool", lambda e, c=c: e.tensor_tensor(out=mT[:, c, 0:16], in0=fix16[:], in1=uT[:, c, 16:32], op=ALU.subtract),
                         reads=["fix16", ("uT_body", c), ("mT", c)], writes=[("mT", c)])
            if g == 0:
                smp_setup()
                issue_gu(NWGU)
                issue_wd(NWD)
            def pool_mm():
                for c in range(4):
                    b = bank()
                    P.op("pe", lambda e, b=b, c=c: e.matmul(ps[b][:], lhsT=w_pool_sb[:, c, :], rhs=mT[:, c, :], start=True, stop=True),
                         reads=["w_pool", ("mT", c)], writes=[("ps", b)])
                    P.op("act", lambda e, b=b, c=c: e.activation(out=mixT[:, 4 + c, :], in_=ps[b][:], func=AF.Copy, scale=psc[:, c:c + 1]),
                         reads=[("ps", b), "psc"], writes=[("mixP", c)])

            npre = {}

            def wout(t):
                T = tiles[t]
                sl = xslot(T)
                for hf in range(2):
                    b = bank()
                    mm_group(ps[b][:], b, [(mixT[:, ch, t * 128:(t + 1) * 128], w_out_sb[:, ch, hf * 512:(hf + 1) * 512]) for ch in range(8)],
                             [("mixA", t), "w_out"] + [("mixP", c) for c in range(4)])
                    P.op("dve", lambda e, b=b, sl=sl, hf=hf: e.tensor_tensor(
                        out=xs[:, sl, hf * 512:(hf + 1) * 512], in0=ps[b][:], in1=xs[:, sl, hf * 512:(hf + 1) * 512], op=ALU.add),
                        reads=[("ps", b), ("x", sl)], writes=[("x", sl)])
                npre[t] = norm_pre(xs[:, sl, :], ("x", sl))

            def n2pe(t):
                norm_pe(npre[t], g2, "g2", h2T, ("h2T", t), t * 128)

            pts = {0: attn_scores(g, 0, tiles[0])}
            hook()
            pts[1] = attn_scores(g, 1, tiles[1]); hook()
            mss = {}
            mss[0] = attn_pv(g, 0, tiles[0], pts[0]); hook()
            pts[2] = attn_scores(g, 2, tiles[2]); hook()
            mss[1] = attn_pv(g, 1, tiles[1], pts[1]); hook()
            attn_tr(0, mss[0])
            pts[3] = attn_scores(g, 3, tiles[3]); hook()
            mss[2] = attn_pv(g, 2, tiles[2], pts[2]); hook()
            attn_tr(1, mss[1])
            mss[3] = attn_pv(g, 3, tiles[3], pts[3]); hook()
            attn_tr(2, mss[2])
            pool_mm(); hook()
            attn_tr(3, mss[3])
            wout(0); hook()
            wout(1); hook()
            n2pe(0)
            wout(2); hook()
            n2pe(1)
            wout(3); hook()
            H2 = [("h2T", 0), ("h2T", 1), ("h2T", 2), ("h2T", 3)]
            NSPLIT = 2
            early = {}

            def gu_half(f, half):
                gi = g * NF + f
                s_ = gi % NWGU
                if half == 0:
                    early[f] = (bank(), bank())
                    held.update(early[f])
                bg_, bu_ = early[f]
                c0, c1 = half * 256, (half + 1) * 256
                rk = [("wgu", s_)] + H2[2 * half:2 * half + 2]
                mm_group(ps[bg_][:, c0:c1], bg_, [(wgu[:, s_, 0, kc, :], h2T[:, kc, c0:c1]) for kc in range(8)], rk)
                mm_group(ps[bu_][:, c0:c1], bu_, [(wgu[:, s_, 1, kc, :], h2T[:, kc, c0:c1]) for kc in range(8)], rk)

            for f in range(NSPLIT):
                gu_half(f, 0)
            n2pe(2)
            n2pe(3)
            for f in range(NSPLIT):
                gu_half(f, 1)
                held.difference_update(early[f])

            smp_here = (g == SMP_G and SMP_LEVEL >= 9)
            smp_front = (g == 0 and SMP_LEVEL >= 9)
            if smp_front:
                P.op("pool", lambda e: e.memset(gate_t[:], 0.0), writes=OWN_KEYS + AL_KEYS)
                smp_dma()
                P.flush_chunk()
                hooks_on[0] = True
            for f in range(NF):
                gi = g * NF + f
                s = gi % NWGU
                if f < NSPLIT:
                    bg, bu = early[f]
                else:
                    bg = bank()
                    mm_group(ps[bg][:], bg, [(wgu[:, s, 0, kc, :], h2T[:, kc, :]) for kc in range(8)], [("wgu", s)] + H2)
                    bu = bank()
                    mm_group(ps[bu][:], bu, [(wgu[:, s, 1, kc, :], h2T[:, kc, :]) for kc in range(8)], [("wgu", s)] + H2)
                if smp_here:
                    bs = bank()
                    mm_group(ps[bs][:, 0:NS], bs, [(wgu[:, s, 0, kc, :], h2Ts[:, kc, :]) for kc in range(8)], [("wgu", s), "h2Ts"])
                    mm_group(ps[bs][:, NS:2 * NS], bs, [(wgu[:, s, 1, kc, :], h2Ts[:, kc, :]) for kc in range(8)], [("wgu", s), "h2Ts"])
                issue_gu(gi + NWGU + 1)
                sgi = nxt("sg", 2)
                P.op("act", lambda e, bg=bg, sgi=sgi: e.activation(out=sg[:, sgi, :], in_=ps[bg][:], func=AF.Silu),
                     reads=[("ps", bg)], writes=[("sg", sgi)])
                P.op("dve", lambda e, bu=bu, sgi=sgi, f=f: e.tensor_tensor(out=actT[:, f, :], in0=ps[bu][:], in1=sg[:, sgi, :], op=ALU.mult),
                     reads=[("ps", bu), ("sg", sgi)], writes=[("actT", f)])
                if smp_here:
                    P.op("act", lambda e, bs=bs: e.activation(out=sgs[:], in_=ps[bs][:, 0:NS], func=AF.Silu), reads=[("ps", bs)], writes=["sgs"])
                    P.op("dve", lambda e, bs=bs, f=f: e.tensor_tensor(out=actTs[:, f, :], in0=ps[bs][:, NS:2 * NS], in1=sgs[:], op=ALU.mult),
                         reads=[("ps", bs), "sgs"], writes=["actTs"])
                if smp_front:
                    if f == 3:
                        smp_setup_selw()
                    hook()
                    if f == NF - 3:
                        P.flush_all()
                        hooks_on[0] = False
                        P.op("pool", lambda e: e.memset(gate_t[:], 0.0), writes=AL_KEYS + OWN_KEYS)
            bss = None
            if smp_here:
                bss = bank()
                held.add(bss)
                P.op("pe", lambda e, bss=bss: e.matmul(ps[bss][:, 0:8 * NS], lhsT=zb[:], rhs=ident[:], start=True, stop=False),
                     reads=["zb", "ident"], writes=[("ps", bss)])
            for hf in range(2):
                banks = [bank() for _ in range(GT)]
                held.update(banks)
                hoist = (hf == 0 and g + 1 < NG)
                nxt_tiles = [(g + 1) * GT + t for t in range(GT)]
                hsx = {}
                if hoist:
                    hsx[0] = norm_pre(xs[:, xslot(nxt_tiles[0]), :], ("x", xslot(nxt_tiles[0])))
                for f in range(NF):
                    di = g * 2 * NF + hf * NF + f
                    s = di % NWD
                    for t in range(GT):
                        P.op("pe", lambda e, t=t, f=f, s=s, bb=banks[t]: e.matmul(
                            ps[bb][:], lhsT=actT[:, f, t * 128:(t + 1) * 128], rhs=wd[:, s, :], start=(f == 0), stop=(f == NF - 1)),
                            reads=[("actT", f), ("wd", s)], writes=[("ps", banks[t])])
                    if smp_here:
                        for cc in range(4):
                            col = (hf * 4 + cc) * NS
                            last_ = (hf == 1 and f == NF - 1 and cc == 3)
                            P.op("pe", lambda e, f=f, s=s, cc=cc, col=col, last_=last_, bss=bss: e.matmul(
                                ps[bss][:, col:col + NS], lhsT=wd[:, s, cc * 128:(cc + 1) * 128], rhs=actTs[:, f, :], start=False, stop=last_),
                                reads=["actTs", ("wd", s)], writes=[("ps", bss)])
                    issue_wd(di + NWD + 1)
                    if hoist and f in (4, 9, 14, 19):
                        tt = (f - 4) // 5
                        norm_pe(hsx[tt], g1, "g1", hT, "hT", tt * 128)
                        if tt + 1 < GT:
                            hsx[tt + 1] = norm_pre(xs[:, xslot(nxt_tiles[tt + 1]), :], ("x", xslot(nxt_tiles[tt + 1])))
                held.difference_update(banks)
                for t, T in enumerate(tiles):
                    sl = xslot(T)
                    P.op("dve", lambda e, bb=banks[t], sl=sl, hf=hf: e.tensor_tensor(
                        out=xs[:, sl, hf * 512:(hf + 1) * 512], in0=ps[bb][:], in1=xs[:, sl, hf * 512:(hf + 1) * 512], op=ALU.add),
                        reads=[("ps", banks[t]), ("x", sl)], writes=[("x", sl)])
            if smp_here:
                held.discard(bss)
                P.op("act", lambda e, bss=bss: e.activation(out=ysT[:], in_=ps[bss][:, 0:8 * NS], func=AF.Copy), reads=[("ps", bss)], writes=["ysT"])
                for hf in range(2):
                    bt_ = bank()
                    for cc in range(4):
                        ch = hf * 4 + cc
                        P.op("pe", lambda e, bt_=bt_, cc=cc, ch=ch: e.transpose(ps[bt_][0:NS, cc * 128:(cc + 1) * 128], ysT[:, ch * NS:(ch + 1) * NS],
                                                                               identF[:]),
                             reads=["ysT", "identF"], writes=[("ps", bt_)])
                    P.op("dve", lambda e, bt_=bt_, hf=hf: e.tensor_tensor(out=xsm[:, hf * 512:(hf + 1) * 512], in0=ps[bt_][0:NS, :],
                                                                        in1=xsm[:, hf * 512:(hf + 1) * 512], op=ALU.add),
                         reads=[("ps", bt_), "xsm"], writes=["xsm"])
            for t, T in enumerate(tiles):
                sl = xslot(T)
                sslot = nxt("ss", 8)
                P.op("act", lambda e, sl=sl, sslot=sslot: e.activation(out=junk[:], in_=xs[:, sl, :], func=AF.Square,
                                                                       accum_out=ss[:, sslot, 0:1]),
                     reads=[("x", sl)], writes=[("ss", sslot)])
                P.op("pool", lambda e, sslot=sslot: e.tensor_scalar(out=var[:, sslot, 0:1], in0=ss[:, sslot, 0:1], scalar1=1.0 / D, scalar2=EPS,
                                                                    op0=ALU.mult, op1=ALU.add), reads=[("ss", sslot)], writes=[("var", sslot)])
                P.op("pool", lambda e, sslot=sslot: e.tensor_tensor(out=rstd[:, sslot, 0:1], in0=var[:, sslot, 0:1], in1=expm[:, 0:1], op=ALU.pow),
                     reads=[("var", sslot), "expm"], writes=[("rstd", sslot)])
                P.op("dve", lambda e, sl=sl, sslot=sslot: e.scalar_tensor_tensor(
                    out=xs[:, sl, :], in0=xs[:, sl, :], scalar=rstd[:, sslot, 0:1], in1=gft[:], op0=ALU.mult, op1=ALU.mult),
                    reads=[("x", sl), ("rstd", sslot), "gft"], writes=[("x", sl)])
                P.dma("sp", lambda e, sl=sl, T=T: e.dma_start(out=y_d[T * 128:(T + 1) * 128, :], in_=xs[:, sl, :]),
                      reads=[("x", sl)], writes=[("out", "y", T)], semkey=("st", sl))
                out_keys.append(("out", "y", T))
                if T + NX < NT:
                    load_x(T + NX)

            if smp_here:
                smp_final()

        P.op("sp", lambda e: e.nop(), reads=out_keys)
        P.emit(st)
    return nc


_CACHE = {}


def _prep_weights(inp):
    w_in = np.asarray(inp["w_in"][0], np.float32)
    qcols = []
    for j in range(4):
        qcols += list(range(j * 64, (j + 1) * 64)) + list(range((4 + j) * 64, (5 + j) * 64))
    cols = qcols + list(range(512, 1280))
    w_in_p = np.ascontiguousarray(w_in[:, cols])
    w_out = np.asarray(inp["w_out"][0], np.float32)
    rows = qcols + list(range(512, 1024))
    w_out_p = np.ascontiguousarray(w_out[rows, :])
    wg = np.asarray(inp["w_gate"][0], np.float32).reshape(8, 128, NF, 128)
    wu = np.asarray(inp["w_up"][0], np.float32).reshape(8, 128, NF, 128)
    w_gu = np.ascontiguousarray(np.stack([wg, wu], 0).transpose(3, 2, 0, 1, 4)).reshape(NF, 128, 2 * 8 * 128)
    wdn = np.asarray(inp["w_down"][0], np.float32).reshape(NF, 128, 2, 512)
    w_d = np.ascontiguousarray(wdn.transpose(2, 0, 1, 3)).reshape(2 * NF, 128, 512)
    return dict(
        w_in=w_in_p, w_out=w_out_p, w_gu=w_gu, w_d=w_d,
        w_pool=np.ascontiguousarray(np.asarray(inp["w_pool"][0], np.float32)),
        g1=np.ascontiguousarray(np.asarray(inp["norm1"][0], np.float32).reshape(8, 128).T),
        g2=np.ascontiguousarray(np.asarray(inp["norm2"][0], np.float32).reshape(8, 128).T),
        psc=np.ascontiguousarray(np.asarray(inp["pool_scale"][0], np.float32).reshape(4, 128).T),
        gf=np.ascontiguousarray(np.broadcast_to(np.asarray(inp["final_norm"], np.float32)[None, :], (128, D))),
        sinks=np.ascontiguousarray(np.broadcast_to(np.asarray(inp["attn_sinks"][0], np.float32)[None, :], (128, 8))),
    )


def kernel(**inp):
    if "nc" not in _CACHE:
        _CACHE["nc"] = build_program()
    nc = _CACHE["nc"]
    xp = np.asarray(inp["x_prompt"], np.float32)
    xsm = np.asarray(inp["x_sample"], np.float32)[:, 0, :]
    ck = np.asarray(inp["cache_k_window"], np.float32)[0].reshape(128, 128, 128)
    cv = np.asarray(inp["cache_v_window"], np.float32)[0].reshape(128, 128, 128)
    spool = np.asarray(inp["state_pool"], np.float32)[0]
    wts = _prep_weights(inp)
    in_maps = []
    for c in range(NCORES):
        b, h = c // 2, c % 2
        m = dict(wts)
        m["x"] = np.ascontiguousarray(xp[b, h * TPC:(h + 1) * TPC])
        m["xh"] = np.ascontiguousarray(xp[b, TPC - 128:TPC]) if h == 1 else np.zeros((128, D), np.float32)
        m["pos0"] = np.full((128, 1), float(h * TPC), np.float32)
        m["xs"] = np.ascontiguousarray(xsm[c * NS:(c + 1) * NS])
        m["ck"] = np.ascontiguousarray(ck[c * NS:(c + 1) * NS])
        m["cv"] = np.ascontiguousarray(cv[c * NS:(c + 1) * NS])
        m["spool"] = np.ascontiguousarray(spool[c * NS:(c + 1) * NS])
        in_maps.append(m)
    res = run_bass_kernel_spmd(nc, in_maps, core_ids=list(range(NCORES)))
    R = res.results
    y_prompt = np.stack([np.concatenate([R[2 * b]["y"], R[2 * b + 1]["y"]], 0) for b in range(4)], 0)
    kvu = np.stack([R[2 * b + 1]["kvu_last"] for b in range(4)], 0)
    new_k_prompt = np.ascontiguousarray(kvu[:, :, 0:128]).reshape(1, 4, 128, 2, 64)
    new_v_prompt = np.ascontiguousarray(kvu[:, :, 128:256]).reshape(1, 4, 128, 2, 64)
    new_pool_prompt = np.ascontiguousarray(kvu[:, 113:128, 256:768]).reshape(1, 4, 15, 512)
    y_sample = np.concatenate([R[c]["ys"] for c in range(NCORES)], 0).reshape(128, 1, D)
    new_k_sample = np.concatenate([R[c]["nk"] for c in range(NCORES)], 0).reshape(1, 128, 128, 2, 64)
    new_v_sample = np.concatenate([R[c]["nv"] for c in range(NCORES)], 0).reshape(1, 128, 128, 2, 64)
    new_pool_sample = np.concatenate([R[c]["npool"] for c in range(NCORES)], 0).reshape(1, 128, 15, 512)
    return (y_prompt.astype(np.float32), y_sample.astype(np.float32), new_k_prompt, new_v_prompt, new_pool_prompt,
            new_k_sample, new_v_sample, new_pool_sample)
```

```python
import numpy as np
from contextlib import ExitStack
import concourse.bass as bass
import concourse.mybir as mybir
from concourse.bass_utils import run_bass_kernel_spmd

F32 = mybir.dt.float32
BF16 = mybir.dt.bfloat16
I32 = mybir.dt.int32
AF = mybir.ActivationFunctionType
ALU = mybir.AluOpType
AX = mybir.AxisListType

NCORES = 8
D = 1024
TPC = 2048
NT = 16
GT = 4
NG = 4
GN = 512
NF = 22
NS = 16
EPS = 1e-5
NX = 8
NWGU = 3
NWD = 8
MASKV = -30000.0
SMP_LEVEL = 99


class Prog:
    ENG = ("pe", "act", "dve", "pool", "sp")

    def __init__(self, nc):
        self.nc = nc
        self.ops = []
        self.last_writer = {}
        self.readers = {}
        self.capture = None
        self.queue = []

    def flush_chunk(self):
        q = self.queue
        while q and q[0][0] == "pe":
            self._add(*q.pop(0))
        while q and q[0][0] != "pe":
            self._add(*q.pop(0))

    def flush_all(self):
        while self.queue:
            self._add(*self.queue.pop(0))

    def _add(self, eng, fn, reads, writes, dma, semkey=None):
        if self.capture is not None:
            self.capture.append((eng, fn, reads, writes, dma, semkey))
            return None
        o = dict(eng=eng, fn=fn, dma=dma, semkey=semkey, idx=len(self.ops), deps=set(), raw=set(), signal=False)
        for k in reads:
            w = self.last_writer.get(k)
            if w is not None:
                o["deps"].add(w); o["raw"].add(w)
        for k in writes:
            w = self.last_writer.get(k)
            if w is not None:
                o["deps"].add(w)
            for r in self.readers.get(k, {}).values():
                for ri in r:
                    o["deps"].add(ri)
        o["deps"].discard(o["idx"])
        for k in writes:
            self.last_writer[k] = o["idx"]
            self.readers[k] = {}
        for k in reads:
            d = self.readers.setdefault(k, {})
            if dma:
                d.setdefault("dma", []).append(o["idx"])
            else:
                d[eng] = [o["idx"]]
        self.ops.append(o)
        return o

    def op(self, eng, fn, reads=(), writes=()):
        return self._add(eng, fn, list(reads), list(writes), False)

    def dma(self, eng, fn, reads=(), writes=(), semkey=None):
        return self._add(eng, fn, list(reads), list(writes), True, semkey)

    def emit(self, stack):
        nc = self.nc
        ops = self.ops
        for o in ops:
            for d in o["deps"]:
                a = ops[d]
                if a["dma"]:
                    continue
                if a["eng"] == o["eng"] and (a["eng"] == "pe" or d not in o["raw"]):
                    continue
                a["signal"] = True
        cnt = {e: 0 for e in self.ENG}
        dcnt = {}
        for o in ops:
            if o["dma"]:
                dcnt[o["semkey"]] = dcnt.get(o["semkey"], 0) + 16
                o["sigval"] = dcnt[o["semkey"]]
            elif o["signal"]:
                cnt[o["eng"]] += 1
                o["sigval"] = cnt[o["eng"]]
        esem = {e: stack.enter_context(nc.semaphore("s_" + e)) for e in self.ENG}
        dsem = {k: stack.enter_context(nc.semaphore("d_%d" % i)) for i, k in enumerate(dcnt)}
        self.n_sems = len(esem) + len(dsem)
        block = stack.enter_context(nc.Block())

        def stream(eng):
            def body(e):
                waited = {}
                for o in ops:
                    if o["eng"] != eng:
                        continue
                    need = {}
                    for d in o["deps"]:
                        a = ops[d]
                        if a["dma"]:
                            key = ("d", a["semkey"]); val = a["sigval"]
                        else:
                            if a["eng"] == eng and (eng == "pe" or d not in o["raw"]):
                                continue
                            key = ("e", a["eng"]); val = a["sigval"]
                        if need.get(key, 0) < val:
                            need[key] = val
                    for key, val in need.items():
                        if waited.get(key, 0) < val:
                            sem = dsem[key[1]] if key[0] == "d" else esem[key[1]]
                            e.wait_ge(sem, val)
                            waited[key] = val
                    ins = o["fn"](e)
                    if o["dma"]:
                        ins.then_inc(dsem[o["semkey"]], 16)
                    elif o["signal"]:
                        ins.then_inc(esem[eng], 1)
            return body

        block.tensor(stream("pe"))
        block.scalar(stream("act"))
        block.vector(stream("dve"))
        block.gpsimd(stream("pool"))
        block.sync(stream("sp"))


def build_program():
    nc = bass.Bass("TRN2", target_bir_lowering=False)

    def din(name, shape):
        return nc.dram_tensor(name, shape, F32, kind="ExternalInput").ap()

    def dout(name, shape):
        return nc.dram_tensor(name, shape, F32, kind="ExternalOutput").ap()

    x_d = din("x", [TPC, D]); xh_d = din("xh", [128, D]); pos_d = din("pos0", [128, 1])
    w_in_d = din("w_in", [D, 1280]); w_out_d = din("w_out", [D, D])
    w_gu_d = din("w_gu", [NF, 128, 2 * 8 * 128]); w_d_d = din("w_d", [2 * NF, 128, 512])
    w_pool_d = din("w_pool", [4, 128, 128])
    g1_d = din("g1", [128, 8]); g2_d = din("g2", [128, 8]); psc_d = din("psc", [128, 4])
    gf_d = din("gf", [128, D]); sink_d = din("sinks", [128, 8])
    xs_d = din("xs", [NS, D]); ck_d = din("ck", [NS, 128, 128]); cv_d = din("cv", [NS, 128, 128])
    sp_d = din("spool", [NS, 15, 512])
    y_d = dout("y", [TPC, D]); kvu_d = dout("kvu_last", [128, 768])
    ys_d = dout("ys", [NS, D]); nk_d = dout("nk", [NS, 128, 128]); nv_d = dout("nv", [NS, 128, 128])
    np_d = dout("npool", [NS, 15, 512])

    st = ExitStack()
    with st:
        def sb(name, shape, dt=F32):
            return st.enter_context(nc.sbuf_tensor("sb_" + name, shape, dt))

        w_in_sb = sb("w_in_sb", [128, 8, 1280], BF16)
        w_out_sb = sb("w_out_sb", [128, 8, D], BF16)
        w_pool_sb = sb("w_pool_sb", [128, 4, 128], BF16)
        wgu = sb("wgu", [128, NWGU, 2, 8, 128], BF16)
        wd = sb("wd", [128, NWD, 512], BF16)
        xs = sb("xs", [128, NX, D], F32)
        hT = sb("hT", [128, 8, GN], BF16)
        h2T = sb("h2T", [128, 8, GN], BF16)
        xn = sb("xn", [128, 2, D], BF16)
        qT = sb("qT", [128, 4, GN], BF16)
        NKR = 8
        kTp = sb("kTp", [128, 2, NKR, 128], BF16)
        NV = 8
        Vaug = sb("Vaug", [128, NV, 256], BF16)
        NU = 8
        u_tm = sb("u_tm", [128, NU, 512], BF16)
        bandM = sb("bandM", [128, 3, 4, 128], BF16)
        mT = sb("mT", [128, 4, GN], BF16)
        PT = sb("PT", [128, 8, GN], BF16)
        rec = sb("rec", [128, 2, GN], F32)
        mixT = sb("mixT", [128, 8, GN], BF16)
        junk = mixT[:].rearrange("p c n -> p (c n)")[:, 0:D]
        actT = sb("actT", [128, NF, GN], BF16)
        sg = sb("sg", [128, 2, GN], F32)
        biasT = sb("biasT", [128, 2, 2, GN], BF16)
        gft = sb("gft", [128, D], F32)
        g1 = sb("g1", [128, 8]); g2 = sb("g2", [128, 8]); psc = sb("psc", [128, 4])
        sinks = sb("sinks", [128, 8]); es = sb("es", [128, 8]); es_hi = sb("es_hi", [128, 8], BF16)
        es_hif = sb("es_hif", [128, 8]); es_lo = sb("es_lo", [128, 8], BF16)
        pos0 = sb("pos0", [128, 1]); flag = sb("flag", [128, 1])
        ss = sb("ss", [128, 8, 4]); var = sb("var", [128, 8, 4]); rstd = sb("rstd", [128, 8, 4])
        expm = sb("expm", [128, 4])
        ident = sb("ident", [128, 128], BF16)
        mix_tm = sb("mix_tm", [128, 2, 512], BF16)
        den = sb("den", [128, 2, 8, 1], F32)
        icnt = sb("icnt", [128, 4, 16], F32)
        io16 = sb("io16", [128, 16], I32)
        io16f = sb("io16f", [128, 16], F32)
        identf = PT[:, 0, 0:256].bitcast(F32)
        iot = PT[:, 1, 0:256].bitcast(I32)
        Rf = PT[:, 2, 0:256].bitcast(F32)
        tmpb_v = [PT[:, 3, 0:256].bitcast(F32), PT[:, 4, 0:256].bitcast(F32)]
        klast = rec[:].rearrange("p a n -> p (a n)")[:, 0:768]
        KL = [("rec", 0), ("rec", 1)]
        xsm = sb("xsm", [NS, D], F32)
        hTs = sb("hTs", [128, 8, NS], BF16)
        h2Ts = sb("h2Ts", [128, 8, NS], BF16)
        qTs = sb("qTs", [128, 4, NS], BF16)
        uTs = sb("uTs", [128, 4, NS], F32)
        tmps = sb("tmps", [128, 4, NS], F32)
        mTs = sb("mTs", [128, 4, NS], BF16)
        mixTs = sb("mixTs", [128, 8, NS], BF16)
        actTs = sb("actTs", [128, NF, NS], BF16)
        sgs = sb("sgs", [128, NS], F32)
        s_sb = sb("s_sb", [128, NS, 8], F32)
        PTs = sb("PTs", [128, NS, 8], BF16)
        iop = sb("iop", [128, 1], I32)
        relc = sb("relc", [128, 1], F32)
        sbias = sb("sbias", [128, 8], F32)
        snew = sb("snew", [NS, 8], F32)
        pnew = sb("pnew", [NS, 8], F32)
        bdmask = sb("bdmask", [NS, 2, NS, 4], F32)
        pbd = sb("pbd", [128, 2, NS, 4], BF16)
        vnew = sb("vnew", [128, 256], BF16)
        recs = sb("recs", [128, 2, 64], F32)
        essm = sb("essm", [128, 2, NS, 4], F32)
        PTf = PT[:].rearrange("p s n -> p (s n)")
        ckb = PTf[:, 0:2048].rearrange("p (b f) -> p b f", f=128)
        ckT = PTf[:, 2048:4096].rearrange("p (b f) -> p b f", f=128)
        cva = mixT[:].rearrange("p c n -> p (c n)").rearrange("p (b f) -> p b f", f=256)
        qTf = qT[:].rearrange("p c n -> p (c n)").bitcast(F32)
        mTb = mT[:].rearrange("p c n -> p (c n)")
        mTf = mTb.bitcast(F32)
        tok_q = qTf[0:NS, 0:512]
        tok_kvu = sb("tok_kvu", [NS, 768], F32)
        hist_v = [qTf[:, 512:1024], mTf[:, 512:1024]]
        xns = mTb[0:NS, 0:1024]
        selw = sb("selw", [128, 2, 4, NS], F32)
        gate_t = sb("gate_t", [128, 1], F32)
        zb = sb("zb", [128, 128], BF16)
        identF = sb("identF", [128, 128], F32)
        ysT = sb("ysT", [128, 8 * NS], F32)

        class _Tok:
            def __getitem__(self, idx):
                p, cs = idx
                c0, c1 = cs.start, cs.stop
                if c1 <= 512:
                    return tok_q[:, c0:c1]
                assert c0 >= 512
                return tok_kvu[:, c0 - 512:c1 - 512]
        tok_s = _Tok()
        AL_KEYS = ["al_ckb", "al_ckT", "al_cvaO", "al_cva0", "al_cva1", "al_tok", "selw", "hist0", "hist1"]
        OWN_KEYS = ([("PT", i) for i in range(8)] + [("mixA", t) for t in range(4)] + [("mixP", c) for c in range(4)]
                    + ["qT"] + [("mT", c) for c in range(4)])
        ps = [st.enter_context(nc.psum_tensor("ps%d" % i, [128, 512], F32)) for i in range(8)]

        P = Prog(nc)
        rr = {"ps": 0, "xn": 0, "ss": 0, "PT": 0, "rec": 0, "sg": 0, "tmpb": 0, "den": 0, "mtm": 0}

        def nxt(name, n):
            v = rr[name]; rr[name] = (v + 1) % n
            return v

        held = set()

        def bank():
            while True:
                v = nxt("ps", 8)
                if v not in held:
                    return v

        def load_tab(nm, t, dsrc):
            P.dma("sp", (lambda e: e.dma_start(out=t[:], in_=dsrc)), writes=[nm], semkey=("ld", nm))

        def early_loads():
            load_x(-1)
            load_x(0)
            load_tab("g1", g1, g1_d); load_tab("pos0", pos0, pos_d)
            for T in range(1, GT):
                load_x(T)
            load_tab("psc", psc, psc_d); load_tab("sinks", sinks, sink_d); load_tab("g2", g2, g2_d)
            P.dma("sp", lambda e: e.dma_start(out=xsm[:], in_=xs_d), writes=["xsm"], semkey=("ld", "xsm"))
            for T in range(GT, NX - 1):
                load_x(T)
            load_tab("gft", gft, gf_d)
            P.dma("sp", lambda e: e.dma_start(out=nk_d[:, 0:127, :], in_=ck_d[:, 1:128, :]), writes=[("out", "nk0")], semkey=("st", "nk0"))
            P.dma("sp", lambda e: e.dma_start(out=nv_d[:, 0:127, :], in_=cv_d[:, 1:128, :]), writes=[("out", "nv0")], semkey=("st", "nv0"))
            P.dma("sp", lambda e: e.dma_start(out=np_d[:, 0:14, :], in_=sp_d[:, 1:15, :]), writes=[("out", "np0")], semkey=("st", "np0"))
            out_keys.extend([("out", "nk0"), ("out", "nv0"), ("out", "np0")])
        P.op("pool", lambda e: e.memset(expm[:], -0.5), writes=["expm"])
        P.op("pool", lambda e: e.memset(identf[:], 1.0), writes=[("PT", 0)])
        P.op("pool", lambda e: e.affine_select(out=identf[:], in_=identf[:], pattern=[[-1, 128]], compare_op=ALU.is_equal,
                                               fill=0.0, base=0, channel_multiplier=1), reads=[("PT", 0)], writes=[("PT", 0)])
        P.op("pool", lambda e: e.tensor_copy(out=ident[:], in_=identf[:]), reads=[("PT", 0)], writes=["ident"])
        P.op("pool", lambda e: e.tensor_copy(out=identF[:], in_=identf[:]), reads=[("PT", 0)], writes=["identF"])
        P.op("pool", lambda e: e.memset(zb[:], 0.0), writes=["zb"])
        w_in_v = w_in_d.rearrange("(kc p) n -> p kc n", p=128)
        P.dma("pool", lambda e: e.dma_start(out=w_in_sb[:, :, 512:1280], in_=w_in_v[:, :, 512:1280]), writes=["w_in"], semkey=("ld", "w_inA"))
        P.dma("pool", lambda e: e.dma_start(out=w_in_sb[:, :, 0:512], in_=w_in_v[:, :, 0:512]), writes=["w_inq"], semkey=("ld", "w_inB"))
        P.op("pool", lambda e: e.memset(Vaug[:, :, 64:192], 1.0), writes=["Vaug_ones"])
        P.op("pool", lambda e: e.memset(kTp[:], 0.0), writes=["kT_zero"])

        def late_setup():
            P.op("pool", lambda e: e.iota(iot[:], pattern=[[1, 128]], base=0, channel_multiplier=-1), writes=[("PT", 1)])
            P.op("dve", lambda e: e.tensor_copy(out=Rf[:], in_=iot[:]), reads=[("PT", 1)], writes=[("PT", 2)])
            for kvh in range(2):
                for g in range(4):
                    slope = 2.0 ** (-(kvh * 4 + g + 1))
                    for kb in range(2):
                        tb = nxt("tmpb", 2)
                        if kb == 1:
                            P.op("dve", lambda e, tb=tb, slope=slope: e.tensor_scalar(
                                out=tmpb_v[tb], in0=Rf[:], scalar1=-8.0 * slope, scalar2=None, op0=ALU.mult),
                                reads=[("PT", 2)], writes=[("PT", 3 + tb)])
                            P.op("pool", lambda e, tb=tb, kvh=kvh, g=g: e.affine_select(
                                out=biasT[:, 1, kvh, g * 128:(g + 1) * 128], in_=tmpb_v[tb], pattern=[[1, 128]],
                                compare_op=ALU.is_ge, fill=MASKV, base=0, channel_multiplier=-1),
                                reads=[("PT", 3 + tb)], writes=["biasT"])
                        else:
                            P.op("dve", lambda e, tb=tb, slope=slope: e.tensor_scalar(
                                out=tmpb_v[tb], in0=Rf[:], scalar1=128.0, scalar2=-8.0 * slope, op0=ALU.add, op1=ALU.mult),
                                reads=[("PT", 2)], writes=[("PT", 3 + tb)])
                            P.op("pool", lambda e, tb=tb, kvh=kvh, g=g: e.affine_select(
                                out=biasT[:, 0, kvh, g * 128:(g + 1) * 128], in_=tmpb_v[tb], pattern=[[-1, 128]],
                                compare_op=ALU.is_ge, fill=MASKV, base=0, channel_multiplier=1),
                                reads=[("PT", 3 + tb)], writes=["biasT"])
            P.op("act", lambda e: e.activation(out=es[:], in_=sinks[:], func=AF.Exp), reads=["sinks"], writes=["es"])
            P.op("dve", lambda e: e.tensor_copy(out=es_hi[:], in_=es[:]), reads=["es"], writes=["es_hi"])
            P.op("dve", lambda e: e.tensor_copy(out=es_hif[:], in_=es_hi[:]), reads=["es_hi"], writes=["es_hif"])
            P.op("dve", lambda e: e.tensor_tensor(out=es_lo[:], in0=es[:], in1=es_hif[:], op=ALU.subtract),
                 reads=["es", "es_hif"], writes=["es_lo"])
            P.op("pool", lambda e: e.iota(io16[:], pattern=[[1, 16]], base=1, channel_multiplier=0), writes=["io16"])
            P.op("dve", lambda e: e.tensor_copy(out=io16f[:], in_=io16[:]), reads=["io16"], writes=["io16f"])
            for c in range(4):
                P.op("dve", lambda e, c=c: e.tensor_scalar(out=icnt[:, c, :], in0=io16f[:], scalar1=pos0[:, 0:1],
                                                           scalar2=float(2 ** (c + 1)), op0=ALU.add, op1=ALU.min),
                     reads=["io16f", "pos0"], writes=["icnt"])
            P.op("dve", lambda e: e.reciprocal(out=icnt[:], in_=icnt[:]), reads=["icnt"], writes=["icnt"])
            sA, sB = tmpb_v[0], tmpb_v[1]
            sC = PT[:, 5, 0:256].bitcast(F32)
            KA, KB, KC = ("PT", 3), ("PT", 4), ("PT", 5)
            for wi in range(4):
                w = 2 ** (wi + 1)
                P.op("pool", lambda e: e.memset(sA, 1.0), reads=[KA], writes=[KA])
                P.op("pool", lambda e: e.affine_select(out=sA, in_=sA, pattern=[[1, 128]], compare_op=ALU.is_ge, fill=0.0, base=0,
                                                       channel_multiplier=-1), reads=[KA], writes=[KA])
                P.op("pool", lambda e, w=w: e.affine_select(out=sA, in_=sA, pattern=[[-1, 128]], compare_op=ALU.is_ge, fill=0.0, base=w - 1,
                                                            channel_multiplier=1), reads=[KA], writes=[KA])
                P.op("dve", lambda e, w=w: e.tensor_scalar(out=sB, in0=sA, scalar1=1.0 / w, scalar2=None, op0=ALU.mult), reads=[KA, KB], writes=[KB])
                P.op("dve", lambda e, wi=wi: e.tensor_tensor(out=bandM[:, 0, wi, :], in0=sB, in1=identf, op=ALU.subtract),
                     reads=[KB, ("PT", 0), "bandM"], writes=["bandM"])
                P.op("dve", lambda e, w=w: e.memset(sC, 1.0 / w), reads=[KC], writes=[KC])
                P.op("dve", lambda e, wi=wi: e.tensor_copy(out=sC[:, 0:16], in_=icnt[:, wi, :]), reads=[KC, "icnt"], writes=[KC])
                P.op("dve", lambda e: e.tensor_tensor(out=sC, in0=sC, in1=sA, op=ALU.mult), reads=[KC, KA], writes=[KC])
                P.op("dve", lambda e, wi=wi: e.tensor_tensor(out=bandM[:, 2, wi, :], in0=sC, in1=identf, op=ALU.subtract),
                     reads=[KC, ("PT", 0), "bandM"], writes=["bandM"])
                P.op("pool", lambda e, w=w: e.memset(sB, 1.0 / w), reads=[KB], writes=[KB])
                P.op("pool", lambda e, w=w, wi=wi: e.affine_select(out=bandM[:, 1, wi, :], in_=sB, pattern=[[-1, 128]], compare_op=ALU.is_ge,
                                                                   fill=0.0, base=-(129 - w), channel_multiplier=1),
                     reads=[KB, "bandM"], writes=["bandM"])

            P.dma("pool", lambda e: e.dma_start(out=w_pool_sb[:], in_=w_pool_d.rearrange("g c d -> c g d")),
                  writes=["w_pool"], semkey=("ld", "w_pool"))
            P.dma("pool", lambda e: e.dma_start(out=w_out_sb[:], in_=w_out_d.rearrange("(kc p) n -> p kc n", p=128)),
                  writes=["w_out"], semkey=("ld", "w_out"))


        def load_x(T):
            slot = (T % NX) if T >= 0 else NX - 1
            src = x_d[T * 128:(T + 1) * 128, :] if T >= 0 else xh_d
            P.dma("sp", lambda e: e.dma_start(out=xs[:, slot, :], in_=src), writes=[("x", slot)], semkey=("x", slot))

        def xslot(T):
            return (T % NX) if T >= 0 else NX - 1

        def norm_stage(tiles, gam, gam_key, dst, dst_key, rows=128, xn_priv=None):
            n = len(tiles)
            sslot = nxt("ss", 8)
            for i, (xap, xkey, off) in enumerate(tiles):
                jout = junk[0:rows, :] if xn_priv is None else xn_priv[0]
                jw = [] if xn_priv is None else [xn_priv[1]]
                P.op("act", lambda e, xap=xap, i=i, jout=jout: e.activation(out=jout, in_=xap, func=AF.Square,
                                                                            accum_out=ss[0:rows, sslot, i:i + 1]),
                     reads=[xkey] + jw, writes=[("ss", sslot)] + jw)
            P.op("pool", lambda e: e.tensor_scalar(out=var[0:rows, sslot, 0:n], in0=ss[0:rows, sslot, 0:n],
                                                   scalar1=1.0 / D, scalar2=EPS, op0=ALU.mult, op1=ALU.add),
                 reads=[("ss", sslot)], writes=[("var", sslot)])
            P.op("pool", lambda e: e.tensor_tensor(out=rstd[0:rows, sslot, 0:n], in0=var[0:rows, sslot, 0:n],
                                                   in1=expm[0:rows, 0:n], op=ALU.pow),
                 reads=[("var", sslot), "expm"], writes=[("rstd", sslot)])
            for i, (xap, xkey, off) in enumerate(tiles):
                if xn_priv is None:
                    s = nxt("xn", 2)
                    xn_ap, xn_key = xn[0:rows, s, :], ("xn", s)
                else:
                    xn_ap, xn_key = xn_priv
                P.op("act", lambda e, xap=xap, i=i, xn_ap=xn_ap: e.activation(out=xn_ap, in_=xap, func=AF.Copy,
                                                                              scale=rstd[0:rows, sslot, i:i + 1]),
                     reads=[xkey, ("rstd", sslot), xn_key], writes=[xn_key])
                b = bank()
                psb = ps[b][:].bitcast(BF16).rearrange("p (k t) -> p k t", t=128)
                for kc in range(8):
                    P.op("pe", lambda e, kc=kc, xn_ap=xn_ap, psb=psb: e.transpose(psb[:, kc, 0:rows], xn_ap[:, kc * 128:(kc + 1) * 128],
                                                                                  ident[0:rows, 0:rows]),
                         reads=[xn_key, "ident"], writes=[("ps", b)])
                P.op("dve", lambda e, psb=psb, off=off: e.tensor_tensor(
                    out=dst[:, :, off:off + rows], in0=psb[:, :, 0:rows], in1=gam[:, :, None].broadcast_to([128, 8, rows]),
                    op=ALU.mult), reads=[("ps", b), gam_key], writes=[dst_key])
            return sslot

        def mm_group(out_ap, b, pairs, reads):
            n = len(pairs)
            for i, (l, r) in enumerate(pairs):
                P.op("pe", lambda e, l=l, r=r, i=i: e.matmul(out_ap, lhsT=l, rhs=r, start=(i == 0), stop=(i == n - 1)),
                     reads=reads, writes=[("ps", b)])

        gu_issued = [0]
        wd_issued = [0]

        def issue_gu(upto):
            while gu_issued[0] < upto and gu_issued[0] < NG * NF:
                i = gu_issued[0]; f = i % NF; s = i % NWGU
                P.dma("pool", lambda e, f=f, s=s: e.dma_start(out=wgu[:, s].rearrange("p a k n -> p (a k n)"), in_=w_gu_d[f],
                                                              max_dma_last_dim=4096),
                      writes=[("wgu", s)], semkey=("wgu", s))
                gu_issued[0] += 1

        def issue_wd(upto):
            while wd_issued[0] < upto and wd_issued[0] < NG * 2 * NF:
                i = wd_issued[0]; j = i % (2 * NF); s = i % NWD
                P.dma("pool", lambda e, j=j, s=s: e.dma_start(out=wd[:, s, :], in_=w_d_d[j]),
                      writes=[("wd", s)], semkey=("wd", s))
                wd_issued[0] += 1

        out_keys = []
        SLOPES = [2.0 ** (-(h + 1)) for h in range(8)]

        CVA = ["al_cvaO", "al_cva0", "al_cva1"]

        def smp_dma():
            P.dma("pool", lambda e: e.dma_start(out=ckb, in_=ck_d.rearrange("b k f -> k b f")), reads=["al_ckb"], writes=["al_ckb"], semkey=("ld", "ckb"))
            P.op("pool", lambda e: e.memset(cva[:, :, 64:192], 1.0), reads=["al_cvaO"], writes=["al_cvaO"])
            cvsrc = cv_d.rearrange("b k f -> k b f")
            P.dma("pool", lambda e: e.dma_start(out=cva[:, :, 0:64], in_=cvsrc[:, :, 0:64]), reads=["al_cva0"], writes=["al_cva0"], semkey=("ld", "cva0"))
            P.dma("pool", lambda e: e.dma_start(out=cva[:, :, 192:256], in_=cvsrc[:, :, 64:128]), reads=["al_cva1"], writes=["al_cva1"], semkey=("ld", "cva1"))
            sp2 = sp_d.rearrange("b j c -> (b j) c")
            P.dma("sp", lambda e: e.dma_start(out=hist_v[0], in_=sp2[0:128, :]), reads=["hist0"], writes=["hist0"], semkey=("ld", "h0"))
            P.op("pool", lambda e: e.memset(hist_v[1], 0.0), reads=["hist1"], writes=["hist1"])
            P.dma("sp", lambda e: e.dma_start(out=hist_v[1][0:112, :], in_=sp2[128:240, :]), reads=["hist1"], writes=["hist1"], semkey=("ld", "h1"))

        def smp_setup():
            P.op("pool", lambda e: e.memset(vnew[:], 0.0), writes=["vnew"])
            P.op("pool", lambda e: e.memset(vnew[0:NS, 64:192], 1.0), reads=["vnew"], writes=["vnew"])
            P.op("pool", lambda e: e.memset(pbd[:], 0.0), writes=["pbd"])
            P.op("dve", lambda e: e.tensor_copy(out=relc[:], in_=iop[:]), reads=["iop"], writes=["relc"])
            for h in range(8):
                P.op("dve", lambda e, h=h: e.tensor_scalar(out=sbias[:, h:h + 1], in0=relc[:], scalar1=-8.0 * SLOPES[h], scalar2=None,
                                                           op0=ALU.mult), reads=["relc", "sbias"], writes=["sbias"])
            P.op("pool", lambda e: e.memset(bdmask[:], 1.0), writes=["bdmask"])
            P.op("pool", lambda e: e.affine_select(out=bdmask[:], in_=bdmask[:], pattern=[[0, 2], [1, NS], [0, 4]],
                                                   compare_op=ALU.is_equal, fill=0.0, base=0, channel_multiplier=-1),
                 reads=["bdmask"], writes=["bdmask"])
            P.op("dve", lambda e: e.tensor_copy(out=essm[:], in_=es[:].rearrange("p (k g) -> p k g", g=4)[:, :, None, :].broadcast_to([128, 2, NS, 4])),
                 reads=["es"], writes=["essm"])

        def smp_setup_selw():
            P.op("pool", lambda e: e.memset(selw[:], 1.0), reads=["selw"], writes=["selw"])
            for kt in range(2):
                for c in range(4):
                    w = 2 ** (c + 1)
                    P.op("pool", lambda e, kt=kt, c=c, w=w: e.affine_select(
                        out=selw[:, kt, c, :], in_=selw[:, kt, c, :], pattern=[[-15, NS]], compare_op=ALU.is_ge, fill=0.0,
                        base=kt * 128 - (16 - w), channel_multiplier=1), reads=["selw"], writes=["selw"])
                    P.op("pool", lambda e, kt=kt, c=c: e.affine_select(
                        out=selw[:, kt, c, :], in_=selw[:, kt, c, :], pattern=[[15, NS]], compare_op=ALU.is_ge, fill=0.0,
                        base=14 - kt * 128, channel_multiplier=-1), reads=["selw"], writes=["selw"])

        def smp_norm1():
            norm_stage([(xsm[:, :], "xsm", 0)], g1, "g1", hTs, "hTs", rows=NS, xn_priv=(xns, "al_tok"))

        def smp_inproj():
            b = bank()
            for c in range(4):
                mm_group(ps[b][:, c * NS:(c + 1) * NS], b, [(w_in_sb[:, kc, c * 128:(c + 1) * 128], hTs[:, kc, :]) for kc in range(8)],
                         ["w_inq", "hTs"])
            for c in range(4):
                mm_group(ps[b][:, (4 + c) * NS:(5 + c) * NS], b,
                         [(w_in_sb[:, kc, 768 + c * 128:768 + (c + 1) * 128], hTs[:, kc, :]) for kc in range(8)], ["w_in", "hTs"])
            P.op("dve", lambda e, b=b: e.tensor_copy(out=qTs[:], in_=ps[b][:, 0:4 * NS].rearrange("p (c n) -> p c n", n=NS)),
                 reads=[("ps", b)], writes=["qTs"])
            P.op("dve", lambda e, b=b: e.tensor_copy(out=uTs[:], in_=ps[b][:, 4 * NS:8 * NS].rearrange("p (c n) -> p c n", n=NS)),
                 reads=[("ps", b)], writes=["uTs"])
            for (c0, c1) in ((0, 512), (512, 1024), (1024, 1280)):
                b = bank()
                mm_group(ps[b][0:NS, 0:c1 - c0], b, [(hTs[:, kc, :], w_in_sb[:, kc, c0:c1]) for kc in range(8)],
                         ["w_inq" if c0 == 0 else "w_in", "hTs"])
                P.op("dve", lambda e, b=b, c0=c0, c1=c1: e.tensor_copy(out=tok_s[:, c0:c1], in_=ps[b][0:NS, 0:c1 - c0]),
                     reads=[("ps", b), "al_tok"], writes=["al_tok"])
            P.dma("sp", lambda e: e.dma_start(out=nk_d[:, 127, :], in_=tok_s[:, 512:640]), reads=["al_tok"], writes=[("out", "nk1")], semkey=("st", "nk1"))
            P.dma("sp", lambda e: e.dma_start(out=nv_d[:, 127, :], in_=tok_s[:, 640:768]), reads=["al_tok"], writes=[("out", "nv1")], semkey=("st", "nv1"))
            P.dma("sp", lambda e: e.dma_start(out=np_d[:, 14, :], in_=tok_s[:, 768:1280]), reads=["al_tok"], writes=[("out", "np1")], semkey=("st", "np1"))
            out_keys.extend([("out", "nk1"), ("out", "nv1"), ("out", "np1")])
            prodv = rec[0:NS, 0, :].rearrange("p (j h d) -> p j h d", h=2, d=64)
            P.op("dve", lambda e: e.tensor_tensor(
                out=prodv, in0=tok_s[:, 0:512].rearrange("p (j h d) -> p j h d", h=2, d=64),
                in1=tok_s[:, 512:640].rearrange("p (h d) -> p h d", d=64)[:, None, :, :].broadcast_to([NS, 4, 2, 64]), op=ALU.mult),
                reads=["al_tok"], writes=[("rec", 0)])
            P.op("dve", lambda e: e.tensor_reduce(out=snew[:], in_=rec[0:NS, 0, :].rearrange("p (a d) -> p a d", d=64), axis=AX.X, op=ALU.add),
                 reads=[("rec", 0)], writes=["snew"])
            P.op("act", lambda e: e.activation(out=pnew[:], in_=snew[:], func=AF.Exp, scale=0.125), reads=["snew"], writes=["pnew"])
            P.op("dve", lambda e: e.tensor_tensor(
                out=pbd[0:NS], in0=bdmask[:], in1=pnew[:].rearrange("p (j h) -> p h j", h=2)[:, :, None, :].broadcast_to([NS, 2, NS, 4]),
                op=ALU.mult), reads=["bdmask", "pnew", "pbd"], writes=["pbd"])
            P.op("dve", lambda e: e.tensor_copy(out=vnew[0:NS, :].rearrange("p (q d) -> p q d", d=64)[:, 0:4:3, :],
                                                in_=tok_s[:, 640:768].rearrange("p (h d) -> p h d", d=64)),
                 reads=["al_tok", "vnew"], writes=["vnew"])

        def smp_ktrans():
            for half in range(2):
                b = bank()
                psb = ps[b][:].bitcast(BF16).rearrange("p (k t) -> p k t", t=128)
                for i in range(8):
                    bb = half * 8 + i
                    P.op("pe", lambda e, psb=psb, i=i, bb=bb: e.transpose(psb[:, i, :], ckb[:, bb, :], ident[:]),
                         reads=["al_ckb", "ident"], writes=[("ps", b)])
                P.op("dve", lambda e, psb=psb, half=half: e.tensor_copy(out=ckT[:, half * 8:(half + 1) * 8, :], in_=psb),
                     reads=[("ps", b), "al_ckT"], writes=["al_ckT"])

        def smp_scores():
            for kvh in range(2):
                b = bank()
                r0 = kvh * 64
                for bb in range(NS):
                    P.op("pe", lambda e, b=b, bb=bb, r0=r0: e.matmul(
                        ps[b][:, bb * 4:bb * 4 + 4], lhsT=ckT[r0:r0 + 64, bb, :], rhs=qTs[r0:r0 + 64, :, bb],
                        start=True, stop=True), reads=["al_ckT", "qTs"], writes=[("ps", b)])
                P.op("dve", lambda e, b=b, kvh=kvh: e.tensor_tensor(
                    out=s_sb[:, :, kvh * 4:(kvh + 1) * 4], in0=ps[b][:, 0:NS * 4].rearrange("p (b g) -> p b g", g=4),
                    in1=sbias[:, None, kvh * 4:(kvh + 1) * 4].broadcast_to([128, NS, 4]), op=ALU.add),
                    reads=[("ps", b), "sbias", "s_sb"], writes=["s_sb"])
            P.op("act", lambda e: e.activation(out=PTs[:], in_=s_sb[:], func=AF.Exp, scale=0.125), reads=["s_sb"], writes=["PTs"])

        def smp_pv():
            for kvh in range(2):
                b = bank()
                a0, s0 = (0, 64) if kvh == 0 else (64, 0)
                for bb in range(NS):
                    P.op("pe", lambda e, b=b, kvh=kvh, bb=bb: e.matmul(
                        ps[b][:, bb * 4:(bb + 1) * 4], lhsT=vnew[:, kvh * 128:(kvh + 1) * 128], rhs=pbd[:, kvh, bb, :],
                        start=True, stop=False), reads=["vnew", "pbd"], writes=[("ps", b)])
                    P.op("pe", lambda e, b=b, kvh=kvh, bb=bb: e.matmul(
                        ps[b][:, bb * 4:(bb + 1) * 4], lhsT=cva[:, bb, kvh * 128:(kvh + 1) * 128], rhs=PTs[:, bb, kvh * 4:(kvh + 1) * 4],
                        start=False, stop=True), reads=CVA + ["PTs"], writes=[("ps", b)])
                P.op("dve", lambda e, b=b, kvh=kvh, s0=s0: e.tensor_tensor(
                    out=recs[s0:s0 + 64, kvh, :], in0=ps[b][s0:s0 + 64, 0:NS * 4],
                    in1=essm[s0:s0 + 64, kvh, :, :].rearrange("p b g -> p (b g)"), op=ALU.add),
                    reads=[("ps", b), "essm"], writes=[("recs", kvh)])
                P.op("dve", lambda e, kvh=kvh, s0=s0: e.reciprocal(out=recs[s0:s0 + 64, kvh, :], in_=recs[s0:s0 + 64, kvh, :]),
                     reads=[("recs", kvh)], writes=[("recs", kvh)])
                P.op("dve", lambda e, b=b, kvh=kvh, s0=s0, a0=a0: e.tensor_tensor(
                    out=mixTs[a0:a0 + 64, 0:4, :].rearrange("p g b -> p b g"),
                    in0=ps[b][a0:a0 + 64, 0:NS * 4].rearrange("p (b g) -> p b g", g=4),
                    in1=recs[s0:s0 + 64, kvh, :].rearrange("p (b g) -> p b g", g=4), op=ALU.mult),
                    reads=[("ps", b), ("recs", kvh)], writes=["mixTs"])

        def smp_pool():
            b = bank()
            for c in range(4):
                for kt, rows in ((0, 128), (1, 128)):
                    P.op("pe", lambda e, b=b, c=c, kt=kt, rows=rows: e.matmul(
                        ps[b][:, c * NS:(c + 1) * NS], lhsT=hist_v[kt][0:rows, c * 128:(c + 1) * 128], rhs=selw[0:rows, kt, c, :],
                        start=(kt == 0), stop=(kt == 1)), reads=["hist%d" % kt, "selw"], writes=[("ps", b)])
            P.op("dve", lambda e, b=b: e.tensor_tensor(out=tmps[:], in0=ps[b][:, 0:4 * NS].rearrange("p (c n) -> p c n", n=NS),
                                                       in1=uTs[:], op=ALU.add), reads=[("ps", b), "uTs"], writes=["tmps"])
            for c in range(4):
                P.op("dve", lambda e, c=c: e.scalar_tensor_tensor(out=mTs[:, c, :], in0=tmps[:, c, :], scalar=1.0 / (2 ** (c + 1)),
                                                                  in1=uTs[:, c, :], op0=ALU.mult, op1=ALU.subtract),
                     reads=["tmps", "uTs", "mTs"], writes=["mTs"])
            b2 = bank()
            for c in range(4):
                P.op("pe", lambda e, b2=b2, c=c: e.matmul(ps[b2][:, c * NS:(c + 1) * NS], lhsT=w_pool_sb[:, c, :], rhs=mTs[:, c, :],
                                                          start=True, stop=True), reads=["w_pool", "mTs"], writes=[("ps", b2)])
            for c in range(4):
                P.op("act", lambda e, b2=b2, c=c: e.activation(out=mixTs[:, 4 + c, :], in_=ps[b2][:, c * NS:(c + 1) * NS], func=AF.Copy,
                                                               scale=psc[:, c:c + 1]), reads=[("ps", b2), "psc", "mixTs"], writes=["mixTs"])

        def smp_wout():
            for hf in range(2):
                b = bank()
                mm_group(ps[b][0:NS, :], b, [(mixTs[:, ch, :], w_out_sb[:, ch, hf * 512:(hf + 1) * 512]) for ch in range(8)],
                         ["mixTs", "w_out"])
                P.op("dve", lambda e, b=b, hf=hf: e.tensor_tensor(out=xsm[:, hf * 512:(hf + 1) * 512], in0=ps[b][0:NS, :],
                                                                  in1=xsm[:, hf * 512:(hf + 1) * 512], op=ALU.add),
                     reads=[("ps", b), "xsm"], writes=["xsm"])
            norm_stage([(xsm[:, :], "xsm", 0)], g2, "g2", h2Ts, "h2Ts", rows=NS, xn_priv=(xns, "al_tok"))

        def smp_final():
            sslot = nxt("ss", 8)
            P.op("act", lambda e: e.activation(out=junk[0:NS, :], in_=xsm[:], func=AF.Square, accum_out=ss[0:NS, sslot, 0:1]),
                 reads=["xsm"], writes=[("ss", sslot)])
            P.op("pool", lambda e: e.tensor_scalar(out=var[0:NS, sslot, 0:1], in0=ss[0:NS, sslot, 0:1], scalar1=1.0 / D, scalar2=EPS,
                                                   op0=ALU.mult, op1=ALU.add), reads=[("ss", sslot)], writes=[("var", sslot)])
            P.op("pool", lambda e: e.tensor_tensor(out=rstd[0:NS, sslot, 0:1], in0=var[0:NS, sslot, 0:1], in1=expm[0:NS, 0:1], op=ALU.pow),
                 reads=[("var", sslot), "expm"], writes=[("rstd", sslot)])
            P.op("dve", lambda e: e.scalar_tensor_tensor(out=xsm[:], in0=xsm[:], scalar=rstd[0:NS, sslot, 0:1], in1=gft[0:NS, :],
                                                         op0=ALU.mult, op1=ALU.mult), reads=["xsm", ("rstd", sslot), "gft"], writes=["xsm"])
            P.dma("sp", lambda e: e.dma_start(out=ys_d, in_=xsm[:]), reads=["xsm"], writes=[("out", "ys")], semkey=("st", "ys"))
            out_keys.append(("out", "ys"))

        early_loads()
        P.op("dve", lambda e: e.tensor_scalar(out=flag[:], in0=pos0[:], scalar1=1.0, scalar2=None, op0=ALU.min),
             reads=["pos0"], writes=["flag"])
        P.op("pool", lambda e: e.iota(iop[:], pattern=[[0, 1]], base=128, channel_multiplier=-1), writes=["iop"])

        hs = xslot(-1)
        norm_stage([(xs[:, hs, :], ("x", hs), 0)], g1, "g1", h2T, ("h2T", 0))
        norm_stage([(xs[:, xslot(T), :], ("x", xslot(T)), T * 128) for T in range(GT)], g1, "g1", hT, "hT")
        b = bank()
        mm_group(ps[b][:, 0:128], b, [(w_in_sb[:, kc, 512:640], h2T[:, kc, 0:128]) for kc in range(8)], ["w_in", ("h2T", 0)])
        for kvh in range(2):
            r0 = kvh * 64
            P.op("dve", lambda e, b=b, kvh=kvh, r0=r0: e.tensor_copy(out=kTp[r0:r0 + 64, kvh, NKR - 1, :], in_=ps[b][r0:r0 + 64, 0:128]),
                 reads=[("ps", b), "kT_zero", ("kT", NKR - 1)], writes=[("kT", NKR - 1)])
        b = bank()
        mm_group(ps[b][:], b, [(h2T[:, kc, 0:128], w_in_sb[:, kc, 768:1280]) for kc in range(8)], ["w_in", ("h2T", 0)])
        P.op("dve", lambda e, b=b: e.tensor_scalar(out=u_tm[:, 0, :], in0=ps[b][:], scalar1=flag[:, 0:1], scalar2=None, op0=ALU.mult),
             reads=[("ps", b), "flag"], writes=[("utm", 0)])
        b = bank()
        mm_group(ps[b][:, 0:128], b, [(h2T[:, kc, 0:128], w_in_sb[:, kc, 640:768]) for kc in range(8)], ["w_in", ("h2T", 0)])
        vview0 = Vaug[:, 0, :].rearrange("p (b d) -> p b d", d=64)[:, 0:4:3, :]
        P.op("dve", lambda e, b=b: e.tensor_scalar(out=vview0, in0=ps[b][:, 0:128].rearrange("p (b d) -> p b d", d=64),
                                                   scalar1=flag[:, 0:1], scalar2=None, op0=ALU.mult),
             reads=[("ps", b), "flag", "Vaug_ones"], writes=[("V", 0)])
        P.op("dve", lambda e: e.tensor_copy(out=Vaug[:, 0, 64:192], in_=flag[:, 0:1].broadcast_to([128, 128])),
             reads=["flag", "Vaug_ones", ("V", 0)], writes=[("V", 0)])

        def norm_pre(xap, xkey, rows=128):
            sslot = nxt("ss", 8)
            sx = nxt("xn", 2)
            P.op("act", lambda e: e.activation(out=xn[0:rows, sx, :], in_=xap, func=AF.Square, accum_out=ss[0:rows, sslot, 0:1]),
                 reads=[xkey], writes=[("ss", sslot), ("xn", sx)])
            P.op("pool", lambda e: e.tensor_scalar(out=var[0:rows, sslot, 0:1], in0=ss[0:rows, sslot, 0:1], scalar1=1.0 / D, scalar2=EPS,
                                                   op0=ALU.mult, op1=ALU.add), reads=[("ss", sslot)], writes=[("var", sslot)])
            P.op("pool", lambda e: e.tensor_tensor(out=rstd[0:rows, sslot, 0:1], in0=var[0:rows, sslot, 0:1], in1=expm[0:rows, 0:1], op=ALU.pow),
                 reads=[("var", sslot), "expm"], writes=[("rstd", sslot)])
            P.op("dve", lambda e: e.tensor_scalar(out=xn[0:rows, sx, :], in0=xap, scalar1=rstd[0:rows, sslot, 0:1], scalar2=None, op0=ALU.mult),
                 reads=[xkey, ("rstd", sslot), ("xn", sx)], writes=[("xn", sx)])
            return sx

        def norm_pe(sx, gam, gam_key, dst, dst_key, off, rows=128):
            b = bank()
            psb = ps[b][:].bitcast(BF16).rearrange("p (k t) -> p k t", t=128)
            for kc in range(8):
                P.op("pe", lambda e, kc=kc: e.transpose(psb[:, kc, 0:rows], xn[0:rows, sx, kc * 128:(kc + 1) * 128], ident[0:rows, 0:rows]),
                     reads=[("xn", sx), "ident"], writes=[("ps", b)])
            P.op("dve", lambda e: e.tensor_tensor(out=dst[:, :, off:off + rows], in0=psb[:, :, 0:rows],
                                                  in1=gam[:, :, None].broadcast_to([128, 8, rows]), op=ALU.mult),
                 reads=[("ps", b), gam_key], writes=[dst_key])

        def attn_scores(g, t, T):
            pts = {}
            for kb, Tk in ((0, T - 1), (1, T)):
                ks = Tk % NKR
                for kvh in range(2):
                    b = bank()
                    pslot = nxt("PT", 8)
                    pts[(kb, kvh)] = pslot
                    P.op("pe", lambda e, b=b, ks=ks, kvh=kvh: e.matmul(
                        ps[b][:], lhsT=kTp[:, kvh, ks, :], rhs=qT[:, :, t * 128:(t + 1) * 128],
                        start=True, stop=False), reads=[("kT", ks), "qT"], writes=[("ps", b)])
                    P.op("pe", lambda e, b=b, kb=kb, kvh=kvh: e.matmul(ps[b][:], lhsT=ident[:], rhs=biasT[:, kb, kvh, :],
                                                                        start=False, stop=True),
                         reads=["ident", "biasT"], writes=[("ps", b)])
                    P.op("act", lambda e, b=b, pslot=pslot: e.activation(out=PT[:, pslot, :], in_=ps[b][:], func=AF.Exp, scale=0.125),
                         reads=[("ps", b)], writes=[("PT", pslot)])
            return pts

        def attn_pv(g, t, T, pts):
            pb = {}
            for kvh in range(2):
                b = bank()
                pb[kvh] = b
                vlo, vhi = (0, 65) if kvh == 0 else (191, 256)
                for gg in range(4):
                    for kb, Tk in ((0, T - 1), (1, T)):
                        vs = (Tk + 1) % NV
                        P.op("pe", lambda e, b=b, kvh=kvh, gg=gg, kb=kb, vs=vs, vlo=vlo, vhi=vhi, s_=pts[(kb, kvh)]: e.matmul(
                            ps[b][:, gg * 65:(gg + 1) * 65], lhsT=PT[:, s_, gg * 128:(gg + 1) * 128], rhs=Vaug[:, vs, vlo:vhi],
                            start=(kb == 0), stop=(kb == 1)), reads=[("V", vs), ("PT", pts[(kb, kvh)])], writes=[("ps", b)])
            ds = nxt("den", 2)
            ms = nxt("mtm", 2)
            for kvh in range(2):
                b = pb[kvh]
                pv = ps[b][:, 0:260].rearrange("p (g c) -> p g c", c=65)
                rc = 64 if kvh == 0 else 0
                P.op("dve", lambda e, pv=pv, rc=rc, kvh=kvh, ds=ds: e.tensor_tensor(
                    out=den[:, ds, kvh * 4:(kvh + 1) * 4, :], in0=pv[:, :, rc:rc + 1], in1=es[:, kvh * 4:(kvh + 1) * 4, None], op=ALU.add),
                    reads=[("ps", b), "es", ("den", ds)], writes=[("den", ds)])
            P.op("dve", lambda e, ds=ds: e.reciprocal(out=den[:, ds, :, :], in_=den[:, ds, :, :]), reads=[("den", ds)], writes=[("den", ds)])
            mv = mix_tm[:, ms, :].rearrange("p (j h d) -> p j h d", h=2, d=64)
            for kvh in range(2):
                b = pb[kvh]
                pv = ps[b][:, 0:260].rearrange("p (g c) -> p g c", c=65)
                a0 = 0 if kvh == 0 else 1
                P.op("dve", lambda e, pv=pv, a0=a0, kvh=kvh, ds=ds, mv=mv: e.tensor_tensor(
                    out=mv[:, :, kvh, :], in0=pv[:, :, a0:a0 + 64], in1=den[:, ds, kvh * 4:(kvh + 1) * 4, :].broadcast_to([128, 4, 64]),
                    op=ALU.mult), reads=[("ps", b), ("den", ds), ("mtm", ms)], writes=[("mtm", ms)])
            return ms

        def attn_tr(t, ms):
            bt = bank()
            psb = ps[bt][:].bitcast(BF16)[:, 0:512].rearrange("p (j q) -> p j q", q=128)
            for j in range(4):
                P.op("pe", lambda e, psb=psb, j=j, ms=ms: e.transpose(psb[:, j, :], mix_tm[:, ms, j * 128:(j + 1) * 128], ident[:]),
                     reads=[("mtm", ms), "ident"], writes=[("ps", bt)])
            P.op("act", lambda e, psb=psb: e.activation(out=mixT[:, 0:4, t * 128:(t + 1) * 128], in_=psb, func=AF.Copy),
                 reads=[("ps", bt)], writes=[("mixA", t)])

        load_x(NX - 1)
        if SMP_LEVEL >= 9:
            P.capture = []
            smp_norm1(); smp_inproj(); smp_ktrans(); smp_scores(); smp_pv(); smp_pool(); smp_wout()
            P.queue = P.capture
            P.capture = None
        SMP_G = 1

        hooks_on = [False]

        def hook():
            if hooks_on[0]:
                P.flush_chunk()

        for g in range(NG):
            tiles = [g * GT + t for t in range(GT)]
            if g == 0:
                late_setup()
            for t, T in enumerate(tiles):
                b = bank()
                us_ = (T + 1) % NU
                mm_group(ps[b][:], b, [(hT[:, kc, t * 128:(t + 1) * 128], w_in_sb[:, kc, 768:1280]) for kc in range(8)], ["w_in", "hT"])
                P.op("dve", lambda e, b=b, us_=us_: e.tensor_copy(out=u_tm[:, us_, :], in_=ps[b][:]), reads=[("ps", b), ("utm", us_)],
                     writes=[("utm", us_)])
                hook()
            for c in range(4):
                b = bank()
                mm_group(ps[b][:], b, [(w_in_sb[:, kc, c * 128:(c + 1) * 128], hT[:, kc, :]) for kc in range(8)], ["w_inq", "hT"])
                P.op("dve", lambda e, b=b, c=c: e.tensor_copy(out=qT[:, c, :], in_=ps[b][:]), reads=[("ps", b)], writes=["qT"])
                hook()
            b = bank()
            mm_group(ps[b][:], b, [(w_in_sb[:, kc, 512:640], hT[:, kc, :]) for kc in range(8)], ["w_in", "hT"])
            ks0 = (g * GT) % NKR
            for kvh in range(2):
                r0 = kvh * 64
                P.op("dve", lambda e, b=b, kvh=kvh, r0=r0, ks0=ks0: e.tensor_copy(
                    out=kTp[r0:r0 + 64, kvh, ks0:ks0 + GT, :], in_=ps[b][r0:r0 + 64, :].rearrange("p (t n) -> p t n", n=128)),
                    reads=[("ps", b), "kT_zero"] + [("kT", ks0 + i) for i in range(GT)], writes=[("kT", ks0 + i) for i in range(GT)])
            hook()
            b = bank()
            for t, T in enumerate(tiles):
                mm_group(ps[b][:, t * 128:(t + 1) * 128], b,
                         [(hT[:, kc, t * 128:(t + 1) * 128], w_in_sb[:, kc, 640:768]) for kc in range(8)], ["w_in", "hT"])
            for t, T in enumerate(tiles):
                vs_ = (T + 1) % NV
                vv = Vaug[:, vs_, :].rearrange("p (b d) -> p b d", d=64)[:, 0:4:3, :]
                if T + 1 == NV:
                    P.op("dve", lambda e: e.memset(Vaug[:, 0, 64:192], 1.0), reads=[("V", 0)], writes=[("V", 0)])
                P.op("dve", lambda e, b=b, t=t, vv=vv: e.tensor_copy(
                    out=vv, in_=ps[b][:, t * 128:(t + 1) * 128].rearrange("p (b d) -> p b d", d=64)),
                    reads=[("ps", b), "Vaug_ones", ("V", vs_)], writes=[("V", vs_)])
            hook()
            if g == NG - 1:
                b1 = bank()
                mm_group(ps[b1][:], b1, [(hT[:, kc, 384:512], w_in_sb[:, kc, 512:1024]) for kc in range(8)], ["w_in", "hT"])
                P.op("dve", lambda e, b1=b1: e.tensor_copy(out=klast[:, 0:512], in_=ps[b1][:]), reads=[("ps", b1)], writes=KL)
                b2 = bank()
                mm_group(ps[b2][:, 0:256], b2, [(hT[:, kc, 384:512], w_in_sb[:, kc, 1024:1280]) for kc in range(8)], ["w_in", "hT"])
                P.op("dve", lambda e, b2=b2: e.tensor_copy(out=klast[:, 512:768], in_=ps[b2][:, 0:256]),
                     reads=[("ps", b2)] + KL, writes=KL)
                P.dma("sp", lambda e: e.dma_start(out=kvu_d, in_=klast), reads=KL, writes=[("out", "kvu")],
                      semkey=("st", "kvu"))
                out_keys.append(("out", "kvu"))
            for c in range(4):
                b = bank()
                for t, T in enumerate(tiles):
                    us_, up_ = (T + 1) % NU, T % NU
                    kind = 2 if T == 0 else 0
                    P.op("pe", lambda e, b=b, t=t, us_=us_, kind=kind, c=c: e.matmul(
                        ps[b][:, t * 128:(t + 1) * 128], lhsT=u_tm[:, us_, c * 128:(c + 1) * 128], rhs=bandM[:, kind, c, :], start=True, stop=False),
                        reads=[("utm", us_), "bandM"], writes=[("ps", b)])
                    P.op("pe", lambda e, b=b, t=t, up_=up_, c=c: e.matmul(
                        ps[b][:, t * 128:(t + 1) * 128], lhsT=u_tm[:, up_, c * 128:(c + 1) * 128], rhs=bandM[:, 1, c, :], start=False, stop=True),
                        reads=[("utm", up_), "bandM"], writes=[("ps", b)])
                P.op("act", lambda e, b=b, c=c: e.activation(out=mT[:, c, :], in_=ps[b][:], func=AF.Copy), reads=[("ps", b)], writes=[("mT", c)])
            if g == 0:
                smp_setup()
                issue_gu(NWGU)
                issue_wd(NWD)
            def pool_mm():
                for c in range(4):
                    b = bank()
                    P.op("pe", lambda e, b=b, c=c: e.matmul(ps[b][:], lhsT=w_pool_sb[:, c, :], rhs=mT[:, c, :], start=True, stop=True),
                         reads=["w_pool", ("mT", c)], writes=[("ps", b)])
                    P.op("act", lambda e, b=b, c=c: e.activation(out=mixT[:, 4 + c, :], in_=ps[b][:], func=AF.Copy, scale=psc[:, c:c + 1]),
                         reads=[("ps", b), "psc"], writes=[("mixP", c)])

            npre = {}

            def wout(t):
                T = tiles[t]
                sl = xslot(T)
                for hf in range(2):
                    b = bank()
                    mm_group(ps[b][:], b, [(mixT[:, ch, t * 128:(t + 1) * 128], w_out_sb[:, ch, hf * 512:(hf + 1) * 512]) for ch in range(8)],
                             [("mixA", t), "w_out"] + [("mixP", c) for c in range(4)])
                    P.op("dve", lambda e, b=b, sl=sl, hf=hf: e.tensor_tensor(
                        out=xs[:, sl, hf * 512:(hf + 1) * 512], in0=ps[b][:], in1=xs[:, sl, hf * 512:(hf + 1) * 512], op=ALU.add),
                        reads=[("ps", b), ("x", sl)], writes=[("x", sl)])
                npre[t] = norm_pre(xs[:, sl, :], ("x", sl))

            def n2pe(t):
                norm_pe(npre[t], g2, "g2", h2T, ("h2T", t), t * 128)

            pts = {0: attn_scores(g, 0, tiles[0])}
            hook()
            pts[1] = attn_scores(g, 1, tiles[1]); hook()
            mss = {}
            mss[0] = attn_pv(g, 0, tiles[0], pts[0]); hook()
            pool_mm(); hook()
            pts[2] = attn_scores(g, 2, tiles[2]); hook()
            mss[1] = attn_pv(g, 1, tiles[1], pts[1]); hook()
            attn_tr(0, mss[0])
            pts[3] = attn_scores(g, 3, tiles[3]); hook()
            mss[2] = attn_pv(g, 2, tiles[2], pts[2]); hook()
            attn_tr(1, mss[1])
            wout(0); hook()
            mss[3] = attn_pv(g, 3, tiles[3], pts[3]); hook()
            attn_tr(2, mss[2])
            wout(1); hook()
            n2pe(0)
            attn_tr(3, mss[3])
            wout(2); hook()
            n2pe(1)
            wout(3); hook()
            H2 = [("h2T", 0), ("h2T", 1), ("h2T", 2), ("h2T", 3)]
            NSPLIT = 2
            early = {}

            def gu_half(f, half):
                gi = g * NF + f
                s_ = gi % NWGU
                if half == 0:
                    early[f] = (bank(), bank())
                    held.update(early[f])
                bg_, bu_ = early[f]
                c0, c1 = half * 256, (half + 1) * 256
                rk = [("wgu", s_)] + H2[2 * half:2 * half + 2]
                mm_group(ps[bg_][:, c0:c1], bg_, [(wgu[:, s_, 0, kc, :], h2T[:, kc, c0:c1]) for kc in range(8)], rk)
                mm_group(ps[bu_][:, c0:c1], bu_, [(wgu[:, s_, 1, kc, :], h2T[:, kc, c0:c1]) for kc in range(8)], rk)

            for f in range(NSPLIT):
                gu_half(f, 0)
            n2pe(2)
            n2pe(3)
            for f in range(NSPLIT):
                gu_half(f, 1)
                held.difference_update(early[f])

            smp_here = (g == SMP_G and SMP_LEVEL >= 9)
            smp_front = (g == 0 and SMP_LEVEL >= 9)
            if smp_front:
                P.op("pool", lambda e: e.memset(gate_t[:], 0.0), writes=OWN_KEYS + AL_KEYS)
                smp_dma()
                P.flush_chunk()
                hooks_on[0] = True
            for f in range(NF):
                gi = g * NF + f
                s = gi % NWGU
                if f < NSPLIT:
                    bg, bu = early[f]
                else:
                    bg = bank()
                    mm_group(ps[bg][:], bg, [(wgu[:, s, 0, kc, :], h2T[:, kc, :]) for kc in range(8)], [("wgu", s)] + H2)
                    bu = bank()
                    mm_group(ps[bu][:], bu, [(wgu[:, s, 1, kc, :], h2T[:, kc, :]) for kc in range(8)], [("wgu", s)] + H2)
                if smp_here:
                    bs = bank()
                    mm_group(ps[bs][:, 0:NS], bs, [(wgu[:, s, 0, kc, :], h2Ts[:, kc, :]) for kc in range(8)], [("wgu", s), "h2Ts"])
                    mm_group(ps[bs][:, NS:2 * NS], bs, [(wgu[:, s, 1, kc, :], h2Ts[:, kc, :]) for kc in range(8)], [("wgu", s), "h2Ts"])
                issue_gu(gi + NWGU + 1)
                sgi = nxt("sg", 2)
                P.op("act", lambda e, bg=bg, sgi=sgi: e.activation(out=sg[:, sgi, :], in_=ps[bg][:], func=AF.Silu),
                     reads=[("ps", bg)], writes=[("sg", sgi)])
                P.op("dve", lambda e, bu=bu, sgi=sgi, f=f: e.tensor_tensor(out=actT[:, f, :], in0=ps[bu][:], in1=sg[:, sgi, :], op=ALU.mult),
                     reads=[("ps", bu), ("sg", sgi)], writes=[("actT", f)])
                if smp_here:
                    P.op("act", lambda e, bs=bs: e.activation(out=sgs[:], in_=ps[bs][:, 0:NS], func=AF.Silu), reads=[("ps", bs)], writes=["sgs"])
                    P.op("dve", lambda e, bs=bs, f=f: e.tensor_tensor(out=actTs[:, f, :], in0=ps[bs][:, NS:2 * NS], in1=sgs[:], op=ALU.mult),
                         reads=[("ps", bs), "sgs"], writes=["actTs"])
                if smp_front:
                    if f == 3:
                        smp_setup_selw()
                    hook()
                    if f == NF - 3:
                        P.flush_all()
                        hooks_on[0] = False
                        P.op("pool", lambda e: e.memset(gate_t[:], 0.0), writes=AL_KEYS + OWN_KEYS)
            bss = None
            if smp_here:
                bss = bank()
                held.add(bss)
                P.op("pe", lambda e, bss=bss: e.matmul(ps[bss][:, 0:8 * NS], lhsT=zb[:], rhs=ident[:], start=True, stop=False),
                     reads=["zb", "ident"], writes=[("ps", bss)])
            for hf in range(2):
                banks = [bank() for _ in range(GT)]
                held.update(banks)
                hoist = (hf == 0 and g + 1 < NG)
                nxt_tiles = [(g + 1) * GT + t for t in range(GT)]
                hsx = {}
                if hoist:
                    hsx[0] = norm_pre(xs[:, xslot(nxt_tiles[0]), :], ("x", xslot(nxt_tiles[0])))
                for f in range(NF):
                    di = g * 2 * NF + hf * NF + f
                    s = di % NWD
                    for t in range(GT):
                        P.op("pe", lambda e, t=t, f=f, s=s, bb=banks[t]: e.matmul(
                            ps[bb][:], lhsT=actT[:, f, t * 128:(t + 1) * 128], rhs=wd[:, s, :], start=(f == 0), stop=(f == NF - 1)),
                            reads=[("actT", f), ("wd", s)], writes=[("ps", banks[t])])
                    if smp_here:
                        for cc in range(4):
                            col = (hf * 4 + cc) * NS
                            last_ = (hf == 1 and f == NF - 1 and cc == 3)
                            P.op("pe", lambda e, f=f, s=s, cc=cc, col=col, last_=last_, bss=bss: e.matmul(
                                ps[bss][:, col:col + NS], lhsT=wd[:, s, cc * 128:(cc + 1) * 128], rhs=actTs[:, f, :], start=False, stop=last_),
                                reads=["actTs", ("wd", s)], writes=[("ps", bss)])
                    issue_wd(di + NWD + 1)
                    if hoist and f in (4, 9, 14, 19):
                        tt = (f - 4) // 5
                        norm_pe(hsx[tt], g1, "g1", hT, "hT", tt * 128)
                        if tt + 1 < GT:
                            hsx[tt + 1] = norm_pre(xs[:, xslot(nxt_tiles[tt + 1]), :], ("x", xslot(nxt_tiles[tt + 1])))
                held.difference_update(banks)
                for t, T in enumerate(tiles):
                    sl = xslot(T)
                    P.op("dve", lambda e, bb=banks[t], sl=sl, hf=hf: e.tensor_tensor(
                        out=xs[:, sl, hf * 512:(hf + 1) * 512], in0=ps[bb][:], in1=xs[:, sl, hf * 512:(hf + 1) * 512], op=ALU.add),
                        reads=[("ps", banks[t]), ("x", sl)], writes=[("x", sl)])
            if smp_here:
                held.discard(bss)
                P.op("act", lambda e, bss=bss: e.activation(out=ysT[:], in_=ps[bss][:, 0:8 * NS], func=AF.Copy), reads=[("ps", bss)], writes=["ysT"])
                for hf in range(2):
                    bt_ = bank()
                    for cc in range(4):
                        ch = hf * 4 + cc
                        P.op("pe", lambda e, bt_=bt_, cc=cc, ch=ch: e.transpose(ps[bt_][0:NS, cc * 128:(cc + 1) * 128], ysT[:, ch * NS:(ch + 1) * NS],
                                                                               identF[:]),
                             reads=["ysT", "identF"], writes=[("ps", bt_)])
                    P.op("dve", lambda e, bt_=bt_, hf=hf: e.tensor_tensor(out=xsm[:, hf * 512:(hf + 1) * 512], in0=ps[bt_][0:NS, :],
                                                                        in1=xsm[:, hf * 512:(hf + 1) * 512], op=ALU.add),
                         reads=[("ps", bt_), "xsm"], writes=["xsm"])
            for t, T in enumerate(tiles):
                sl = xslot(T)
                sslot = nxt("ss", 8)
                P.op("act", lambda e, sl=sl, sslot=sslot: e.activation(out=junk[:], in_=xs[:, sl, :], func=AF.Square,
                                                                       accum_out=ss[:, sslot, 0:1]),
                     reads=[("x", sl)], writes=[("ss", sslot)])
                P.op("pool", lambda e, sslot=sslot: e.tensor_scalar(out=var[:, sslot, 0:1], in0=ss[:, sslot, 0:1], scalar1=1.0 / D, scalar2=EPS,
                                                                    op0=ALU.mult, op1=ALU.add), reads=[("ss", sslot)], writes=[("var", sslot)])
                P.op("pool", lambda e, sslot=sslot: e.tensor_tensor(out=rstd[:, sslot, 0:1], in0=var[:, sslot, 0:1], in1=expm[:, 0:1], op=ALU.pow),
                     reads=[("var", sslot), "expm"], writes=[("rstd", sslot)])
                P.op("dve", lambda e, sl=sl, sslot=sslot: e.scalar_tensor_tensor(
                    out=xs[:, sl, :], in0=xs[:, sl, :], scalar=rstd[:, sslot, 0:1], in1=gft[:], op0=ALU.mult, op1=ALU.mult),
                    reads=[("x", sl), ("rstd", sslot), "gft"], writes=[("x", sl)])
                P.dma("sp", lambda e, sl=sl, T=T: e.dma_start(out=y_d[T * 128:(T + 1) * 128, :], in_=xs[:, sl, :]),
                      reads=[("x", sl)], writes=[("out", "y", T)], semkey=("st", sl))
                out_keys.append(("out", "y", T))
                if T + NX < NT:
                    load_x(T + NX)

            if smp_here:
                smp_final()

        P.op("sp", lambda e: e.nop(), reads=out_keys)
        P.emit(st)
    return nc


_CACHE = {}


def _prep_weights(inp):
    w_in = np.asarray(inp["w_in"][0], np.float32)
    qcols = []
    for j in range(4):
        qcols += list(range(j * 64, (j + 1) * 64)) + list(range((4 + j) * 64, (5 + j) * 64))
    cols = qcols + list(range(512, 1280))
    w_in_p = np.ascontiguousarray(w_in[:, cols])
    w_out = np.asarray(inp["w_out"][0], np.float32)
    rows = qcols + list(range(512, 1024))
    w_out_p = np.ascontiguousarray(w_out[rows, :])
    wg = np.asarray(inp["w_gate"][0], np.float32).reshape(8, 128, NF, 128)
    wu = np.asarray(inp["w_up"][0], np.float32).reshape(8, 128, NF, 128)
    w_gu = np.ascontiguousarray(np.stack([wg, wu], 0).transpose(3, 2, 0, 1, 4)).reshape(NF, 128, 2 * 8 * 128)
    wdn = np.asarray(inp["w_down"][0], np.float32).reshape(NF, 128, 2, 512)
    w_d = np.ascontiguousarray(wdn.transpose(2, 0, 1, 3)).reshape(2 * NF, 128, 512)
    return dict(
        w_in=w_in_p, w_out=w_out_p, w_gu=w_gu, w_d=w_d,
        w_pool=np.ascontiguousarray(np.asarray(inp["w_pool"][0], np.float32)),
        g1=np.ascontiguousarray(np.asarray(inp["norm1"][0], np.float32).reshape(8, 128).T),
        g2=np.ascontiguousarray(np.asarray(inp["norm2"][0], np.float32).reshape(8, 128).T),
        psc=np.ascontiguousarray(np.asarray(inp["pool_scale"][0], np.float32).reshape(4, 128).T),
        gf=np.ascontiguousarray(np.broadcast_to(np.asarray(inp["final_norm"], np.float32)[None, :], (128, D))),
        sinks=np.ascontiguousarray(np.broadcast_to(np.asarray(inp["attn_sinks"][0], np.float32)[None, :], (128, 8))),
    )


def kernel(**inp):
    if "nc" not in _CACHE:
        _CACHE["nc"] = build_program()
    nc = _CACHE["nc"]
    xp = np.asarray(inp["x_prompt"], np.float32)
    xsm = np.asarray(inp["x_sample"], np.float32)[:, 0, :]
    ck = np.asarray(inp["cache_k_window"], np.float32)[0].reshape(128, 128, 128)
    cv = np.asarray(inp["cache_v_window"], np.float32)[0].reshape(128, 128, 128)
    spool = np.asarray(inp["state_pool"], np.float32)[0]
    wts = _prep_weights(inp)
    in_maps = []
    for c in range(NCORES):
        b, h = c // 2, c % 2
        m = dict(wts)
        m["x"] = np.ascontiguousarray(xp[b, h * TPC:(h + 1) * TPC])
        m["xh"] = np.ascontiguousarray(xp[b, TPC - 128:TPC]) if h == 1 else np.zeros((128, D), np.float32)
        m["pos0"] = np.full((128, 1), float(h * TPC), np.float32)
        m["xs"] = np.ascontiguousarray(xsm[c * NS:(c + 1) * NS])
        m["ck"] = np.ascontiguousarray(ck[c * NS:(c + 1) * NS])
        m["cv"] = np.ascontiguousarray(cv[c * NS:(c + 1) * NS])
        m["spool"] = np.ascontiguousarray(spool[c * NS:(c + 1) * NS])
        in_maps.append(m)
    res = run_bass_kernel_spmd(nc, in_maps, core_ids=list(range(NCORES)))
    R = res.results
    y_prompt = np.stack([np.concatenate([R[2 * b]["y"], R[2 * b + 1]["y"]], 0) for b in range(4)], 0)
    kvu = np.stack([R[2 * b + 1]["kvu_last"] for b in range(4)], 0)
    new_k_prompt = np.ascontiguousarray(kvu[:, :, 0:128]).reshape(1, 4, 128, 2, 64)
    new_v_prompt = np.ascontiguousarray(kvu[:, :, 128:256]).reshape(1, 4, 128, 2, 64)
    new_pool_prompt = np.ascontiguousarray(kvu[:, 113:128, 256:768]).reshape(1, 4, 15, 512)
    y_sample = np.concatenate([R[c]["ys"] for c in range(NCORES)], 0).reshape(128, 1, D)
    new_k_sample = np.concatenate([R[c]["nk"] for c in range(NCORES)], 0).reshape(1, 128, 128, 2, 64)
    new_v_sample = np.concatenate([R[c]["nv"] for c in range(NCORES)], 0).reshape(1, 128, 128, 2, 64)
    new_pool_sample = np.concatenate([R[c]["npool"] for c in range(NCORES)], 0).reshape(1, 128, 15, 512)
    return (y_prompt.astype(np.float32), y_sample.astype(np.float32), new_k_prompt, new_v_prompt, new_pool_prompt,
            new_k_sample, new_v_sample, new_pool_sample)
```

```python
import numpy as np
from contextlib import ExitStack
import concourse.bass as bass
import concourse.mybir as mybir
from concourse.bass_utils import run_bass_kernel_spmd

F32 = mybir.dt.float32
BF16 = mybir.dt.bfloat16
I32 = mybir.dt.int32
AF = mybir.ActivationFunctionType
ALU = mybir.AluOpType
AX = mybir.AxisListType

NCORES = 8
D = 1024
TPC = 2048
NT = 16
GT = 4
NG = 4
GN = 512
NF = 22
NS = 16
EPS = 1e-5
NX = 8
NWGU = 3
NWD = 8
MASKV = -30000.0
SMP_LEVEL = 99


class Prog:
    ENG = ("pe", "act", "dve", "pool", "sp")

    def __init__(self, nc):
        self.nc = nc
        self.ops = []
        self.last_writer = {}
        self.readers = {}
        self.capture = None
        self.queue = []

    def flush_chunk(self):
        q = self.queue
        while q and q[0][0] == "pe":
            self._add(*q.pop(0))
        while q and q[0][0] != "pe":
            self._add(*q.pop(0))

    def flush_all(self):
        while self.queue:
            self._add(*self.queue.pop(0))

    def _add(self, eng, fn, reads, writes, dma, semkey=None):
        if self.capture is not None:
            self.capture.append((eng, fn, reads, writes, dma, semkey))
            return None
        o = dict(eng=eng, fn=fn, dma=dma, semkey=semkey, idx=len(self.ops), deps=set(), raw=set(), signal=False)
        for k in reads:
            w = self.last_writer.get(k)
            if w is not None:
                o["deps"].add(w); o["raw"].add(w)
        for k in writes:
            w = self.last_writer.get(k)
            if w is not None:
                o["deps"].add(w)
            for r in self.readers.get(k, {}).values():
                for ri in r:
                    o["deps"].add(ri)
        o["deps"].discard(o["idx"])
        for k in writes:
            self.last_writer[k] = o["idx"]
            self.readers[k] = {}
        for k in reads:
            d = self.readers.setdefault(k, {})
            if dma:
                d.setdefault("dma", []).append(o["idx"])
            else:
                d[eng] = [o["idx"]]
        self.ops.append(o)
        return o

    def op(self, eng, fn, reads=(), writes=()):
        return self._add(eng, fn, list(reads), list(writes), False)

    def dma(self, eng, fn, reads=(), writes=(), semkey=None):
        return self._add(eng, fn, list(reads), list(writes), True, semkey)

    def emit(self, stack):
        nc = self.nc
        ops = self.ops
        for o in ops:
            for d in o["deps"]:
                a = ops[d]
                if a["dma"]:
                    continue
                if a["eng"] == o["eng"] and (a["eng"] == "pe" or d not in o["raw"]):
                    continue
                a["signal"] = True
        cnt = {e: 0 for e in self.ENG}
        dcnt = {}
        for o in ops:
            if o["dma"]:
                dcnt[o["semkey"]] = dcnt.get(o["semkey"], 0) + 16
                o["sigval"] = dcnt[o["semkey"]]
            elif o["signal"]:
                cnt[o["eng"]] += 1
                o["sigval"] = cnt[o["eng"]]
        esem = {e: stack.enter_context(nc.semaphore("s_" + e)) for e in self.ENG}
        dsem = {k: stack.enter_context(nc.semaphore("d_%d" % i)) for i, k in enumerate(dcnt)}
        self.n_sems = len(esem) + len(dsem)
        block = stack.enter_context(nc.Block())

        def stream(eng):
            def body(e):
                waited = {}
                for o in ops:
                    if o["eng"] != eng:
                        continue
                    need = {}
                    for d in o["deps"]:
                        a = ops[d]
                        if a["dma"]:
                            key = ("d", a["semkey"]); val = a["sigval"]
                        else:
                            if a["eng"] == eng and (eng == "pe" or d not in o["raw"]):
                                continue
                            key = ("e", a["eng"]); val = a["sigval"]
                        if need.get(key, 0) < val:
                            need[key] = val
                    for key, val in need.items():
                        if waited.get(key, 0) < val:
                            sem = dsem[key[1]] if key[0] == "d" else esem[key[1]]
                            e.wait_ge(sem, val)
                            waited[key] = val
                    ins = o["fn"](e)
                    if o["dma"]:
                        ins.then_inc(dsem[o["semkey"]], 16)
                    elif o["signal"]:
                        ins.then_inc(esem[eng], 1)
            return body

        block.tensor(stream("pe"))
        block.scalar(stream("act"))
        block.vector(stream("dve"))
        block.gpsimd(stream("pool"))
        block.sync(stream("sp"))


def build_program():
    nc = bass.Bass("TRN2", target_bir_lowering=False)

    def din(name, shape):
        return nc.dram_tensor(name, shape, F32, kind="ExternalInput").ap()

    def dout(name, shape):
        return nc.dram_tensor(name, shape, F32, kind="ExternalOutput").ap()

    x_d = din("x", [TPC, D]); xh_d = din("xh", [128, D]); pos_d = din("pos0", [128, 1])
    w_in_d = din("w_in", [D, 1280]); w_out_d = din("w_out", [D, D])
    w_gu_d = din("w_gu", [NF, 128, 2 * 8 * 128]); w_d_d = din("w_d", [2 * NF, 128, 512])
    w_pool_d = din("w_pool", [4, 128, 128])
    g1_d = din("g1", [128, 8]); g2_d = din("g2", [128, 8]); psc_d = din("psc", [128, 4])
    gf_d = din("gf", [128, D]); sink_d = din("sinks", [128, 8])
    xs_d = din("xs", [NS, D]); ck_d = din("ck", [NS, 128, 128]); cv_d = din("cv", [NS, 128, 128])
    sp_d = din("spool", [NS, 15, 512])
    y_d = dout("y", [TPC, D]); kvu_d = dout("kvu_last", [128, 768])
    ys_d = dout("ys", [NS, D]); nk_d = dout("nk", [NS, 128, 128]); nv_d = dout("nv", [NS, 128, 128])
    np_d = dout("npool", [NS, 15, 512])

    st = ExitStack()
    with st:
        def sb(name, shape, dt=F32):
            return st.enter_context(nc.sbuf_tensor("sb_" + name, shape, dt))

        w_in_sb = sb("w_in_sb", [128, 8, 1280], BF16)
        w_out_sb = sb("w_out_sb", [128, 8, D], BF16)
        w_pool_sb = sb("w_pool_sb", [128, 4, 128], BF16)
        wgu = sb("wgu", [128, NWGU, 2, 8, 128], BF16)
        wd = sb("wd", [128, NWD, 512], BF16)
        xs = sb("xs", [128, NX, D], F32)
        hT = sb("hT", [128, 8, GN], BF16)
        h2T = sb("h2T", [128, 8, GN], BF16)
        xn = sb("xn", [128, 2, D], BF16)
        qT = sb("qT", [128, 4, GN], BF16)
        NKR = 8
        kTp = sb("kTp", [128, 2, NKR, 128], BF16)
        NV = 8
        Vaug = sb("Vaug", [128, NV, 256], BF16)
        uT = sb("uT", [128, 4, 16 + GN], F32)
        ptmp = sb("ptmp", [128, 2, 16 + GN], F32)
        mT = sb("mT", [128, 4, GN], BF16)
        PT = sb("PT", [128, 8, GN], BF16)
        rec = sb("rec", [128, 2, GN], F32)
        mixT = sb("mixT", [128, 8, GN], BF16)
        junk = mixT[:].rearrange("p c n -> p (c n)")[:, 0:D]
        actT = sb("actT", [128, NF, GN], BF16)
        sg = sb("sg", [128, 2, GN], F32)
        biasT = sb("biasT", [128, 2, 2, GN], BF16)
        gft = sb("gft", [128, D], F32)
        g1 = sb("g1", [128, 8]); g2 = sb("g2", [128, 8]); psc = sb("psc", [128, 4])
        sinks = sb("sinks", [128, 8]); es = sb("es", [128, 8]); es_hi = sb("es_hi", [128, 8], BF16)
        es_hif = sb("es_hif", [128, 8]); es_lo = sb("es_lo", [128, 8], BF16)
        pos0 = sb("pos0", [128, 1]); flag = sb("flag", [128, 1])
        ss = sb("ss", [128, 8, 4]); var = sb("var", [128, 8, 4]); rstd = sb("rstd", [128, 8, 4])
        expm = sb("expm", [128, 4])
        ident = sb("ident", [128, 128], BF16)
        mix_tm = sb("mix_tm", [128, 2, 512], BF16)
        den = sb("den", [128, 2, 8, 1], F32)
        icnt = sb("icnt", [128, 4, 16], F32)
        io16 = sb("io16", [128, 16], I32)
        io16f = sb("io16f", [128, 16], F32)
        fix16 = sb("fix16", [128, 16], F32)
        identf = PT[:, 0, 0:256].bitcast(F32)
        iot = PT[:, 1, 0:256].bitcast(I32)
        Rf = PT[:, 2, 0:256].bitcast(F32)
        tmpb_v = [PT[:, 3, 0:256].bitcast(F32), PT[:, 4, 0:256].bitcast(F32)]
        klast = rec[:].rearrange("p a n -> p (a n)")[:, 0:768]
        KL = [("rec", 0), ("rec", 1)]
        xsm = sb("xsm", [NS, D], F32)
        hTs = sb("hTs", [128, 8, NS], BF16)
        h2Ts = sb("h2Ts", [128, 8, NS], BF16)
        qTs = sb("qTs", [128, 4, NS], BF16)
        uTs = sb("uTs", [128, 4, NS], F32)
        tmps = sb("tmps", [128, 4, NS], F32)
        mTs = sb("mTs", [128, 4, NS], BF16)
        mixTs = sb("mixTs", [128, 8, NS], BF16)
        actTs = sb("actTs", [128, NF, NS], BF16)
        sgs = sb("sgs", [128, NS], F32)
        s_sb = sb("s_sb", [128, NS, 8], F32)
        PTs = sb("PTs", [128, NS, 8], BF16)
        iop = sb("iop", [128, 1], I32)
        relc = sb("relc", [128, 1], F32)
        sbias = sb("sbias", [128, 8], F32)
        snew = sb("snew", [NS, 8], F32)
        pnew = sb("pnew", [NS, 8], F32)
        bdmask = sb("bdmask", [NS, 2, NS, 4], F32)
        pbd = sb("pbd", [128, 2, NS, 4], BF16)
        vnew = sb("vnew", [128, 256], BF16)
        recs = sb("recs", [128, 2, 64], F32)
        essm = sb("essm", [128, 2, NS, 4], F32)
        PTf = PT[:].rearrange("p s n -> p (s n)")
        ckb = PTf[:, 0:2048].rearrange("p (b f) -> p b f", f=128)
        ckT = PTf[:, 2048:4096].rearrange("p (b f) -> p b f", f=128)
        cva = mixT[:].rearrange("p c n -> p (c n)").rearrange("p (b f) -> p b f", f=256)
        qTf = qT[:].rearrange("p c n -> p (c n)").bitcast(F32)
        mTb = mT[:].rearrange("p c n -> p (c n)")
        mTf = mTb.bitcast(F32)
        ptf = ptmp[:].rearrange("p a n -> p (a n)")
        tok_q = qTf[0:NS, 0:512]
        tok_kvu = ptf[0:NS, 0:768]
        hist_v = [qTf[:, 512:1024], mTf[:, 512:1024]]
        xns = mTb[0:NS, 0:1024]
        selw = ptf[:, 768:896].rearrange("p (k c n) -> p k c n", k=2, c=4)
        gate_t = sb("gate_t", [128, 1], F32)
        zb = sb("zb", [128, 128], BF16)
        identF = sb("identF", [128, 128], F32)
        ysT = sb("ysT", [128, 8 * NS], F32)

        class _Tok:
            def __getitem__(self, idx):
                p, cs = idx
                c0, c1 = cs.start, cs.stop
                if c1 <= 512:
                    return tok_q[:, c0:c1]
                assert c0 >= 512
                return tok_kvu[:, c0 - 512:c1 - 512]
        tok_s = _Tok()
        AL_KEYS = ["al_ckb", "al_ckT", "al_cvaO", "al_cva0", "al_cva1", "al_tok", "selw", "hist0", "hist1"]
        OWN_KEYS = ([("PT", i) for i in range(8)] + [("mixA", t) for t in range(4)] + [("mixP", c) for c in range(4)]
                    + ["qT"] + [("mT", c) for c in range(4)] + [("ptmp", 0), ("ptmp", 1)])
        ps = [st.enter_context(nc.psum_tensor("ps%d" % i, [128, 512], F32)) for i in range(8)]

        P = Prog(nc)
        rr = {"ps": 0, "xn": 0, "ss": 0, "PT": 0, "rec": 0, "sg": 0, "tmpb": 0, "den": 0, "mtm": 0}

        def nxt(name, n):
            v = rr[name]; rr[name] = (v + 1) % n
            return v

        held = set()

        def bank():
            while True:
                v = nxt("ps", 8)
                if v not in held:
                    return v

        def load_tab(nm, t, dsrc):
            P.dma("sp", (lambda e: e.dma_start(out=t[:], in_=dsrc)), writes=[nm], semkey=("ld", nm))

        def early_loads():
            load_x(-1)
            load_x(0)
            load_tab("g1", g1, g1_d); load_tab("pos0", pos0, pos_d)
            for T in range(1, GT):
                load_x(T)
            load_tab("psc", psc, psc_d); load_tab("sinks", sinks, sink_d); load_tab("g2", g2, g2_d)
            P.dma("sp", lambda e: e.dma_start(out=xsm[:], in_=xs_d), writes=["xsm"], semkey=("ld", "xsm"))
            for T in range(GT, NX - 1):
                load_x(T, after=["w_inq", "w_in"])
            load_tab("gft", gft, gf_d)
            P.dma("sp", lambda e: e.dma_start(out=nk_d[:, 0:127, :], in_=ck_d[:, 1:128, :]), writes=[("out", "nk0")], semkey=("st", "nk0"))
            P.dma("sp", lambda e: e.dma_start(out=nv_d[:, 0:127, :], in_=cv_d[:, 1:128, :]), writes=[("out", "nv0")], semkey=("st", "nv0"))
            P.dma("sp", lambda e: e.dma_start(out=np_d[:, 0:14, :], in_=sp_d[:, 1:15, :]), writes=[("out", "np0")], semkey=("st", "np0"))
            out_keys.extend([("out", "nk0"), ("out", "nv0"), ("out", "np0")])
        P.op("pool", lambda e: e.memset(expm[:], -0.5), writes=["expm"])
        P.op("pool", lambda e: e.memset(identf[:], 1.0), writes=[("PT", 0)])
        P.op("pool", lambda e: e.affine_select(out=identf[:], in_=identf[:], pattern=[[-1, 128]], compare_op=ALU.is_equal,
                                               fill=0.0, base=0, channel_multiplier=1), reads=[("PT", 0)], writes=[("PT", 0)])
        P.op("pool", lambda e: e.tensor_copy(out=ident[:], in_=identf[:]), reads=[("PT", 0)], writes=["ident"])
        P.op("pool", lambda e: e.tensor_copy(out=identF[:], in_=identf[:]), reads=[("PT", 0)], writes=["identF"])
        P.op("pool", lambda e: e.memset(zb[:], 0.0), writes=["zb"])
        w_in_v = w_in_d.rearrange("(kc p) n -> p kc n", p=128)
        P.dma("pool", lambda e: e.dma_start(out=w_in_sb[:, :, 512:1280], in_=w_in_v[:, :, 512:1280]), writes=["w_in"], semkey=("ld", "w_inA"))
        P.dma("pool", lambda e: e.dma_start(out=w_in_sb[:, :, 0:512], in_=w_in_v[:, :, 0:512]), writes=["w_inq"], semkey=("ld", "w_inB"))
        P.op("pool", lambda e: e.memset(Vaug[:, :, 64:192], 1.0), writes=["Vaug_ones"])
        P.op("pool", lambda e: e.memset(kTp[:], 0.0), writes=["kT_zero"])
        P.op("pool", lambda e: e.memset(uT[:, :, 0:1], 0.0), writes=["uT_halo"])

        def late_setup():
            P.op("pool", lambda e: e.iota(iot[:], pattern=[[1, 128]], base=0, channel_multiplier=-1), writes=[("PT", 1)])
            P.op("dve", lambda e: e.tensor_copy(out=Rf[:], in_=iot[:]), reads=[("PT", 1)], writes=[("PT", 2)])
            for kvh in range(2):
                for g in range(4):
                    slope = 2.0 ** (-(kvh * 4 + g + 1))
                    for kb in range(2):
                        tb = nxt("tmpb", 2)
                        if kb == 1:
                            P.op("dve", lambda e, tb=tb, slope=slope: e.tensor_scalar(
                                out=tmpb_v[tb], in0=Rf[:], scalar1=-8.0 * slope, scalar2=None, op0=ALU.mult),
                                reads=[("PT", 2)], writes=[("PT", 3 + tb)])
                            P.op("pool", lambda e, tb=tb, kvh=kvh, g=g: e.affine_select(
                                out=biasT[:, 1, kvh, g * 128:(g + 1) * 128], in_=tmpb_v[tb], pattern=[[1, 128]],
                                compare_op=ALU.is_ge, fill=MASKV, base=0, channel_multiplier=-1),
                                reads=[("PT", 3 + tb)], writes=["biasT"])
                        else:
                            P.op("dve", lambda e, tb=tb, slope=slope: e.tensor_scalar(
                                out=tmpb_v[tb], in0=Rf[:], scalar1=128.0, scalar2=-8.0 * slope, op0=ALU.add, op1=ALU.mult),
                                reads=[("PT", 2)], writes=[("PT", 3 + tb)])
                            P.op("pool", lambda e, tb=tb, kvh=kvh, g=g: e.affine_select(
                                out=biasT[:, 0, kvh, g * 128:(g + 1) * 128], in_=tmpb_v[tb], pattern=[[-1, 128]],
                                compare_op=ALU.is_ge, fill=MASKV, base=0, channel_multiplier=1),
                                reads=[("PT", 3 + tb)], writes=["biasT"])
            P.op("act", lambda e: e.activation(out=es[:], in_=sinks[:], func=AF.Exp), reads=["sinks"], writes=["es"])
            P.op("dve", lambda e: e.tensor_copy(out=es_hi[:], in_=es[:]), reads=["es"], writes=["es_hi"])
            P.op("dve", lambda e: e.tensor_copy(out=es_hif[:], in_=es_hi[:]), reads=["es_hi"], writes=["es_hif"])
            P.op("dve", lambda e: e.tensor_tensor(out=es_lo[:], in0=es[:], in1=es_hif[:], op=ALU.subtract),
                 reads=["es", "es_hif"], writes=["es_lo"])
            P.op("pool", lambda e: e.iota(io16[:], pattern=[[1, 16]], base=1, channel_multiplier=0), writes=["io16"])
            P.op("dve", lambda e: e.tensor_copy(out=io16f[:], in_=io16[:]), reads=["io16"], writes=["io16f"])
            for c in range(4):
                P.op("dve", lambda e, c=c: e.tensor_scalar(out=icnt[:, c, :], in0=io16f[:], scalar1=pos0[:, 0:1],
                                                           scalar2=float(2 ** (c + 1)), op0=ALU.add, op1=ALU.min),
                     reads=["io16f", "pos0"], writes=["icnt"])
            P.op("dve", lambda e: e.reciprocal(out=icnt[:], in_=icnt[:]), reads=["icnt"], writes=["icnt"])

            P.dma("pool", lambda e: e.dma_start(out=w_pool_sb[:], in_=w_pool_d.rearrange("g c d -> c g d")),
                  writes=["w_pool"], semkey=("ld", "w_pool"))
            P.dma("pool", lambda e: e.dma_start(out=w_out_sb[:], in_=w_out_d.rearrange("(kc p) n -> p kc n", p=128)),
                  writes=["w_out"], semkey=("ld", "w_out"))


        def load_x(T, after=()):
            slot = (T % NX) if T >= 0 else NX - 1
            src = x_d[T * 128:(T + 1) * 128, :] if T >= 0 else xh_d
            P.dma("sp", lambda e: e.dma_start(out=xs[:, slot, :], in_=src), reads=list(after), writes=[("x", slot)], semkey=("x", slot))

        def xslot(T):
            return (T % NX) if T >= 0 else NX - 1

        def norm_stage(tiles, gam, gam_key, dst, dst_key, rows=128, xn_priv=None):
            n = len(tiles)
            sslot = nxt("ss", 8)
            for i, (xap, xkey, off) in enumerate(tiles):
                jout = junk[0:rows, :] if xn_priv is None else xn_priv[0]
                jw = [] if xn_priv is None else [xn_priv[1]]
                P.op("act", lambda e, xap=xap, i=i, jout=jout: e.activation(out=jout, in_=xap, func=AF.Square,
                                                                            accum_out=ss[0:rows, sslot, i:i + 1]),
                     reads=[xkey] + jw, writes=[("ss", sslot)] + jw)
            P.op("pool", lambda e: e.tensor_scalar(out=var[0:rows, sslot, 0:n], in0=ss[0:rows, sslot, 0:n],
                                                   scalar1=1.0 / D, scalar2=EPS, op0=ALU.mult, op1=ALU.add),
                 reads=[("ss", sslot)], writes=[("var", sslot)])
            P.op("pool", lambda e: e.tensor_tensor(out=rstd[0:rows, sslot, 0:n], in0=var[0:rows, sslot, 0:n],
                                                   in1=expm[0:rows, 0:n], op=ALU.pow),
                 reads=[("var", sslot), "expm"], writes=[("rstd", sslot)])
            for i, (xap, xkey, off) in enumerate(tiles):
                if xn_priv is None:
                    s = nxt("xn", 2)
                    xn_ap, xn_key = xn[0:rows, s, :], ("xn", s)
                else:
                    xn_ap, xn_key = xn_priv
                P.op("act", lambda e, xap=xap, i=i, xn_ap=xn_ap: e.activation(out=xn_ap, in_=xap, func=AF.Copy,
                                                                              scale=rstd[0:rows, sslot, i:i + 1]),
                     reads=[xkey, ("rstd", sslot), xn_key], writes=[xn_key])
                b = bank()
                psb = ps[b][:].bitcast(BF16).rearrange("p (k t) -> p k t", t=128)
                for kc in range(8):
                    P.op("pe", lambda e, kc=kc, xn_ap=xn_ap, psb=psb: e.transpose(psb[:, kc, 0:rows], xn_ap[:, kc * 128:(kc + 1) * 128],
                                                                                  ident[0:rows, 0:rows]),
                         reads=[xn_key, "ident"], writes=[("ps", b)])
                P.op("dve", lambda e, psb=psb, off=off: e.tensor_tensor(
                    out=dst[:, :, off:off + rows], in0=psb[:, :, 0:rows], in1=gam[:, :, None].broadcast_to([128, 8, rows]),
                    op=ALU.mult), reads=[("ps", b), gam_key], writes=[dst_key])
            return sslot

        def mm_group(out_ap, b, pairs, reads):
            n = len(pairs)
            for i, (l, r) in enumerate(pairs):
                P.op("pe", lambda e, l=l, r=r, i=i: e.matmul(out_ap, lhsT=l, rhs=r, start=(i == 0), stop=(i == n - 1)),
                     reads=reads, writes=[("ps", b)])

        gu_issued = [0]
        wd_issued = [0]

        def issue_gu(upto):
            while gu_issued[0] < upto and gu_issued[0] < NG * NF:
                i = gu_issued[0]; f = i % NF; s = i % NWGU
                P.dma("pool", lambda e, f=f, s=s: e.dma_start(out=wgu[:, s].rearrange("p a k n -> p (a k n)"), in_=w_gu_d[f],
                                                              max_dma_last_dim=4096),
                      writes=[("wgu", s)], semkey=("wgu", s))
                gu_issued[0] += 1

        def issue_wd(upto):
            while wd_issued[0] < upto and wd_issued[0] < NG * 2 * NF:
                i = wd_issued[0]; j = i % (2 * NF); s = i % NWD
                P.dma("pool", lambda e, j=j, s=s: e.dma_start(out=wd[:, s, :], in_=w_d_d[j]),
                      writes=[("wd", s)], semkey=("wd", s))
                wd_issued[0] += 1

        out_keys = []
        SLOPES = [2.0 ** (-(h + 1)) for h in range(8)]

        CVA = ["al_cvaO", "al_cva0", "al_cva1"]

        def smp_dma():
            P.dma("pool", lambda e: e.dma_start(out=ckb, in_=ck_d.rearrange("b k f -> k b f")), reads=["al_ckb"], writes=["al_ckb"], semkey=("ld", "ckb"))
            P.op("pool", lambda e: e.memset(cva[:, :, 64:192], 1.0), reads=["al_cvaO"], writes=["al_cvaO"])
            cvsrc = cv_d.rearrange("b k f -> k b f")
            P.dma("pool", lambda e: e.dma_start(out=cva[:, :, 0:64], in_=cvsrc[:, :, 0:64]), reads=["al_cva0"], writes=["al_cva0"], semkey=("ld", "cva0"))
            P.dma("pool", lambda e: e.dma_start(out=cva[:, :, 192:256], in_=cvsrc[:, :, 64:128]), reads=["al_cva1"], writes=["al_cva1"], semkey=("ld", "cva1"))
            sp2 = sp_d.rearrange("b j c -> (b j) c")
            P.dma("sp", lambda e: e.dma_start(out=hist_v[0], in_=sp2[0:128, :]), reads=["hist0"], writes=["hist0"], semkey=("ld", "h0"))
            P.op("pool", lambda e: e.memset(hist_v[1], 0.0), reads=["hist1"], writes=["hist1"])
            P.dma("sp", lambda e: e.dma_start(out=hist_v[1][0:112, :], in_=sp2[128:240, :]), reads=["hist1"], writes=["hist1"], semkey=("ld", "h1"))

        def smp_setup():
            P.op("pool", lambda e: e.memset(vnew[:], 0.0), writes=["vnew"])
            P.op("pool", lambda e: e.memset(vnew[0:NS, 64:192], 1.0), reads=["vnew"], writes=["vnew"])
            P.op("pool", lambda e: e.memset(pbd[:], 0.0), writes=["pbd"])
            P.op("dve", lambda e: e.tensor_copy(out=relc[:], in_=iop[:]), reads=["iop"], writes=["relc"])
            for h in range(8):
                P.op("dve", lambda e, h=h: e.tensor_scalar(out=sbias[:, h:h + 1], in0=relc[:], scalar1=-8.0 * SLOPES[h], scalar2=None,
                                                           op0=ALU.mult), reads=["relc", "sbias"], writes=["sbias"])
            P.op("pool", lambda e: e.memset(bdmask[:], 1.0), writes=["bdmask"])
            P.op("pool", lambda e: e.affine_select(out=bdmask[:], in_=bdmask[:], pattern=[[0, 2], [1, NS], [0, 4]],
                                                   compare_op=ALU.is_equal, fill=0.0, base=0, channel_multiplier=-1),
                 reads=["bdmask"], writes=["bdmask"])
            P.op("dve", lambda e: e.tensor_copy(out=essm[:], in_=es[:].rearrange("p (k g) -> p k g", g=4)[:, :, None, :].broadcast_to([128, 2, NS, 4])),
                 reads=["es"], writes=["essm"])

        def smp_setup_selw():
            P.op("pool", lambda e: e.memset(selw, 1.0), reads=["selw"], writes=["selw"])
            for kt in range(2):
                for c in range(4):
                    w = 2 ** (c + 1)
                    P.op("pool", lambda e, kt=kt, c=c, w=w: e.affine_select(
                        out=selw[:, kt, c, :], in_=selw[:, kt, c, :], pattern=[[-15, NS]], compare_op=ALU.is_ge, fill=0.0,
                        base=kt * 128 - (16 - w), channel_multiplier=1), reads=["selw"], writes=["selw"])
                    P.op("pool", lambda e, kt=kt, c=c: e.affine_select(
                        out=selw[:, kt, c, :], in_=selw[:, kt, c, :], pattern=[[15, NS]], compare_op=ALU.is_ge, fill=0.0,
                        base=14 - kt * 128, channel_multiplier=-1), reads=["selw"], writes=["selw"])

        def smp_norm1():
            norm_stage([(xsm[:, :], "xsm", 0)], g1, "g1", hTs, "hTs", rows=NS, xn_priv=(xns, "al_tok"))

        def smp_inproj():
            b = bank()
            for c in range(4):
                mm_group(ps[b][:, c * NS:(c + 1) * NS], b, [(w_in_sb[:, kc, c * 128:(c + 1) * 128], hTs[:, kc, :]) for kc in range(8)],
                         ["w_inq", "hTs"])
            for c in range(4):
                mm_group(ps[b][:, (4 + c) * NS:(5 + c) * NS], b,
                         [(w_in_sb[:, kc, 768 + c * 128:768 + (c + 1) * 128], hTs[:, kc, :]) for kc in range(8)], ["w_in", "hTs"])
            P.op("dve", lambda e, b=b: e.tensor_copy(out=qTs[:], in_=ps[b][:, 0:4 * NS].rearrange("p (c n) -> p c n", n=NS)),
                 reads=[("ps", b)], writes=["qTs"])
            P.op("dve", lambda e, b=b: e.tensor_copy(out=uTs[:], in_=ps[b][:, 4 * NS:8 * NS].rearrange("p (c n) -> p c n", n=NS)),
                 reads=[("ps", b)], writes=["uTs"])
            for (c0, c1) in ((0, 512), (512, 1024), (1024, 1280)):
                b = bank()
                mm_group(ps[b][0:NS, 0:c1 - c0], b, [(hTs[:, kc, :], w_in_sb[:, kc, c0:c1]) for kc in range(8)],
                         ["w_inq" if c0 == 0 else "w_in", "hTs"])
                P.op("dve", lambda e, b=b, c0=c0, c1=c1: e.tensor_copy(out=tok_s[:, c0:c1], in_=ps[b][0:NS, 0:c1 - c0]),
                     reads=[("ps", b), "al_tok"], writes=["al_tok"])
            P.dma("sp", lambda e: e.dma_start(out=nk_d[:, 127, :], in_=tok_s[:, 512:640]), reads=["al_tok"], writes=[("out", "nk1")], semkey=("st", "nk1"))
            P.dma("sp", lambda e: e.dma_start(out=nv_d[:, 127, :], in_=tok_s[:, 640:768]), reads=["al_tok"], writes=[("out", "nv1")], semkey=("st", "nv1"))
            P.dma("sp", lambda e: e.dma_start(out=np_d[:, 14, :], in_=tok_s[:, 768:1280]), reads=["al_tok"], writes=[("out", "np1")], semkey=("st", "np1"))
            out_keys.extend([("out", "nk1"), ("out", "nv1"), ("out", "np1")])
            prodv = rec[0:NS, 0, :].rearrange("p (j h d) -> p j h d", h=2, d=64)
            P.op("dve", lambda e: e.tensor_tensor(
                out=prodv, in0=tok_s[:, 0:512].rearrange("p (j h d) -> p j h d", h=2, d=64),
                in1=tok_s[:, 512:640].rearrange("p (h d) -> p h d", d=64)[:, None, :, :].broadcast_to([NS, 4, 2, 64]), op=ALU.mult),
                reads=["al_tok"], writes=[("rec", 0)])
            P.op("dve", lambda e: e.tensor_reduce(out=snew[:], in_=rec[0:NS, 0, :].rearrange("p (a d) -> p a d", d=64), axis=AX.X, op=ALU.add),
                 reads=[("rec", 0)], writes=["snew"])
            P.op("act", lambda e: e.activation(out=pnew[:], in_=snew[:], func=AF.Exp, scale=0.125), reads=["snew"], writes=["pnew"])
            P.op("dve", lambda e: e.tensor_tensor(
                out=pbd[0:NS], in0=bdmask[:], in1=pnew[:].rearrange("p (j h) -> p h j", h=2)[:, :, None, :].broadcast_to([NS, 2, NS, 4]),
                op=ALU.mult), reads=["bdmask", "pnew", "pbd"], writes=["pbd"])
            P.op("dve", lambda e: e.tensor_copy(out=vnew[0:NS, :].rearrange("p (q d) -> p q d", d=64)[:, 0:4:3, :],
                                                in_=tok_s[:, 640:768].rearrange("p (h d) -> p h d", d=64)),
                 reads=["al_tok", "vnew"], writes=["vnew"])

        def smp_ktrans():
            for half in range(2):
                b = bank()
                psb = ps[b][:].bitcast(BF16).rearrange("p (k t) -> p k t", t=128)
                for i in range(8):
                    bb = half * 8 + i
                    P.op("pe", lambda e, psb=psb, i=i, bb=bb: e.transpose(psb[:, i, :], ckb[:, bb, :], ident[:]),
                         reads=["al_ckb", "ident"], writes=[("ps", b)])
                P.op("dve", lambda e, psb=psb, half=half: e.tensor_copy(out=ckT[:, half * 8:(half + 1) * 8, :], in_=psb),
                     reads=[("ps", b), "al_ckT"], writes=["al_ckT"])

        def smp_scores():
            for kvh in range(2):
                b = bank()
                r0 = kvh * 64
                for bb in range(NS):
                    P.op("pe", lambda e, b=b, bb=bb, r0=r0: e.matmul(
                        ps[b][:, bb * 4:bb * 4 + 4], lhsT=ckT[r0:r0 + 64, bb, :], rhs=qTs[r0:r0 + 64, :, bb],
                        start=True, stop=True), reads=["al_ckT", "qTs"], writes=[("ps", b)])
                P.op("dve", lambda e, b=b, kvh=kvh: e.tensor_tensor(
                    out=s_sb[:, :, kvh * 4:(kvh + 1) * 4], in0=ps[b][:, 0:NS * 4].rearrange("p (b g) -> p b g", g=4),
                    in1=sbias[:, None, kvh * 4:(kvh + 1) * 4].broadcast_to([128, NS, 4]), op=ALU.add),
                    reads=[("ps", b), "sbias", "s_sb"], writes=["s_sb"])
            P.op("act", lambda e: e.activation(out=PTs[:], in_=s_sb[:], func=AF.Exp, scale=0.125), reads=["s_sb"], writes=["PTs"])

        def smp_pv():
            for kvh in range(2):
                b = bank()
                a0, s0 = (0, 64) if kvh == 0 else (64, 0)
                for bb in range(NS):
                    P.op("pe", lambda e, b=b, kvh=kvh, bb=bb: e.matmul(
                        ps[b][:, bb * 4:(bb + 1) * 4], lhsT=vnew[:, kvh * 128:(kvh + 1) * 128], rhs=pbd[:, kvh, bb, :],
                        start=True, stop=False), reads=["vnew", "pbd"], writes=[("ps", b)])
                    P.op("pe", lambda e, b=b, kvh=kvh, bb=bb: e.matmul(
                        ps[b][:, bb * 4:(bb + 1) * 4], lhsT=cva[:, bb, kvh * 128:(kvh + 1) * 128], rhs=PTs[:, bb, kvh * 4:(kvh + 1) * 4],
                        start=False, stop=True), reads=CVA + ["PTs"], writes=[("ps", b)])
                P.op("dve", lambda e, b=b, kvh=kvh, s0=s0: e.tensor_tensor(
                    out=recs[s0:s0 + 64, kvh, :], in0=ps[b][s0:s0 + 64, 0:NS * 4],
                    in1=essm[s0:s0 + 64, kvh, :, :].rearrange("p b g -> p (b g)"), op=ALU.add),
                    reads=[("ps", b), "essm"], writes=[("recs", kvh)])
                P.op("dve", lambda e, kvh=kvh, s0=s0: e.reciprocal(out=recs[s0:s0 + 64, kvh, :], in_=recs[s0:s0 + 64, kvh, :]),
                     reads=[("recs", kvh)], writes=[("recs", kvh)])
                P.op("dve", lambda e, b=b, kvh=kvh, s0=s0, a0=a0: e.tensor_tensor(
                    out=mixTs[a0:a0 + 64, 0:4, :].rearrange("p g b -> p b g"),
                    in0=ps[b][a0:a0 + 64, 0:NS * 4].rearrange("p (b g) -> p b g", g=4),
                    in1=recs[s0:s0 + 64, kvh, :].rearrange("p (b g) -> p b g", g=4), op=ALU.mult),
                    reads=[("ps", b), ("recs", kvh)], writes=["mixTs"])

        def smp_pool():
            b = bank()
            for c in range(4):
                for kt, rows in ((0, 128), (1, 128)):
                    P.op("pe", lambda e, b=b, c=c, kt=kt, rows=rows: e.matmul(
                        ps[b][:, c * NS:(c + 1) * NS], lhsT=hist_v[kt][0:rows, c * 128:(c + 1) * 128], rhs=selw[0:rows, kt, c, :],
                        start=(kt == 0), stop=(kt == 1)), reads=["hist%d" % kt, "selw"], writes=[("ps", b)])
            P.op("dve", lambda e, b=b: e.tensor_tensor(out=tmps[:], in0=ps[b][:, 0:4 * NS].rearrange("p (c n) -> p c n", n=NS),
                                                       in1=uTs[:], op=ALU.add), reads=[("ps", b), "uTs"], writes=["tmps"])
            for c in range(4):
                P.op("dve", lambda e, c=c: e.scalar_tensor_tensor(out=mTs[:, c, :], in0=tmps[:, c, :], scalar=1.0 / (2 ** (c + 1)),
                                                                  in1=uTs[:, c, :], op0=ALU.mult, op1=ALU.subtract),
                     reads=["tmps", "uTs", "mTs"], writes=["mTs"])
            b2 = bank()
            for c in range(4):
                P.op("pe", lambda e, b2=b2, c=c: e.matmul(ps[b2][:, c * NS:(c + 1) * NS], lhsT=w_pool_sb[:, c, :], rhs=mTs[:, c, :],
                                                          start=True, stop=True), reads=["w_pool", "mTs"], writes=[("ps", b2)])
            for c in range(4):
                P.op("act", lambda e, b2=b2, c=c: e.activation(out=mixTs[:, 4 + c, :], in_=ps[b2][:, c * NS:(c + 1) * NS], func=AF.Copy,
                                                               scale=psc[:, c:c + 1]), reads=[("ps", b2), "psc", "mixTs"], writes=["mixTs"])

        def smp_wout():
            for hf in range(2):
                b = bank()
                mm_group(ps[b][0:NS, :], b, [(mixTs[:, ch, :], w_out_sb[:, ch, hf * 512:(hf + 1) * 512]) for ch in range(8)],
                         ["mixTs", "w_out"])
                P.op("dve", lambda e, b=b, hf=hf: e.tensor_tensor(out=xsm[:, hf * 512:(hf + 1) * 512], in0=ps[b][0:NS, :],
                                                                  in1=xsm[:, hf * 512:(hf + 1) * 512], op=ALU.add),
                     reads=[("ps", b), "xsm"], writes=["xsm"])
            norm_stage([(xsm[:, :], "xsm", 0)], g2, "g2", h2Ts, "h2Ts", rows=NS, xn_priv=(xns, "al_tok"))

        def smp_final():
            sslot = nxt("ss", 8)
            P.op("act", lambda e: e.activation(out=junk[0:NS, :], in_=xsm[:], func=AF.Square, accum_out=ss[0:NS, sslot, 0:1]),
                 reads=["xsm"], writes=[("ss", sslot)])
            P.op("pool", lambda e: e.tensor_scalar(out=var[0:NS, sslot, 0:1], in0=ss[0:NS, sslot, 0:1], scalar1=1.0 / D, scalar2=EPS,
                                                   op0=ALU.mult, op1=ALU.add), reads=[("ss", sslot)], writes=[("var", sslot)])
            P.op("pool", lambda e: e.tensor_tensor(out=rstd[0:NS, sslot, 0:1], in0=var[0:NS, sslot, 0:1], in1=expm[0:NS, 0:1], op=ALU.pow),
                 reads=[("var", sslot), "expm"], writes=[("rstd", sslot)])
            P.op("dve", lambda e: e.scalar_tensor_tensor(out=xsm[:], in0=xsm[:], scalar=rstd[0:NS, sslot, 0:1], in1=gft[0:NS, :],
                                                         op0=ALU.mult, op1=ALU.mult), reads=["xsm", ("rstd", sslot), "gft"], writes=["xsm"])
            P.dma("sp", lambda e: e.dma_start(out=ys_d, in_=xsm[:]), reads=["xsm"], writes=[("out", "ys")], semkey=("st", "ys"))
            out_keys.append(("out", "ys"))

        early_loads()
        P.op("dve", lambda e: e.tensor_scalar(out=flag[:], in0=pos0[:], scalar1=1.0, scalar2=None, op0=ALU.min),
             reads=["pos0"], writes=["flag"])
        P.op("pool", lambda e: e.iota(iop[:], pattern=[[0, 1]], base=128, channel_multiplier=-1), writes=["iop"])

        hs = xslot(-1)
        norm_stage([(xs[:, hs, :], ("x", hs), 0)], g1, "g1", h2T, ("h2T", 0))
        norm_stage([(xs[:, xslot(T), :], ("x", xslot(T)), T * 128) for T in range(GT)], g1, "g1", hT, "hT")
        b = bank()
        mm_group(ps[b][:, 0:128], b, [(w_in_sb[:, kc, 512:640], h2T[:, kc, 0:128]) for kc in range(8)], ["w_in", ("h2T", 0)])
        for kvh in range(2):
            r0 = kvh * 64
            P.op("dve", lambda e, b=b, kvh=kvh, r0=r0: e.tensor_copy(out=kTp[r0:r0 + 64, kvh, NKR - 1, :], in_=ps[b][r0:r0 + 64, 0:128]),
                 reads=[("ps", b), "kT_zero", ("kT", NKR - 1)], writes=[("kT", NKR - 1)])
        for c in range(4):
            b = bank()
            mm_group(ps[b][:, 0:128], b, [(w_in_sb[:, kc, 768 + c * 128:768 + (c + 1) * 128], h2T[:, kc, 0:128]) for kc in range(8)],
                     ["w_in", ("h2T", 0)])
            P.op("dve", lambda e, b=b, c=c: e.tensor_scalar(out=uT[:, c, 0:16], in0=ps[b][:, 112:128], scalar1=flag[:, 0:1],
                                                            scalar2=None, op0=ALU.mult),
                 reads=[("ps", b), "flag", "uT_halo"], writes=["uT_halo"])
        b = bank()
        mm_group(ps[b][:, 0:128], b, [(h2T[:, kc, 0:128], w_in_sb[:, kc, 640:768]) for kc in range(8)], ["w_in", ("h2T", 0)])
        vview0 = Vaug[:, 0, :].rearrange("p (b d) -> p b d", d=64)[:, 0:4:3, :]
        P.op("dve", lambda e, b=b: e.tensor_scalar(out=vview0, in0=ps[b][:, 0:128].rearrange("p (b d) -> p b d", d=64),
                                                   scalar1=flag[:, 0:1], scalar2=None, op0=ALU.mult),
             reads=[("ps", b), "flag", "Vaug_ones"], writes=[("V", 0)])
        P.op("dve", lambda e: e.tensor_copy(out=Vaug[:, 0, 64:192], in_=flag[:, 0:1].broadcast_to([128, 128])),
             reads=["flag", "Vaug_ones", ("V", 0)], writes=[("V", 0)])

        def norm_pre(xap, xkey, rows=128):
            sslot = nxt("ss", 8)
            sx = nxt("xn", 2)
            P.op("act", lambda e: e.activation(out=xn[0:rows, sx, :], in_=xap, func=AF.Square, accum_out=ss[0:rows, sslot, 0:1]),
                 reads=[xkey], writes=[("ss", sslot), ("xn", sx)])
            P.op("pool", lambda e: e.tensor_scalar(out=var[0:rows, sslot, 0:1], in0=ss[0:rows, sslot, 0:1], scalar1=1.0 / D, scalar2=EPS,
                                                   op0=ALU.mult, op1=ALU.add), reads=[("ss", sslot)], writes=[("var", sslot)])
            P.op("pool", lambda e: e.tensor_tensor(out=rstd[0:rows, sslot, 0:1], in0=var[0:rows, sslot, 0:1], in1=expm[0:rows, 0:1], op=ALU.pow),
                 reads=[("var", sslot), "expm"], writes=[("rstd", sslot)])
            P.op("dve", lambda e: e.tensor_scalar(out=xn[0:rows, sx, :], in0=xap, scalar1=rstd[0:rows, sslot, 0:1], scalar2=None, op0=ALU.mult),
                 reads=[xkey, ("rstd", sslot), ("xn", sx)], writes=[("xn", sx)])
            return sx

        def norm_pe(sx, gam, gam_key, dst, dst_key, off, rows=128):
            b = bank()
            psb = ps[b][:].bitcast(BF16).rearrange("p (k t) -> p k t", t=128)
            for kc in range(8):
                P.op("pe", lambda e, kc=kc: e.transpose(psb[:, kc, 0:rows], xn[0:rows, sx, kc * 128:(kc + 1) * 128], ident[0:rows, 0:rows]),
                     reads=[("xn", sx), "ident"], writes=[("ps", b)])
            P.op("dve", lambda e: e.tensor_tensor(out=dst[:, :, off:off + rows], in0=psb[:, :, 0:rows],
                                                  in1=gam[:, :, None].broadcast_to([128, 8, rows]), op=ALU.mult),
                 reads=[("ps", b), gam_key], writes=[dst_key])

        def attn_scores(g, t, T):
            pts = {}
            for kb, Tk in ((0, T - 1), (1, T)):
                ks = Tk % NKR
                for kvh in range(2):
                    b = bank()
                    pslot = nxt("PT", 8)
                    pts[(kb, kvh)] = pslot
                    P.op("pe", lambda e, b=b, ks=ks, kvh=kvh: e.matmul(
                        ps[b][:], lhsT=kTp[:, kvh, ks, :], rhs=qT[:, :, t * 128:(t + 1) * 128],
                        start=True, stop=False), reads=[("kT", ks), "qT"], writes=[("ps", b)])
                    P.op("pe", lambda e, b=b, kb=kb, kvh=kvh: e.matmul(ps[b][:], lhsT=ident[:], rhs=biasT[:, kb, kvh, :],
                                                                        start=False, stop=True),
                         reads=["ident", "biasT"], writes=[("ps", b)])
                    P.op("act", lambda e, b=b, pslot=pslot: e.activation(out=PT[:, pslot, :], in_=ps[b][:], func=AF.Exp, scale=0.125),
                         reads=[("ps", b)], writes=[("PT", pslot)])
            return pts

        def attn_pv(g, t, T, pts):
            pb = {}
            for kvh in range(2):
                b = bank()
                pb[kvh] = b
                vlo, vhi = (0, 65) if kvh == 0 else (191, 256)
                for gg in range(4):
                    for kb, Tk in ((0, T - 1), (1, T)):
                        vs = (Tk + 1) % NV
                        P.op("pe", lambda e, b=b, kvh=kvh, gg=gg, kb=kb, vs=vs, vlo=vlo, vhi=vhi, s_=pts[(kb, kvh)]: e.matmul(
                            ps[b][:, gg * 65:(gg + 1) * 65], lhsT=PT[:, s_, gg * 128:(gg + 1) * 128], rhs=Vaug[:, vs, vlo:vhi],
                            start=(kb == 0), stop=(kb == 1)), reads=[("V", vs), ("PT", pts[(kb, kvh)])], writes=[("ps", b)])
            ds = nxt("den", 2)
            ms = nxt("mtm", 2)
            for kvh in range(2):
                b = pb[kvh]
                pv = ps[b][:, 0:260].rearrange("p (g c) -> p g c", c=65)
                rc = 64 if kvh == 0 else 0
                P.op("dve", lambda e, pv=pv, rc=rc, kvh=kvh, ds=ds: e.tensor_tensor(
                    out=den[:, ds, kvh * 4:(kvh + 1) * 4, :], in0=pv[:, :, rc:rc + 1], in1=es[:, kvh * 4:(kvh + 1) * 4, None], op=ALU.add),
                    reads=[("ps", b), "es", ("den", ds)], writes=[("den", ds)])
            P.op("dve", lambda e, ds=ds: e.reciprocal(out=den[:, ds, :, :], in_=den[:, ds, :, :]), reads=[("den", ds)], writes=[("den", ds)])
            mv = mix_tm[:, ms, :].rearrange("p (j h d) -> p j h d", h=2, d=64)
            for kvh in range(2):
                b = pb[kvh]
                pv = ps[b][:, 0:260].rearrange("p (g c) -> p g c", c=65)
                a0 = 0 if kvh == 0 else 1
                P.op("dve", lambda e, pv=pv, a0=a0, kvh=kvh, ds=ds, mv=mv: e.tensor_tensor(
                    out=mv[:, :, kvh, :], in0=pv[:, :, a0:a0 + 64], in1=den[:, ds, kvh * 4:(kvh + 1) * 4, :].broadcast_to([128, 4, 64]),
                    op=ALU.mult), reads=[("ps", b), ("den", ds), ("mtm", ms)], writes=[("mtm", ms)])
            return ms

        def attn_tr(t, ms):
            bt = bank()
            psb = ps[bt][:].bitcast(BF16)[:, 0:512].rearrange("p (j q) -> p j q", q=128)
            for j in range(4):
                P.op("pe", lambda e, psb=psb, j=j, ms=ms: e.transpose(psb[:, j, :], mix_tm[:, ms, j * 128:(j + 1) * 128], ident[:]),
                     reads=[("mtm", ms), "ident"], writes=[("ps", bt)])
            P.op("act", lambda e, psb=psb: e.activation(out=mixT[:, 0:4, t * 128:(t + 1) * 128], in_=psb, func=AF.Copy),
                 reads=[("ps", bt)], writes=[("mixA", t)])

        load_x(NX - 1)
        if SMP_LEVEL >= 9:
            P.capture = []
            smp_norm1(); smp_inproj(); smp_ktrans(); smp_scores(); smp_pv(); smp_pool(); smp_wout()
            P.queue = P.capture
            P.capture = None
        SMP_G = 1

        hooks_on = [False]

        def hook():
            if hooks_on[0]:
                P.flush_chunk()

        for g in range(NG):
            tiles = [g * GT + t for t in range(GT)]
            if g == 0:
                late_setup()
            if g > 0:
                P.op("pool", lambda e: e.tensor_copy(out=uT[:, :, 0:16], in_=uT[:, :, GN:GN + 16]),
                     reads=[("uT_body", 0), ("uT_body", 1), ("uT_body", 2), ("uT_body", 3)] + ["uT_halo"], writes=["uT_halo"])
            for c in range(4):
                b = bank()
                mm_group(ps[b][:], b, [(w_in_sb[:, kc, 768 + c * 128:768 + (c + 1) * 128], hT[:, kc, :]) for kc in range(8)],
                         ["w_in", "hT"])
                P.op("dve", lambda e, b=b, c=c: e.tensor_copy(out=uT[:, c, 16:16 + GN], in_=ps[b][:]),
                     reads=[("ps", b), "uT_halo"], writes=[("uT_body", c)])
                hook()
            for c in range(4):
                b = bank()
                mm_group(ps[b][:], b, [(w_in_sb[:, kc, c * 128:(c + 1) * 128], hT[:, kc, :]) for kc in range(8)], ["w_inq", "hT"])
                P.op("dve", lambda e, b=b, c=c: e.tensor_copy(out=qT[:, c, :], in_=ps[b][:]), reads=[("ps", b)], writes=["qT"])
                hook()
            b = bank()
            mm_group(ps[b][:], b, [(w_in_sb[:, kc, 512:640], hT[:, kc, :]) for kc in range(8)], ["w_in", "hT"])
            ks0 = (g * GT) % NKR
            for kvh in range(2):
                r0 = kvh * 64
                P.op("dve", lambda e, b=b, kvh=kvh, r0=r0, ks0=ks0: e.tensor_copy(
                    out=kTp[r0:r0 + 64, kvh, ks0:ks0 + GT, :], in_=ps[b][r0:r0 + 64, :].rearrange("p (t n) -> p t n", n=128)),
                    reads=[("ps", b), "kT_zero"] + [("kT", ks0 + i) for i in range(GT)], writes=[("kT", ks0 + i) for i in range(GT)])
            hook()
            b = bank()
            for t, T in enumerate(tiles):
                mm_group(ps[b][:, t * 128:(t + 1) * 128], b,
                         [(hT[:, kc, t * 128:(t + 1) * 128], w_in_sb[:, kc, 640:768]) for kc in range(8)], ["w_in", "hT"])
            for t, T in enumerate(tiles):
                vs_ = (T + 1) % NV
                vv = Vaug[:, vs_, :].rearrange("p (b d) -> p b d", d=64)[:, 0:4:3, :]
                if T + 1 == NV:
                    P.op("dve", lambda e: e.memset(Vaug[:, 0, 64:192], 1.0), reads=[("V", 0)], writes=[("V", 0)])
                P.op("dve", lambda e, b=b, t=t, vv=vv: e.tensor_copy(
                    out=vv, in_=ps[b][:, t * 128:(t + 1) * 128].rearrange("p (b d) -> p b d", d=64)),
                    reads=[("ps", b), "Vaug_ones", ("V", vs_)], writes=[("V", vs_)])
            hook()
            if g == NG - 1:
                b1 = bank()
                mm_group(ps[b1][:], b1, [(hT[:, kc, 384:512], w_in_sb[:, kc, 512:1024]) for kc in range(8)], ["w_in", "hT"])
                P.op("dve", lambda e, b1=b1: e.tensor_copy(out=klast[:, 0:512], in_=ps[b1][:]), reads=[("ps", b1)], writes=KL)
                b2 = bank()
                mm_group(ps[b2][:, 0:256], b2, [(hT[:, kc, 384:512], w_in_sb[:, kc, 1024:1280]) for kc in range(8)], ["w_in", "hT"])
                P.op("dve", lambda e, b2=b2: e.tensor_copy(out=klast[:, 512:768], in_=ps[b2][:, 0:256]),
                     reads=[("ps", b2)] + KL, writes=KL)
                P.dma("sp", lambda e: e.dma_start(out=kvu_d, in_=klast), reads=KL, writes=[("out", "kvu")],
                      semkey=("st", "kvu"))
                out_keys.append(("out", "kvu"))
            for c in range(4):
                w = 2 ** (c + 1)
                cur = uT[:, c, :]
                for k in range(c + 1):
                    d = 2 ** k
                    dstp = ptmp[:, k % 2, :]
                    P.op("pool", lambda e, cur=cur, dstp=dstp, d=d: e.tensor_tensor(
                        out=dstp[:, d:16 + GN], in0=cur[:, d:16 + GN], in1=cur[:, 0:16 + GN - d], op=ALU.add),
                        reads=[("uT_body", c), "uT_halo", ("ptmp", (k + 1) % 2)] if k > 0 else [("uT_body", c), "uT_halo"],
                        writes=[("ptmp", k % 2)])
                    cur = dstp
                last = c % 2
                oth = (c + 1) % 2
                if g == 0:
                    P.op("pool", lambda e, cur=cur, c=c: e.tensor_tensor(out=fix16[:], in0=cur[:, 16:32], in1=icnt[:, c, :], op=ALU.mult),
                         reads=[("ptmp", last), "icnt"], writes=["fix16"])
                P.op("pool", lambda e, cur=cur, oth=oth, w=w: e.tensor_scalar(
                    out=ptmp[:, oth, 16:16 + GN], in0=cur[:, 16:16 + GN], scalar1=1.0 / w, scalar2=0.0, op0=ALU.mult, op1=ALU.add),
                    reads=[("ptmp", last)], writes=[("ptmp", oth)])
                P.op("pool", lambda e, oth=oth, c=c: e.tensor_tensor(out=mT[:, c, :], in0=ptmp[:, oth, 16:16 + GN], in1=uT[:, c, 16:16 + GN],
                                                                     op=ALU.subtract), reads=[("ptmp", oth), ("uT_body", c)], writes=[("mT", c)])
                if g == 0:
                    P.op("pool", lambda e, c=c: e.tensor_tensor(out=mT[:, c, 0:16], in0=fix16[:], in1=uT[:, c, 16:32], op=ALU.subtract),
                         reads=["fix16", ("uT_body", c), ("mT", c)], writes=[("mT", c)])
            if g == 0:
                smp_setup()
                issue_gu(NWGU)
                issue_wd(NWD)
            def pool_mm():
                for c in range(4):
                    b = bank()
                    P.op("pe", lambda e, b=b, c=c: e.matmul(ps[b][:], lhsT=w_pool_sb[:, c, :], rhs=mT[:, c, :], start=True, stop=True),
                         reads=["w_pool", ("mT", c)], writes=[("ps", b)])
                    P.op("act", lambda e, b=b, c=c: e.activation(out=mixT[:, 4 + c, :], in_=ps[b][:], func=AF.Copy, scale=psc[:, c:c + 1]),
                         reads=[("ps", b), "psc"], writes=[("mixP", c)])

            npre = {}

            def wout(t):
                T = tiles[t]
                sl = xslot(T)
                for hf in range(2):
                    b = bank()
                    mm_group(ps[b][:], b, [(mixT[:, ch, t * 128:(t + 1) * 128], w_out_sb[:, ch, hf * 512:(hf + 1) * 512]) for ch in range(8)],
                             [("mixA", t), "w_out"] + [("mixP", c) for c in range(4)])
                    P.op("dve", lambda e, b=b, sl=sl, hf=hf: e.tensor_tensor(
                        out=xs[:, sl, hf * 512:(hf + 1) * 512], in0=ps[b][:], in1=xs[:, sl, hf * 512:(hf + 1) * 512], op=ALU.add),
                        reads=[("ps", b), ("x", sl)], writes=[("x", sl)])
                npre[t] = norm_pre(xs[:, sl, :], ("x", sl))

            def n2pe(t):
                norm_pe(npre[t], g2, "g2", h2T, ("h2T", t), t * 128)

            pts = {0: attn_scores(g, 0, tiles[0])}
            hook()
            pts[1] = attn_scores(g, 1, tiles[1]); hook()
            mss = {}
            mss[0] = attn_pv(g, 0, tiles[0], pts[0]); hook()
            pts[2] = attn_scores(g, 2, tiles[2]); hook()
            mss[1] = attn_pv(g, 1, tiles[1], pts[1]); hook()
            attn_tr(0, mss[0])
            pts[3] = attn_scores(g, 3, tiles[3]); hook()
            mss[2] = attn_pv(g, 2, tiles[2], pts[2]); hook()
            attn_tr(1, mss[1])
            mss[3] = attn_pv(g, 3, tiles[3], pts[3]); hook()
            attn_tr(2, mss[2])
            pool_mm(); hook()
            attn_tr(3, mss[3])
            wout(0); hook()
            wout(1); hook()
            n2pe(0)
            wout(2); hook()
            n2pe(1)
            wout(3); hook()
            H2 = [("h2T", 0), ("h2T", 1), ("h2T", 2), ("h2T", 3)]
            NSPLIT = 2
            early = {}

            def gu_half(f, half):
                gi = g * NF + f
                s_ = gi % NWGU
                if half == 0:
                    early[f] = (bank(), bank())
                    held.update(early[f])
                bg_, bu_ = early[f]
                c0, c1 = half * 256, (half + 1) * 256
                rk = [("wgu", s_)] + H2[2 * half:2 * half + 2]
                mm_group(ps[bg_][:, c0:c1], bg_, [(wgu[:, s_, 0, kc, :], h2T[:, kc, c0:c1]) for kc in range(8)], rk)
                mm_group(ps[bu_][:, c0:c1], bu_, [(wgu[:, s_, 1, kc, :], h2T[:, kc, c0:c1]) for kc in range(8)], rk)

            for f in range(NSPLIT):
                gu_half(f, 0)
            n2pe(2)
            n2pe(3)
            for f in range(NSPLIT):
                gu_half(f, 1)
                held.difference_update(early[f])

            smp_here = (g == SMP_G and SMP_LEVEL >= 9)
            smp_front = (g == 0 and SMP_LEVEL >= 9)
            if smp_front:
                P.op("pool", lambda e: e.memset(gate_t[:], 0.0), writes=OWN_KEYS + AL_KEYS)
                smp_dma()
                P.flush_chunk()
                hooks_on[0] = True
            for f in range(NF):
                gi = g * NF + f
                s = gi % NWGU
                if f < NSPLIT:
                    bg, bu = early[f]
                else:
                    bg = bank()
                    mm_group(ps[bg][:], bg, [(wgu[:, s, 0, kc, :], h2T[:, kc, :]) for kc in range(8)], [("wgu", s)] + H2)
                    bu = bank()
                    mm_group(ps[bu][:], bu, [(wgu[:, s, 1, kc, :], h2T[:, kc, :]) for kc in range(8)], [("wgu", s)] + H2)
                if smp_here:
                    bs = bank()
                    mm_group(ps[bs][:, 0:NS], bs, [(wgu[:, s, 0, kc, :], h2Ts[:, kc, :]) for kc in range(8)], [("wgu", s), "h2Ts"])
                    mm_group(ps[bs][:, NS:2 * NS], bs, [(wgu[:, s, 1, kc, :], h2Ts[:, kc, :]) for kc in range(8)], [("wgu", s), "h2Ts"])
                issue_gu(gi + NWGU + 1)
                sgi = nxt("sg", 2)
                P.op("act", lambda e, bg=bg, sgi=sgi: e.activation(out=sg[:, sgi, :], in_=ps[bg][:], func=AF.Silu),
                     reads=[("ps", bg)], writes=[("sg", sgi)])
                P.op("dve", lambda e, bu=bu, sgi=sgi, f=f: e.tensor_tensor(out=actT[:, f, :], in0=ps[bu][:], in1=sg[:, sgi, :], op=ALU.mult),
                     reads=[("ps", bu), ("sg", sgi)], writes=[("actT", f)])
                if smp_here:
                    P.op("act", lambda e, bs=bs: e.activation(out=sgs[:], in_=ps[bs][:, 0:NS], func=AF.Silu), reads=[("ps", bs)], writes=["sgs"])
                    P.op("dve", lambda e, bs=bs, f=f: e.tensor_tensor(out=actTs[:, f, :], in0=ps[bs][:, NS:2 * NS], in1=sgs[:], op=ALU.mult),
                         reads=[("ps", bs), "sgs"], writes=["actTs"])
                if smp_front:
                    if f == 3:
                        smp_setup_selw()
                    hook()
                    if f == NF - 3:
                        P.flush_all()
                        hooks_on[0] = False
                        P.op("pool", lambda e: e.memset(gate_t[:], 0.0), writes=AL_KEYS + OWN_KEYS)
            bss = None
            if smp_here:
                bss = bank()
                held.add(bss)
                P.op("pe", lambda e, bss=bss: e.matmul(ps[bss][:, 0:8 * NS], lhsT=zb[:], rhs=ident[:], start=True, stop=False),
                     reads=["zb", "ident"], writes=[("ps", bss)])
            for hf in range(2):
                banks = [bank() for _ in range(GT)]
                held.update(banks)
                hoist = (hf == 0 and g + 1 < NG)
                nxt_tiles = [(g + 1) * GT + t for t in range(GT)]
                hsx = {}
                if hoist:
                    hsx[0] = norm_pre(xs[:, xslot(nxt_tiles[0]), :], ("x", xslot(nxt_tiles[0])))
                for f in range(NF):
                    di = g * 2 * NF + hf * NF + f
                    s = di % NWD
                    for t in range(GT):
                        P.op("pe", lambda e, t=t, f=f, s=s, bb=banks[t]: e.matmul(
                            ps[bb][:], lhsT=actT[:, f, t * 128:(t + 1) * 128], rhs=wd[:, s, :], start=(f == 0), stop=(f == NF - 1)),
                            reads=[("actT", f), ("wd", s)], writes=[("ps", banks[t])])
                    if smp_here:
                        for cc in range(4):
                            col = (hf * 4 + cc) * NS
                            last_ = (hf == 1 and f == NF - 1 and cc == 3)
                            P.op("pe", lambda e, f=f, s=s, cc=cc, col=col, last_=last_, bss=bss: e.matmul(
                                ps[bss][:, col:col + NS], lhsT=wd[:, s, cc * 128:(cc + 1) * 128], rhs=actTs[:, f, :], start=False, stop=last_),
                                reads=["actTs", ("wd", s)], writes=[("ps", bss)])
                    issue_wd(di + NWD + 1)
                    if hoist and f in (4, 9, 14, 19):
                        tt = (f - 4) // 5
                        norm_pe(hsx[tt], g1, "g1", hT, "hT", tt * 128)
                        if tt + 1 < GT:
                            hsx[tt + 1] = norm_pre(xs[:, xslot(nxt_tiles[tt + 1]), :], ("x", xslot(nxt_tiles[tt + 1])))
                held.difference_update(banks)
                for t, T in enumerate(tiles):
                    sl = xslot(T)
                    P.op("dve", lambda e, bb=banks[t], sl=sl, hf=hf: e.tensor_tensor(
                        out=xs[:, sl, hf * 512:(hf + 1) * 512], in0=ps[bb][:], in1=xs[:, sl, hf * 512:(hf + 1) * 512], op=ALU.add),
                        reads=[("ps", banks[t]), ("x", sl)], writes=[("x", sl)])
            if smp_here:
                held.discard(bss)
                P.op("act", lambda e, bss=bss: e.activation(out=ysT[:], in_=ps[bss][:, 0:8 * NS], func=AF.Copy), reads=[("ps", bss)], writes=["ysT"])
                for hf in range(2):
                    bt_ = bank()
                    for cc in range(4):
                        ch = hf * 4 + cc
                        P.op("pe", lambda e, bt_=bt_, cc=cc, ch=ch: e.transpose(ps[bt_][0:NS, cc * 128:(cc + 1) * 128], ysT[:, ch * NS:(ch + 1) * NS],
                                                                               identF[:]),
                             reads=["ysT", "identF"], writes=[("ps", bt_)])
                    P.op("dve", lambda e, bt_=bt_, hf=hf: e.tensor_tensor(out=xsm[:, hf * 512:(hf + 1) * 512], in0=ps[bt_][0:NS, :],
                                                                        in1=xsm[:, hf * 512:(hf + 1) * 512], op=ALU.add),
                         reads=[("ps", bt_), "xsm"], writes=["xsm"])
            for t, T in enumerate(tiles):
                sl = xslot(T)
                sslot = nxt("ss", 8)
                P.op("act", lambda e, sl=sl, sslot=sslot: e.activation(out=junk[:], in_=xs[:, sl, :], func=AF.Square,
                                                                       accum_out=ss[:, sslot, 0:1]),
                     reads=[("x", sl)], writes=[("ss", sslot)])
                P.op("pool", lambda e, sslot=sslot: e.tensor_scalar(out=var[:, sslot, 0:1], in0=ss[:, sslot, 0:1], scalar1=1.0 / D, scalar2=EPS,
                                                                    op0=ALU.mult, op1=ALU.add), reads=[("ss", sslot)], writes=[("var", sslot)])
                P.op("pool", lambda e, sslot=sslot: e.tensor_tensor(out=rstd[:, sslot, 0:1], in0=var[:, sslot, 0:1], in1=expm[:, 0:1], op=ALU.pow),
                     reads=[("var", sslot), "expm"], writes=[("rstd", sslot)])
                P.op("dve", lambda e, sl=sl, sslot=sslot: e.scalar_tensor_tensor(
                    out=xs[:, sl, :], in0=xs[:, sl, :], scalar=rstd[:, sslot, 0:1], in1=gft[:], op0=ALU.mult, op1=ALU.mult),
                    reads=[("x", sl), ("rstd", sslot), "gft"], writes=[("x", sl)])
                P.dma("sp", lambda e, sl=sl, T=T: e.dma_start(out=y_d[T * 128:(T + 1) * 128, :], in_=xs[:, sl, :]),
                      reads=[("x", sl)], writes=[("out", "y", T)], semkey=("st", sl))
                out_keys.append(("out", "y", T))
                if T + NX < NT:
                    load_x(T + NX)

            if smp_here:
                smp_final()

        P.op("sp", lambda e: e.nop(), reads=out_keys)
        P.emit(st)
    return nc


_CACHE = {}


def _prep_weights(inp):
    w_in = np.asarray(inp["w_in"][0], np.float32)
    qcols = []
    for j in range(4):
        qcols += list(range(j * 64, (j + 1) * 64)) + list(range((4 + j) * 64, (5 + j) * 64))
    cols = qcols + list(range(512, 1280))
    w_in_p = np.ascontiguousarray(w_in[:, cols])
    w_out = np.asarray(inp["w_out"][0], np.float32)
    rows = qcols + list(range(512, 1024))
    w_out_p = np.ascontiguousarray(w_out[rows, :])
    wg = np.asarray(inp["w_gate"][0], np.float32).reshape(8, 128, NF, 128)
    wu = np.asarray(inp["w_up"][0], np.float32).reshape(8, 128, NF, 128)
    w_gu = np.ascontiguousarray(np.stack([wg, wu], 0).transpose(3, 2, 0, 1, 4)).reshape(NF, 128, 2 * 8 * 128)
    wdn = np.asarray(inp["w_down"][0], np.float32).reshape(NF, 128, 2, 512)
    w_d = np.ascontiguousarray(wdn.transpose(2, 0, 1, 3)).reshape(2 * NF, 128, 512)
    return dict(
        w_in=w_in_p, w_out=w_out_p, w_gu=w_gu, w_d=w_d,
        w_pool=np.ascontiguousarray(np.asarray(inp["w_pool"][0], np.float32)),
        g1=np.ascontiguousarray(np.asarray(inp["norm1"][0], np.float32).reshape(8, 128).T),
        g2=np.ascontiguousarray(np.asarray(inp["norm2"][0], np.float32).reshape(8, 128).T),
        psc=np.ascontiguousarray(np.asarray(inp["pool_scale"][0], np.float32).reshape(4, 128).T),
        gf=np.ascontiguousarray(np.broadcast_to(np.asarray(inp["final_norm"], np.float32)[None, :], (128, D))),
        sinks=np.ascontiguousarray(np.broadcast_to(np.asarray(inp["attn_sinks"][0], np.float32)[None, :], (128, 8))),
    )


def kernel(**inp):
    if "nc" not in _CACHE:
        _CACHE["nc"] = build_program()
    nc = _CACHE["nc"]
    xp = np.asarray(inp["x_prompt"], np.float32)
    xsm = np.asarray(inp["x_sample"], np.float32)[:, 0, :]
    ck = np.asarray(inp["cache_k_window"], np.float32)[0].reshape(128, 128, 128)
    cv = np.asarray(inp["cache_v_window"], np.float32)[0].reshape(128, 128, 128)
    spool = np.asarray(inp["state_pool"], np.float32)[0]
    wts = _prep_weights(inp)
    in_maps = []
    for c in range(NCORES):
        b, h = c // 2, c % 2
        m = dict(wts)
        m["x"] = np.ascontiguousarray(xp[b, h * TPC:(h + 1) * TPC])
        m["xh"] = np.ascontiguousarray(xp[b, TPC - 128:TPC]) if h == 1 else np.zeros((128, D), np.float32)
        m["pos0"] = np.full((128, 1), float(h * TPC), np.float32)
        m["xs"] = np.ascontiguousarray(xsm[c * NS:(c + 1) * NS])
        m["ck"] = np.ascontiguousarray(ck[c * NS:(c + 1) * NS])
        m["cv"] = np.ascontiguousarray(cv[c * NS:(c + 1) * NS])
        m["spool"] = np.ascontiguousarray(spool[c * NS:(c + 1) * NS])
        in_maps.append(m)
    res = run_bass_kernel_spmd(nc, in_maps, core_ids=list(range(NCORES)))
    R = res.results
    y_prompt = np.stack([np.concatenate([R[2 * b]["y"], R[2 * b + 1]["y"]], 0) for b in range(4)], 0)
    kvu = np.stack([R[2 * b + 1]["kvu_last"] for b in range(4)], 0)
    new_k_prompt = np.ascontiguousarray(kvu[:, :, 0:128]).reshape(1, 4, 128, 2, 64)
    new_v_prompt = np.ascontiguousarray(kvu[:, :, 128:256]).reshape(1, 4, 128, 2, 64)
    new_pool_prompt = np.ascontiguousarray(kvu[:, 113:128, 256:768]).reshape(1, 4, 15, 512)
    y_sample = np.concatenate([R[c]["ys"] for c in range(NCORES)], 0).reshape(128, 1, D)
    new_k_sample = np.concatenate([R[c]["nk"] for c in range(NCORES)], 0).reshape(1, 128, 128, 2, 64)
    new_v_sample = np.concatenate([R[c]["nv"] for c in range(NCORES)], 0).reshape(1, 128, 128, 2, 64)
    new_pool_sample = np.concatenate([R[c]["npool"] for c in range(NCORES)], 0).reshape(1, 128, 15, 512)
    return (y_prompt.astype(np.float32), y_sample.astype(np.float32), new_k_prompt, new_v_prompt, new_pool_prompt,
            new_k_sample, new_v_sample, new_pool_sample)
```
